# Optimizing a Trainium2 kernel written in Bass

```python
import math
import jax, jax.numpy as jnp
from jax import lax
import numpy as np

D_MODEL = 2048
BATCH = 4
SEQ = 2048
DEPTH = 1
DEC_BATCH = 128
DEC_SEQ = 4
PAST_LEN = 16384
PAGE_SIZE = 128

E_CONV = D_MODEL // 2
CONV_W = 3
E_SSM = D_MODEL // 2
GROUP = 16
N_GROUPS = E_SSM // GROUP
P_STATE = 64
N_HEADS = 4
HEAD_DIM = D_MODEL // 8
E_ATTN = N_HEADS * HEAD_DIM
MEM_LEN = 256
N_BRANCH = 3
N_IN = 4 * E_CONV + 2 * E_SSM + 2 * E_ATTN + N_BRANCH * D_MODEL
EPS = 1e-6

kernel_name = "hybrid_conv_s5_memxattn_decode_step"


def _rmsnorm(x, g):
    x32 = x.astype(jnp.float32)
    y = x32 * lax.rsqrt(jnp.mean(x32 * x32, axis=-1, keepdims=True) + EPS)
    return (y * g.astype(jnp.float32)).astype(x.dtype)


def _split_points():
    sizes = [E_CONV, E_CONV, E_CONV, E_CONV, E_SSM, E_SSM, E_ATTN, E_ATTN]
    return list(np.cumsum(sizes))


def _combine(e1, e2):
    a1r, a1i, b1r, b1i = e1
    a2r, a2i, b2r, b2i = e2
    ar = a1r * a2r - a1i * a2i
    ai = a1r * a2i + a1i * a2r
    br = a2r * b1r - a2i * b1i + b2r
    bi = a2r * b1i + a2i * b1r + b2i
    return (ar, ai, br, bi)


def _s5(u, s0_re, s0_im, lam_re, lam_im, log_dt, b_re, b_im, c_re, c_im, d_skip):
    bsz, L, _ = u.shape
    u32 = u.astype(jnp.float32).reshape(bsz, L, N_GROUPS, GROUP)
    dt = jnp.exp(log_dt.astype(jnp.float32))[:, None]
    lr = lam_re.astype(jnp.float32)
    li = lam_im.astype(jnp.float32)
    mag = jnp.exp(lr * dt)
    ang = li * dt
    ar = mag * jnp.cos(ang)
    ai = mag * jnp.sin(ang)
    den = lr * lr + li * li
    fr = ((ar - 1.0) * lr + ai * li) / den
    fi = (ai * lr - (ar - 1.0) * li) / den
    br32 = b_re.astype(jnp.float32)
    bi32 = b_im.astype(jnp.float32)
    bbr = fr[..., None] * br32 - fi[..., None] * bi32
    bbi = fr[..., None] * bi32 + fi[..., None] * br32
    xr = jnp.einsum('blgc,gpc->blgp', u32, bbr)
    xi = jnp.einsum('blgc,gpc->blgp', u32, bbi)
    s0r = s0_re.astype(jnp.float32)
    s0i = s0_im.astype(jnp.float32)
    xr = xr.at[:, 0].add(ar * s0r - ai * s0i)
    xi = xi.at[:, 0].add(ar * s0i + ai * s0r)
    a_r = jnp.broadcast_to(ar, xr.shape)
    a_i = jnp.broadcast_to(ai, xr.shape)
    _, _, sr, si = lax.associative_scan(_combine, (a_r, a_i, xr, xi), axis=1)
    y = (jnp.einsum('blgp,gcp->blgc', sr, c_re.astype(jnp.float32))
         - jnp.einsum('blgp,gcp->blgc', si, c_im.astype(jnp.float32)))
    y = y.reshape(bsz, L, E_SSM) + d_skip.astype(jnp.float32) * u32.reshape(bsz, L, E_SSM)
    return y.astype(u.dtype), sr[:, -1], si[:, -1]


def _layer(x, mem_k, mem_v, conv_prev, s_re, s_im, p):
    bsz, L, _ = x.shape
    h = _rmsnorm(x, p['norm_g'])
    proj = h @ p['w_in']
    cb, cc, ch, cz, su, sz, aq, az, gpre = jnp.split(proj, _split_points(), axis=-1)

    v = cc * ch
    pad = jnp.concatenate([conv_prev.astype(v.dtype), v], axis=1)
    w = p['conv_w']
    conv = w[0] * pad[:, :L] + w[1] * pad[:, 1:L + 1] + w[2] * pad[:, 2:L + 2]
    conv_out = (cb * conv * jax.nn.silu(cz)) @ p['w_conv_out']
    new_conv = pad[:, L:]

    ys, new_re, new_im = _s5(su, s_re, s_im, p['lam_re'], p['lam_im'], p['log_dt'],
                             p['b_re'], p['b_im'], p['c_re'], p['c_im'], p['d_skip'])
    ys = jax.nn.gelu(ys * jax.nn.silu(sz))
    ssm_out = (ys @ p['w_glu_a']) * jax.nn.sigmoid(ys @ p['w_glu_b'])

    q = aq.reshape(bsz, L, N_HEADS, HEAD_DIM)
    scores = jnp.einsum('blhd,bmhd->bhlm', q, mem_k).astype(jnp.float32) * (HEAD_DIM ** -0.5)
    probs = jax.nn.softmax(scores, axis=-1).astype(mem_v.dtype)
    o = jnp.einsum('bhlm,bmhd->blhd', probs, mem_v).reshape(bsz, L, E_ATTN)
    attn_out = (o * jax.nn.silu(az)) @ p['w_attn_out']

    g = jax.nn.sigmoid(gpre).reshape(bsz, L, N_BRANCH, D_MODEL)
    merged = g[:, :, 0] * conv_out + g[:, :, 1] * ssm_out + g[:, :, 2] * attn_out
    return x + merged @ p['w_out'], new_conv, new_re, new_im


def setup_inputs(seed: int = 0) -> dict:
    key = jax.random.key(seed)
    ks = jax.random.split(key, 32)
    f = jnp.float32
    nrm = lambda k, shape, s: jax.random.normal(k, shape, f) * s
    lam_im0 = math.pi * jnp.arange(P_STATE, dtype=f)
    return {
        'x_prompt': nrm(ks[0], (BATCH, SEQ, D_MODEL), 1.0),
        'x_sample': nrm(ks[1], (DEC_BATCH, DEC_SEQ, D_MODEL), 1.0),
        'mem_prompt': nrm(ks[2], (BATCH, MEM_LEN, D_MODEL), 1.0),
        'cache_mem_k': nrm(ks[3], (DEPTH, DEC_BATCH, MEM_LEN, N_HEADS, HEAD_DIM), 1.0),
        'cache_mem_v': nrm(ks[4], (DEPTH, DEC_BATCH, MEM_LEN, N_HEADS, HEAD_DIM), 1.0),
        'state_conv': nrm(ks[5], (DEPTH, DEC_BATCH, CONV_W - 1, E_CONV), 0.5),
        'state_ssm_re': nrm(ks[6], (DEPTH, DEC_BATCH, N_GROUPS, P_STATE), 0.5),
        'state_ssm_im': nrm(ks[7], (DEPTH, DEC_BATCH, N_GROUPS, P_STATE), 0.5),
        'norm_g': 1.0 + nrm(ks[8], (DEPTH, D_MODEL), 0.02),
        'mem_norm_g': 1.0 + nrm(ks[9], (DEPTH, D_MODEL), 0.02),
        'w_in': nrm(ks[10], (DEPTH, D_MODEL, N_IN), D_MODEL ** -0.5),
        'conv_w': nrm(ks[11], (DEPTH, CONV_W, E_CONV), CONV_W ** -0.5),
        'w_conv_out': nrm(ks[12], (DEPTH, E_CONV, D_MODEL), E_CONV ** -0.5),
        'ssm_lambda_re': -0.5 + nrm(ks[13], (DEPTH, N_GROUPS, P_STATE), 0.01),
        'ssm_lambda_im': lam_im0 + nrm(ks[14], (DEPTH, N_GROUPS, P_STATE), 0.01),
        'ssm_log_dt': jax.random.uniform(ks[15], (DEPTH, N_GROUPS), f, math.log(1e-3), math.log(1e-1)),
        'ssm_b_re': nrm(ks[16], (DEPTH, N_GROUPS, P_STATE, GROUP), GROUP ** -0.5),
        'ssm_b_im': nrm(ks[17], (DEPTH, N_GROUPS, P_STATE, GROUP), GROUP ** -0.5),
        'ssm_c_re': nrm(ks[18], (DEPTH, N_GROUPS, GROUP, P_STATE), P_STATE ** -0.5),
        'ssm_c_im': nrm(ks[19], (DEPTH, N_GROUPS, GROUP, P_STATE), P_STATE ** -0.5),
        'ssm_d': 1.0 + nrm(ks[20], (DEPTH, E_SSM), 0.1),
        'w_glu_a': nrm(ks[21], (DEPTH, E_SSM, D_MODEL), E_SSM ** -0.5),
        'w_glu_b': nrm(ks[22], (DEPTH, E_SSM, D_MODEL), E_SSM ** -0.5),
        'w_mem_k': nrm(ks[23], (DEPTH, D_MODEL, E_ATTN), D_MODEL ** -0.5),
        'w_mem_v': nrm(ks[24], (DEPTH, D_MODEL, E_ATTN), D_MODEL ** -0.5),
        'w_attn_out': nrm(ks[25], (DEPTH, E_ATTN, D_MODEL), E_ATTN ** -0.5),
        'w_out': nrm(ks[26], (DEPTH, D_MODEL, D_MODEL), D_MODEL ** -0.5),
        'final_norm_g': 1.0 + nrm(ks[27], (D_MODEL,), 0.02),
    }


def reference(x_prompt, x_sample, mem_prompt, cache_mem_k, cache_mem_v, state_conv,
              state_ssm_re, state_ssm_im, norm_g, mem_norm_g, w_in, conv_w, w_conv_out,
              ssm_lambda_re, ssm_lambda_im, ssm_log_dt, ssm_b_re, ssm_b_im, ssm_c_re,
              ssm_c_im, ssm_d, w_glu_a, w_glu_b, w_mem_k, w_mem_v, w_attn_out, w_out,
              final_norm_g):
    xp = x_prompt
    xs = x_sample
    mk_p, mv_p, cv_p, sr_p, si_p = [], [], [], [], []
    cv_s, sr_s, si_s = [], [], []
    for l in range(DEPTH):
        p = {
            'norm_g': norm_g[l], 'w_in': w_in[l], 'conv_w': conv_w[l],
            'w_conv_out': w_conv_out[l], 'lam_re': ssm_lambda_re[l],
            'lam_im': ssm_lambda_im[l], 'log_dt': ssm_log_dt[l], 'b_re': ssm_b_re[l],
            'b_im': ssm_b_im[l], 'c_re': ssm_c_re[l], 'c_im': ssm_c_im[l],
            'd_skip': ssm_d[l], 'w_glu_a': w_glu_a[l], 'w_glu_b': w_glu_b[l],
            'w_attn_out': w_attn_out[l], 'w_out': w_out[l],
        }
        mem_n = _rmsnorm(mem_prompt, mem_norm_g[l])
        mk = (mem_n @ w_mem_k[l]).reshape(BATCH, MEM_LEN, N_HEADS, HEAD_DIM)
        mv = (mem_n @ w_mem_v[l]).reshape(BATCH, MEM_LEN, N_HEADS, HEAD_DIM)
        zc = jnp.zeros((BATCH, CONV_W - 1, E_CONV), xp.dtype)
        zs = jnp.zeros((BATCH, N_GROUPS, P_STATE), jnp.float32)
        xp, ncp, nrp, nip = _layer(xp, mk, mv, zc, zs, zs, p)
        mk_p.append(mk); mv_p.append(mv); cv_p.append(ncp); sr_p.append(nrp); si_p.append(nip)
        xs, ncs, nrs, nis = _layer(xs, cache_mem_k[l], cache_mem_v[l], state_conv[l],
                                   state_ssm_re[l], state_ssm_im[l], p)
        cv_s.append(ncs); sr_s.append(nrs); si_s.append(nis)
    y_prompt = _rmsnorm(xp, final_norm_g)
    y_sample = _rmsnorm(xs, final_norm_g)
    return (y_prompt, y_sample, jnp.stack(mk_p), jnp.stack(mv_p), jnp.stack(cv_p),
            jnp.stack(sr_p), jnp.stack(si_p), jnp.stack(cv_s), jnp.stack(sr_s),
            jnp.stack(si_s))
```

```python
import math
import numpy as np
import concourse.bass as bass
import concourse.mybir as mybir
from concourse.bass_utils import run_bass_kernel_spmd
from contextlib import ExitStack

F32 = mybir.dt.float32
BF16 = mybir.dt.bfloat16
I32 = mybir.dt.int32
AF = mybir.ActivationFunctionType
ALU = mybir.AluOpType

D = 2048
NIN = 14336
NCORE = 8
EPS = 1e-6
NM = 1024
NS = 64
NX = NM + NS + 2
S0 = NM
H0 = NM + NS
C_CB, C_CC, C_CH, C_CZ, C_SU, C_SZ, C_AQ, C_AZ, C_G = 0, 1024, 2048, 3072, 4096, 5120, 6144, 7168, 8192


class Res:
    __slots__ = ("w", "r", "name")

    def __init__(self, name=""):
        self.w = None
        self.r = []
        self.name = name


class Op:
    __slots__ = ("q", "fn", "deps", "flag", "val", "dma", "idx", "alldeps", "cost", "lat", "seq", "seg", "waits")


DEF_COST = {"pe": 0.15, "act": 0.8, "dve": 0.8, "pool": 1.0, "sp": 0.1}
RESCHED = True
KEEP_ORDER = ("pe",)
PE_FREE_SEGS = ()


class Sched:
    QS = ["pe", "act", "dve", "pool", "sp"]

    def __init__(self):
        self.all_ops = []
        self.streams = {}

    def op(self, q, fn, reads=(), writes=(), dma=None, after=(), cost=None, lat=0.0):
        o = Op()
        o.q = q
        o.fn = fn
        o.flag = False
        o.dma = dma
        o.val = 0
        o.idx = 0
        o.cost = cost if cost is not None else getattr(fn, "cost", None)
        if o.cost is None:
            o.cost = DEF_COST[q]
        o.lat = lat
        deps = []
        seen = set()

        def add(d):
            if d is None or id(d) in seen:
                return
            seen.add(id(d))
            deps.append(d)

        for r in reads:
            add(r.w)
        for w in writes:
            add(w.w)
            for x in w.r:
                add(x)
        for d in after:
            add(d)
        o.alldeps = deps
        for r in reads:
            r.r.append(o)
        for w in writes:
            w.w = o
            w.r = []
        if dma is not None:
            self.streams[dma] = True
        o.seq = len(self.all_ops)
        self.all_ops.append(o)
        return o

    def barrier(self):
        self.all_ops.append(None)

    def _schedule_segment(self, seg):
        import heapq
        inseg = {id(o) for o in seg}
        preds = {id(o): [d for d in o.alldeps if id(d) in inseg] for o in seg}
        last_stream = {}
        for o in seg:
            if o.dma is not None:
                p = last_stream.get(o.dma)
                if p is not None and all(id(p) != id(x) for x in preds[id(o)]):
                    preds[id(o)].append(p)
                last_stream[o.dma] = o
        last_q = {}
        for o in seg:
            if o.q in KEEP_ORDER and not (o.q == "pe" and self.cur_seg in PE_FREE_SEGS):
                p = last_q.get(o.q)
                if p is not None and all(id(p) != id(x) for x in preds[id(o)]):
                    preds[id(o)].append(p)
                last_q[o.q] = o
        if not RESCHED:
            out = {q: [] for q in self.QS}
            for o in seg:
                out[o.q].append(o)
            return out
        succ = {id(o): [] for o in seg}
        indeg = {}
        for o in seg:
            indeg[id(o)] = len(preds[id(o)])
            for d in preds[id(o)]:
                succ[id(d)].append(o)
        done = {}
        ready = {q: [] for q in self.QS}
        for o in seg:
            if indeg[id(o)] == 0:
                heapq.heappush(ready[o.q], (0.0, o.seq, o))
        qfree = {q: 0.0 for q in self.QS}
        out = {q: [] for q in self.QS}
        n = 0
        while n < len(seg):
            best = None
            for q in self.QS:
                hp = ready[q]
                if not hp:
                    continue
                cands = []
                while hp and hp[0][0] <= qfree[q]:
                    cands.append(heapq.heappop(hp))
                if cands:
                    c = min(cands, key=lambda x: x[1])
                    for x in cands:
                        if x is not c:
                            heapq.heappush(hp, x)
                    heapq.heappush(hp, c)
                    st = qfree[q]
                    key = (st, c[1])
                    pick = c
                else:
                    pick = hp[0]
                    st = pick[0]
                    key = (st, pick[1])
                if best is None or key < best[0]:
                    best = (key, q, pick, st)
            _, q, pick, st = best
            hp = ready[q]
            hp.remove(pick)
            heapq.heapify(hp)
            o = pick[2]
            out[q].append(o)
            qfree[q] = st + o.cost
            done[id(o)] = st + o.cost + o.lat
            n += 1
            for s_ in succ[id(o)]:
                indeg[id(s_)] -= 1
                if indeg[id(s_)] == 0:
                    rt = max(done[id(d)] for d in preds[id(s_)])
                    heapq.heappush(ready[s_.q], (rt, s_.seq, s_))
        self.est_time = getattr(self, "est_time", 0.0) + max(qfree.values())
        return out

    def finalize(self):
        self.queues = {q: [] for q in self.QS}
        segs = [[]]
        for o in self.all_ops:
            if o is None:
                segs.append([])
            else:
                segs[-1].append(o)
        for si, seg in enumerate(segs):
            for o in seg:
                o.seg = si
            self.cur_seg = si
            out = self._schedule_segment(seg)
            for q in self.QS:
                self.queues[q].extend(out[q])
            if si < len(segs) - 1:
                lasts = []
                for q in self.QS:
                    for o in reversed(self.queues[q]):
                        if o.dma is None and o.fn is not None:
                            lasts.append(o)
                            break
                lastdma = {}
                for q in self.QS:
                    for o in self.queues[q]:
                        if o.dma is not None:
                            lastdma[o.dma] = o
                lasts += list(lastdma.values())
                for q in self.QS:
                    b = Op()
                    b.q = q
                    b.fn = None
                    b.flag = False
                    b.dma = None
                    b.val = 0
                    b.idx = 0
                    b.seg = -1
                    b.alldeps = lasts
                    self.queues[q].append(b)
        cnt = {}
        for q in self.QS:
            for o in self.queues[q]:
                if o.dma is not None:
                    cnt[o.dma] = cnt.get(o.dma, 0) + 1
                    o.idx = cnt[o.dma]
        pos = {}
        for q in self.QS:
            for i_, o in enumerate(self.queues[q]):
                pos[id(o)] = i_
        for q in self.QS:
            for o in self.queues[q]:
                best = {}
                for d in o.alldeps:
                    if o.seg != -1 and d.seg != o.seg:
                        continue
                    if d.dma is None and d.q == "pe" and q == "pe":
                        continue
                    key = d.q if d.dma is None else ("dma", d.dma)
                    if key not in best or pos[id(d)] > pos[id(best[key])]:
                        best[key] = d
                o.waits = list(best.values())
                for d in o.waits:
                    if d.dma is None:
                        d.flag = True
        for q in self.QS:
            c = 0
            for o in self.queues[q]:
                if o.dma is None and o.flag:
                    c += 1
                    o.val = c

    def emit(self, nc, stack):
        self.finalize()
        qsem = {q: stack.enter_context(nc.semaphore("q_" + q)) for q in self.QS}
        ssem = {s: stack.enter_context(nc.semaphore("s_" + s)) for s in self.streams}
        block = stack.enter_context(nc.Block())

        def tok(d):
            if d.dma is not None:
                return ssem[d.dma], 16 * d.idx
            return qsem[d.q], d.val

        def run(q, eng):
            waited = {}
            for o in self.queues[q]:
                for d in o.waits:
                    sem, val = tok(d)
                    if waited.get(id(sem), 0) >= val:
                        continue
                    eng.wait_ge(sem, val)
                    waited[id(sem)] = val
                if o.fn is None:
                    continue
                ins = o.fn(eng)
                if o.dma is not None:
                    ins.then_inc(ssem[o.dma], 16)
                elif o.flag:
                    ins.then_inc(qsem[q], 1)
            fin = {}
            for o in self.queues[q]:
                if o.dma is not None:
                    fin[o.dma] = max(fin.get(o.dma, 0), o.idx)
            for s, n in fin.items():
                eng.wait_ge(ssem[s], 16 * n)

        @block.tensor
        def _(e):
            run("pe", e)

        @block.scalar
        def _(e):
            run("act", e)

        @block.vector
        def _(e):
            run("dve", e)

        @block.gpsimd
        def _(e):
            run("pool", e)

        @block.sync
        def _(e):
            run("sp", e)


class Builder:
    def __init__(self, dbg=()):
        self.dbg = set(dbg)
        self.nc = bass.Bass("TRN2", target_bir_lowering=False)
        self.S = Sched()
        self.stack = ExitStack()
        self.ins = {}
        self.outs = {}
        self.rr = 0

    def din(self, name, shape, dt=F32):
        t = self.nc.dram_tensor(name, list(shape), dt, kind="ExternalInput").ap()
        self.ins[name] = t
        return t

    def dout(self, name, shape, dt=F32):
        t = self.nc.dram_tensor(name, list(shape), dt, kind="ExternalOutput").ap()
        self.outs[name] = t
        return t

    def sb(self, name, shape, dt=F32):
        return self.stack.enter_context(self.nc.sbuf_tensor(name, list(shape), dt))

    def ps(self, name, shape, dt=F32):
        return self.stack.enter_context(self.nc.psum_tensor(name, list(shape), dt))

    def pe(self, fn, reads=(), writes=(), drain=False):
        after = [self.last_pe] if (drain and getattr(self, "last_pe", None) is not None) else []
        o = self.S.op("pe", fn, reads, writes, after=after)
        self.last_pe = o
        return o

    def act(self, fn, reads=(), writes=()):
        return self.S.op("act", fn, reads, writes)

    def dve(self, fn, reads=(), writes=()):
        return self.S.op("dve", fn, reads, writes)

    def pool(self, fn, reads=(), writes=()):
        return self.S.op("pool", fn, reads, writes)

    def alt(self, fn, reads=(), writes=()):
        self.rr ^= 1
        return self.S.op("act" if self.rr else "dve", fn, reads, writes)

    def dma(self, out, in_, reads=(), writes=(), stream="ld", q="sp", **kw):
        n = 1
        for x in out.shape:
            n *= x
        lat = 2.5 + n * 4 / 200e3
        cost = 1.0 if q == "pool" else 0.15
        if kw.get("allow_slow_non_contiguous"):
            lat += n * 0.004
        return self.S.op(q, lambda e: e.dma_start(out=out, in_=in_, **kw), reads, writes, dma=stream, cost=cost, lat=lat)

    def dump(self, name, ap_sb, res, shape, dt=F32):
        if name not in self.dbg:
            return
        o = self.dout("dbg_" + name, shape, dt)
        self.dma(o, ap_sb, reads=[res], stream="dbg")


class Arena:
    def __init__(self, B, words):
        self.t = B.sb("arena", [128, words], F32)
        self.words = words
        self.top = 0

    def alloc(self, shape, dt=F32):
        n = 1
        for x in shape:
            n *= x
        w = n if dt == F32 else (n + 1) // 2
        w = (w + 7) // 8 * 8
        off = self.top
        self.top += w
        assert self.top <= self.words, f"arena overflow {self.top} > {self.words}"
        v = self.t[:, off:off + w]
        if dt != F32:
            v = v.bitcast(dt)
        v = v[:, 0:n]
        if len(shape) == 2:
            v = v.rearrange("p (a b) -> p a b", a=shape[0])
        elif len(shape) == 3:
            v = v.rearrange("p (a b c) -> p a b c", a=shape[0], b=shape[1])
        elif len(shape) == 4:
            v = v.rearrange("p (a b c d) -> p a b c d", a=shape[0], b=shape[1], c=shape[2])
        return v


def cp(e, out, in_):
    if hasattr(e, "tensor_copy"):
        return e.tensor_copy(out=out, in_=in_)
    return e.activation(out=out, in_=in_, func=AF.Copy)


def mm(out, lhsT, rhs, start, stop):
    f = lambda e: e.matmul(out, lhsT, rhs, start=start, stop=stop)
    n = rhs.shape[-1]
    f.cost = 0.03 + max(n, 64) * (4 if rhs.dtype == F32 else 1) / 2000.0
    return f


MAGIC = 12582912.0
TWO_PI = 2.0 * math.pi
PI_S = 3.1415925
BIGW = 8704
ARENA_W = 23300
NJ = 272
SCR1_W = 2112
YW = 144


def build(dbg=(), stop=None, nloop=8):
    B = Builder(dbg)

    def finish():
        B.S.emit(B.nc, B.stack)
        B.stack.close()
        return B

    nc = B.nc
    R = Res
    xm = B.din("xm", [NM, D])
    xp = B.din("xp", [NM, D])
    xs = B.din("xs", [NS, D])
    mem = B.din("mem", [256, D])
    ck = B.din("ck", [16, 256, 1024])
    cv = B.din("cv", [16, 256, 1024])
    sconv = B.din("sconv", [16, 2, 1024])
    ssr = B.din("ssr", [16, 4096])
    ssi = B.din("ssi", [16, 4096])
    norm_g = B.din("norm_g", [D])
    mem_norm_g = B.din("mem_norm_g", [D])
    final_g = B.din("final_g", [D])
    w_in = B.din("w_in", [D, NIN])
    conv_w = B.din("conv_w", [3, 1024])
    w_conv_out = B.din("w_conv_out", [1024, D])
    lam_re = B.din("lam_re", [64, 64])
    lam_im = B.din("lam_im", [64, 64])
    log_dt = B.din("log_dt", [64])
    b_re = B.din("b_re", [64, 64, 16])
    b_im = B.din("b_im", [64, 64, 16])
    c_re = B.din("c_re", [64, 16, 64])
    c_im = B.din("c_im", [64, 16, 64])
    ssm_d = B.din("ssm_d", [1024])
    w_glu_a = B.din("w_glu_a", [1024, D])
    w_glu_b = B.din("w_glu_b", [1024, D])
    w_mem_k = B.din("w_mem_k", [D, 1024])
    w_mem_v = B.din("w_mem_v", [D, 1024])
    w_attn_out = B.din("w_attn_out", [1024, D])
    w_out = B.din("w_out", [D, D])

    ym_o = B.dout("ym", [NM, D])
    ys_o = B.dout("ysm", [NS, D])
    mk_o = B.dout("mk", [128, 1024])
    mv_o = B.dout("mv", [128, 1024])
    ncp_o = B.dout("ncp", [2, 1024])
    spr_o = B.dout("spr", [4096])
    spi_o = B.dout("spi", [4096])
    ncs_o = B.dout("ncs", [16, 2, 1024])
    snr_o = B.dout("snr", [16, 4096])
    sni_o = B.dout("sni", [16, 4096])

    scr1 = [nc.dram_tensor(f"scr1_{i}", [128, SCR1_W], BF16).ap() for i in range(2)]
    scr2 = [nc.dram_tensor(f"scr2_{i}", [128, 8 * YW], F32).ap() for i in range(2)]

    def DAP(t, off, pat):
        return bass.AP(t.tensor, off, pat)

    hT = B.sb("hT", [128, 16, NX], BF16)
    BIG = B.sb("BIG", [128, BIGW], F32)
    wbs = [B.sb(f"wb{i}", [128, 2048], BF16) for i in range(3)]
    NWB = 3
    identb = B.sb("identb", [128, 128], BF16)
    identf = B.sb("identf", [128, 128], F32)
    onesb = B.sb("onesb", [128, 128], BF16)
    cact = B.sb("cact", [128, 8, 1088], BF16)
    ysT = B.sb("ysT", [128, 8, 1088], BF16)
    stat = B.sb("stat", [128, 64], F32)
    pall = B.ps("pall", [128, 8, 512], F32)
    AR = Arena(B, ARENA_W)

    hTp = BIG[:, 0:8192].bitcast(BF16).rearrange("p (a b) -> p a b", a=16)
    merged = BIG[:, 0:BIGW].bitcast(BF16).rearrange("p (a b) -> p a b", a=16)

    r_pb = [R(f"pb{i}") for i in range(8)]
    r_hT = [R(f"hT{i}") for i in range(16)]
    r_ident = R("ident")
    r_wb = [R(f"wb{i}") for i in range(3)]
    r_stat = [R(f"st{i}") for i in range(32)]
    r_cact = [R(f"cact{i}") for i in range(8)]
    r_ys = [R(f"ys{i}") for i in range(8)]

    fctr = [0]

    def fence(q, reads):
        r = R("fence")
        i = fctr[0] % 2
        fctr[0] += 1
        if q == "act":
            B.act(lambda e: e.activation(out=stat[0:1, 56 + i:57 + i], in_=stat[0:1, 58:59], func=AF.Copy), reads=reads,
                  writes=[r])
        else:
            B.dve(lambda e: e.tensor_copy(out=stat[0:1, 60 + i:61 + i], in_=stat[0:1, 59:60]), reads=reads, writes=[r])
        return r

    def bank(b, n=512):
        return pall[:, b, 0:n]

    def pset(s):
        return pall[:, 3 * s:3 * s + 3, :].rearrange("p b c -> p (b c)")

    B.pool(lambda e: e.memset(identf[:], 0.0), writes=[r_ident])
    B.pool(lambda e: e.affine_select(out=identf[:], in_=identf[:], pattern=[[-1, 128]],
                                     compare_op=ALU.not_equal, fill=1.0, base=0,
                                     channel_multiplier=1), reads=[r_ident], writes=[r_ident])
    B.dve(lambda e: e.tensor_copy(out=identb[:], in_=identf[:]), reads=[r_ident], writes=[r_ident])
    B.dve(lambda e: e.memset(onesb[:], 1.0), writes=[r_ident])
    B.dve(lambda e: e.memset(stat[:], 0.0), writes=r_stat)

    wcount = [0]

    def load_w(parts, kt):
        assert len(parts) == 1
        p = parts[0]
        sl = wcount[0] % NWB
        wcount[0] += 1
        assert p.shape[1] == 128 and kt * 128 <= 2048
        v = wbs[sl][:, 0:kt * 128].rearrange("p (a c) -> p a c", a=kt)
        B.dma(v, p.rearrange("(a p) c -> p a c", p=128), writes=[r_wb[sl]], stream=f"w{sl}", q="pool")
        return v, r_wb[sl]

    setctr = [0]

    def project(wv, r_w, col, nkt, act, r_act, segs, s=None):
        if s is None:
            s = setctr[0] % 2
            setctr[0] += 1
        for kt in range(nkt):
            for (c0, n, bi) in segs:
                bk = 3 * s + bi
                B.pe(mm(pall[:, bk, 0:n], wv[:, kt, col:col + 128], act[:, kt, c0:c0 + n], kt == 0, kt == nkt - 1),
                     reads=[r_w, r_act[kt]], writes=[r_pb[bk]])
        return s

    SEG_MS = [(0, 512, 0), (512, 512, 1), (1024, NS, 2)]
    SEG_MSH = [(0, 512, 0), (512, 512, 1), (1024, NS + 2, 2)]

    def rset(s):
        return [r_pb[3 * s], r_pb[3 * s + 1], r_pb[3 * s + 2]]

    def norm_transpose(tiles, gsrc, AR_):
        xst = [AR_.alloc([D]) for _ in range(2)]
        xbf = [AR_.alloc([D], BF16) for _ in range(2)]
        junk = AR_.alloc([D], BF16)
        gbc = AR_.alloc([D])
        r_xst = [R(), R()]
        r_xbf = [R(), R()]
        r_junk = R()
        r_gbc = R()
        B.dma(gbc, gsrc.partition_broadcast(128), writes=[r_gbc], stream="gbc")
        for ti, (src, rows, dstT, r_dst, col0) in enumerate(tiles):
            sl = ti % 2
            xt, xb = xst[sl], xbf[sl]
            B.dma(xt[:rows, :], src, writes=[r_xst[sl]], stream=f"xld{sl}")
            si = ti % 32
            sq = stat[:rows, 2 * si:2 * si + 1]
            rs = stat[:rows, 2 * si + 1:2 * si + 2]
            r_s = r_stat[si]
            B.act(lambda e, xt=xt, rows=rows, sq=sq: e.activation(out=junk[:rows, :], in_=xt[:rows, :], func=AF.Square,
                                                                 accum_out=sq),
                  reads=[r_xst[sl]], writes=[r_junk, r_s])
            B.act(lambda e, sq=sq, rs=rs: e.activation(out=rs, in_=sq, func=AF.Sqrt, scale=1.0 / D, bias=EPS),
                  reads=[r_s], writes=[r_s])
            B.dve(lambda e, rs=rs: e.reciprocal(out=rs, in_=rs), reads=[r_s], writes=[r_s])
            B.dve(lambda e, xt=xt, xb=xb, rows=rows, rs=rs: e.scalar_tensor_tensor(
                out=xb[:rows, :], in0=xt[:rows, :], scalar=rs, in1=gbc[:rows, :], op0=ALU.mult, op1=ALU.mult),
                reads=[r_xst[sl], r_s, r_gbc], writes=[r_xbf[sl]])
            for g4 in range(4):
                pbk = 6 + (ti * 4 + g4) % 2
                pv = pall[:, pbk, :].bitcast(BF16)
                for i in range(4):
                    dt_ = g4 * 4 + i
                    B.pe(lambda e, pv=pv, xb=xb, rows=rows, dt_=dt_, i=i: e.transpose(
                        out=pv[:, i * 128:i * 128 + rows], in_=xb[:rows, dt_ * 128:(dt_ + 1) * 128],
                        identity=identb[:rows, :rows]),
                        reads=[r_xbf[sl], r_ident], writes=[r_pb[pbk]])
                B.alt(lambda e, pv=pv, dstT=dstT, g4=g4, rows=rows, col0=col0: cp(
                    e, dstT[:, g4 * 4:(g4 + 1) * 4, col0:col0 + rows],
                    pv[:, 0:512].rearrange("p (a b) -> p a b", a=4)[:, :, 0:rows]),
                    reads=[r_pb[pbk]], writes=r_dst[g4 * 4:(g4 + 1) * 4])

    suTp = AR.alloc([8, 1024], BF16)
    r_suTp = [R(f"suTp{i}") for i in range(8)]
    names = ["lamr", "lami", "dtl", "dt", "lrdt", "mag", "imag", "ang", "angw", "sn", "cs", "ar", "ai", "air", "aii",
             "den", "rden", "am1", "fr", "fi", "tA", "tB", "phi8", "p8s", "r8", "tC"]
    P = {n: AR.alloc([32]) for n in names}
    rP = {n: R(n) for n in names}
    Ppr = AR.alloc([17, 32])
    Ppi = AR.alloc([17, 32])
    Pmr = AR.alloc([8, 32])
    Pmi = AR.alloc([8, 32])
    r_Pp = R("Pp")
    r_Pm = R("Pm")
    Bbr = AR.alloc([32, 16])
    Bbi = AR.alloc([32, 16])
    cnr = AR.alloc([32, 16])
    cni = AR.alloc([32, 16])
    r_Bb = R("Bb")
    r_cn = R("cn")
    Dcol = AR.alloc([64])
    cw = AR.alloc([8, 3])
    jtab = AR.alloc([256])
    mask1 = AR.alloc([128])
    r_const = R("const")
    sconvT = AR.alloc([8, 16, 2])
    ncp = AR.alloc([8, 2])
    ncs = AR.alloc([8, 16, 2])
    r_sconvT = R("sconvT")
    r_ncp = R("ncp")
    r_ncs = R("ncs")
    SinTr = AR.alloc([32, 16])
    SinTi = AR.alloc([32, 16])
    Spr = AR.alloc([32, 16], BF16)
    Spi = AR.alloc([32, 16], BF16)
    ZSr = AR.alloc([32, 16])
    ZSi = AR.alloc([32, 16])
    SFr = AR.alloc([32])
    SFi = AR.alloc([32])
    r_SinT = R("SinT")
    r_Sp = R("Sp")
    r_ZS = [R(f"ZS{i}") for i in range(8)]
    r_SF = [R(f"SF{i}") for i in range(8)]
    ld_state = {"stream": "par0", "ops": [], "res": []}

    def ld(dst, src, res, q="act", **kw):
        o = B.dma(dst, src, writes=[res], stream=ld_state["stream"], q=q, **kw)
        ld_state["ops"].append(o)
        ld_state["res"].append(res)

    def ld_group_done():
        last = ld_state["ops"][-1]
        for r_ in ld_state["res"]:
            r_.w = last
        ld_state["ops"], ld_state["res"] = [], []

    nce = dict(allow_slow_non_contiguous=True)
    stf = ysT[:, :, :].rearrange("p a b -> p (a b)").bitcast(F32)
    stc = cact[:, :, :].rearrange("p a b -> p (a b)").bitcast(F32)
    r_stf = R("stf")
    r_stc = R("stc")
    ld_state["stream"] = "par0"
    q0 = "sp"
    ld(stf[0:32, 0:128], lam_re.rearrange("(a b) p -> a (b p)", b=2), r_stf, q=q0)
    ld(stf[0:32, 128:256], lam_im.rearrange("(a b) p -> a (b p)", b=2), r_stf, q=q0)
    ld(stf[0:32, 384:386], log_dt.rearrange("(a b) -> a b", b=2), r_stf, q=q0)
    ld(stf[0:64, 512:528], ssm_d.rearrange("(g c) -> g c", c=16), r_stf, q=q0)
    for i in range(3):
        ld(cw[:, :, i], DAP(conv_w, i * 1024, [[1, 128], [128, 8]]), r_const, q=q0, **nce)
    ld(stf[0:32, 2048:3072], sconv.rearrange("s r c -> (s r) c"), r_stf, q=q0)
    for (src_, base) in ((c_re, 0), (c_im, 1024)):
        st3 = stc[0:64, base:base + 1024].rearrange("p (t x) -> p t x", t=8)
        for t in range(8):
            for gpl in range(4):
                ld(st3[gpl * 16:(gpl + 1) * 16, t, :].rearrange("p (a b) -> p a b", a=2),
                   DAP(src_, (8 * t + 2 * gpl) * 1024, [[64, 16], [1024, 2], [1, 64]]), r_stc, q=q0)
    ld_group_done()
    B.dve(lambda e: e.tensor_copy(out=stf[0:32, 256:384].rearrange("p (a b) -> p a b", a=2),
                                  in_=stf[0:32, 384:386].unsqueeze(2).to_broadcast([32, 2, 64])), reads=[r_stf], writes=[r_stf])
    B.dve(lambda e: e.tensor_copy(out=stf[0:64, 640:768].rearrange("p (a b) -> p a b", a=8),
                                  in_=stf[0:64, 512:528].unsqueeze(1).to_broadcast([64, 8, 16])), reads=[r_stf], writes=[r_stf])

    def ptr(out_ps, in_sb, rows, res_in, bk):
        B.pe(lambda e: e.transpose(out=out_ps, in_=in_sb, identity=identf[0:rows, 0:rows]), reads=[res_in, r_ident],
             writes=[r_pb[bk]])

    for i, nm in enumerate(("lamr", "lami", "dtl")):
        ptr(pall[:, 6, i * 32:(i + 1) * 32], stf[0:32, i * 128:(i + 1) * 128], 32, r_stf, 6)
    ptr(pall[:, 6, 128:192], stf[0:64, 640:768], 64, r_stf, 6)
    for i, nm in enumerate(("lamr", "lami", "dtl")):
        B.alt(lambda e, i=i, nm=nm: cp(e, P[nm], pall[:, 6, i * 32:(i + 1) * 32]), reads=[r_pb[6]], writes=[rP[nm]])
    B.alt(lambda e: cp(e, Dcol, pall[:, 6, 128:192]), reads=[r_pb[6]], writes=[r_const])
    if stop == "e1":
        return finish()
    for t in range(8):
        ptr(pall[:, 7, 64 + t * 32:64 + (t + 1) * 32], stf[0:32, 2048 + t * 128:2048 + (t + 1) * 128], 32, r_stf, 7)
    B.alt(lambda e: cp(e, sconvT.rearrange("p t s r -> p t (s r)"), pall[:, 7, 64:320].rearrange("p (a b) -> p a b", a=8)),
          reads=[r_pb[7]], writes=[r_sconvT])
    if stop == "e2":
        return finish()
    for (dstC, base, bk) in ((cnr, 0, 6), (cni, 1024, 7)):
        for t in range(8):
            ptr(pall[:, bk, t * 64:(t + 1) * 64], stc[0:64, base + t * 128:base + (t + 1) * 128], 64, r_stc, bk)
        B.alt(lambda e, dstC=dstC, bk=bk: cp(e, dstC.rearrange("p a b -> p (a b)"), pall[:, bk, :]), reads=[r_pb[bk]],
              writes=[r_cn])
    if stop == "e3":
        return finish()
    p1_top = AR.top
    r_hTp = [R(f"hTp{i}") for i in range(16)]
    xm_v = xm.rearrange("(j k) d -> k j d", k=8)
    xp_v = xp.rearrange("(j k) d -> k j d", k=8)
    tiles = []
    for k in range(8):
        tiles.append((xp_v[k], 128, hTp, r_hTp, k * 128))
    for k in range(8):
        tiles.append((xm_v[k], 128, hT, r_hT, k * 128))
    tiles.append((xs, NS, hT, r_hT, S0))
    norm_transpose(tiles, norm_g, AR)
    B.dve(lambda e: e.tensor_copy(out=hT[:, :, H0:H0 + 2], in_=hTp[:, :, 6 * 128 + 127:8 * 128:128]),
          reads=r_hTp, writes=r_hT)
    for t in range(8):
        wv, r_w = load_w([w_in[:, C_SU + t * 128:C_SU + (t + 1) * 128]], 16)
        s = project(wv, r_w, 0, 16, hTp, r_hTp, [(0, 512, 0), (512, 512, 1)])
        B.alt(lambda e, s=s, t=t: cp(e, suTp[:, t, :], pset(s)[:, 0:1024]), reads=rset(s)[0:2], writes=[r_suTp[t]])
    B.S.barrier()
    if stop == "p1":
        return finish()

    AR.top = p1_top
    BT = [0]

    def balloc(shape, dt=F32):
        n = 1
        for x in shape:
            n *= x
        w = n if dt == F32 else (n + 1) // 2
        w = (w + 7) // 8 * 8
        off = BT[0]
        BT[0] += w
        assert BT[0] <= BIGW, "BIG overflow"
        v = BIG[:, off:off + w]
        if dt != F32:
            v = v.bitcast(dt)
        v = v[:, 0:n]
        if len(shape) == 2:
            v = v.rearrange("p (a b) -> p a b", a=shape[0])
        elif len(shape) == 3:
            v = v.rearrange("p (a b c) -> p a b c", a=shape[0], b=shape[1])
        return v

    WTr = [AR.alloc([8, 64], BF16) for _ in range(2)]
    WTi = [AR.alloc([8, 64], BF16) for _ in range(2)]
    CS2r = [AR.alloc([8, 128], BF16) for _ in range(2)]
    CS2i = [AR.alloc([8, 128], BF16) for _ in range(2)]
    Mg = [AR.alloc([8, 128], BF16) for _ in range(2)]
    r_WT = [R("WT0"), R("WT1")]
    r_CS2 = [R("CS20"), R("CS21")]
    r_Mg = [R("Mg0"), R("Mg1")]
    Wnr = AR.alloc([128])
    Wni = AR.alloc([128])
    CSnr = AR.alloc([128])
    CSni = AR.alloc([128])
    gt1 = AR.alloc([128])
    gt2 = AR.alloc([128])
    T2r = AR.alloc([128])
    T2i = AR.alloc([128])
    CSzr = AR.alloc([128])
    CSzi = AR.alloc([128])
    hm = AR.alloc([2])
    r_T2 = R("T2")
    r_CSz = R("CSz")
    r_Wn = R("Wn")
    r_CSn = R("CSn")
    r_gt = R("gt")
    rcos = AR.alloc([256])
    rsin = AR.alloc([256])
    rt1 = AR.alloc([256])
    rt2 = AR.alloc([256])
    rzr = AR.alloc([256])
    rzi = AR.alloc([256])
    r_rot = R("rot")
    suTb = AR.alloc([1088], BF16)
    r_suTb = R("suTb")
    r_suTb2 = R("suTb2")
    Ub = [AR.alloc([8, NJ], BF16) for _ in range(2)]
    r_Ub = [[R(f"Ub{a}_{i}") for i in range(20)] for a in range(2)]
    r_scr1 = [[R(), R()] for _ in range(2)]
    r_scr2 = [[R() for _ in range(12)] for _ in range(2)]
    Spvr = [AR.alloc([4, 132], BF16) for _ in range(2)]
    Spvi = [AR.alloc([4, 132], BF16) for _ in range(2)]
    r_Spv = [R("Spv0"), R("Spv1")]
    r_Ysb = R("Ysb")
    print("arena top after S5 allocs", AR.top, "of", ARENA_W)
    Ysb = balloc([8, YW])
    tmp_cc = balloc([NX])
    vext = balloc([10, 128])
    vs = balloc([16, 6])
    vh = balloc([2])
    acc = balloc([1088])
    tmp_s = balloc([1088])
    bnr = balloc([32, 16])
    bni = balloc([32, 16])
    r_tmpcc = R("tmpcc")
    r_vext = R("vext")
    r_acc = R("acc")
    r_acc2 = R("acc2")
    r_tmps = R("tmps")
    r_bn = R("bn")
    print("BIG top", BT[0], "of", BIGW)

    ld_state["stream"] = "par1"
    ld(bnr, DAP(b_re, 0, [[16, 128], [2048, 32], [1, 16]]), r_bn)
    ld(bni, DAP(b_im, 0, [[16, 128], [2048, 32], [1, 16]]), r_bn)
    ld_group_done()
    for q4 in range(4):
        B.dma(acc[0:16, 0:1024], ssr[:, q4 * 1024:(q4 + 1) * 1024], writes=[r_acc], stream="sin0")
        B.dma(tmp_s[0:16, 0:1024], ssi[:, q4 * 1024:(q4 + 1) * 1024], writes=[r_tmps], stream="sin1")
        for gpl in range(8):
            B.pe(lambda e, gpl=gpl: e.transpose(out=pall[:, 6, gpl * 16:(gpl + 1) * 16], in_=acc[0:16, gpl * 128:(gpl + 1) * 128],
                                                identity=identf[0:16, 0:16]), reads=[r_acc, r_ident], writes=[r_pb[6]])
            B.pe(lambda e, gpl=gpl: e.transpose(out=pall[:, 7, gpl * 16:(gpl + 1) * 16], in_=tmp_s[0:16, gpl * 128:(gpl + 1) * 128],
                                                identity=identf[0:16, 0:16]), reads=[r_tmps, r_ident], writes=[r_pb[7]])
        B.act(lambda e, q4=q4: cp(e, SinTr[:, q4 * 8:(q4 + 1) * 8, :].rearrange("p a b -> p (a b)"), pall[:, 6, 0:128]),
              reads=[r_pb[6]], writes=[r_SinT])
        B.dve(lambda e, q4=q4: cp(e, SinTi[:, q4 * 8:(q4 + 1) * 8, :].rearrange("p a b -> p (a b)"), pall[:, 7, 0:128]),
              reads=[r_pb[7]], writes=[r_SinT])
    B.pool(lambda e: e.iota(jtab, [[1, 256]], base=0, channel_multiplier=0, allow_small_or_imprecise_dtypes=True),
           writes=[r_const])
    B.pool(lambda e: e.memset(hm[0:64, 0:1], 1.0), writes=[r_const])
    B.pool(lambda e: e.memset(hm[64:128, 0:1], 0.0), writes=[r_const])
    B.pool(lambda e: e.memset(hm[0:64, 1:2], 0.0), writes=[r_const])
    B.pool(lambda e: e.memset(hm[64:128, 1:2], 1.0), writes=[r_const])
    for sl_ in range(2):
        B.pool(lambda e, sl_=sl_: e.memset(Ub[sl_][64:128, :, 256:272], 0.0), writes=[r_Ub[sl_][19]])
    B.pool(lambda e: e.memset(mask1, 1.0), writes=[r_const])
    B.pool(lambda e: e.affine_select(out=mask1, in_=mask1, pattern=[[16, 8], [0, 16]],
                                     compare_op=ALU.is_ge, fill=0.0, base=15, channel_multiplier=-1),
           reads=[r_const], writes=[r_const])

    def tt(out, a, b, op, reads, writes):
        B.dve(lambda e: e.tensor_tensor(out=out, in0=a, in1=b, op=op), reads=reads, writes=writes)

    def ts(out, a, s1, s2, op0, op1, reads, writes):
        if s2 is None:
            B.dve(lambda e: e.tensor_scalar(out=out, in0=a, scalar1=s1, scalar2=None, op0=op0), reads=reads, writes=writes)
        else:
            B.dve(lambda e: e.tensor_scalar(out=out, in0=a, scalar1=s1, scalar2=s2, op0=op0, op1=op1), reads=reads,
                  writes=writes)

    def stt(out, a, s, b, op0, op1, reads, writes):
        B.dve(lambda e: e.scalar_tensor_tensor(out=out, in0=a, scalar=s, in1=b, op0=op0, op1=op1), reads=reads,
              writes=writes)

    def actf(out, a, func, reads, writes, scale=1.0, bias=0.0):
        B.act(lambda e: e.activation(out=out, in_=a, func=func, scale=scale, bias=bias), reads=reads, writes=writes)

    def pp(n):
        return P[n], rP[n]

    def ptt(o, a, b, op):
        tt(P[o], P[a], P[b], op, [rP[a], rP[b]], [rP[o]])

    def wrap(o, a):
        ts(P["tC"], P[a], 1.0 / TWO_PI, MAGIC, ALU.mult, ALU.add, [rP[a]], [rP["tC"]])
        ts(P["tC"], P["tC"], MAGIC, None, ALU.subtract, None, [rP["tC"]], [rP["tC"]])
        stt(P[o], P["tC"], -TWO_PI, P[a], ALU.mult, ALU.add, [rP["tC"], rP[a]], [rP[o]])
        ts(P[o], P[o], -PI_S, PI_S, ALU.max, ALU.min, [rP[o]], [rP[o]])

    actf(P["dt"], P["dtl"], AF.Exp, [rP["dtl"]], [rP["dt"]])
    ptt("lrdt", "lamr", "dt", ALU.mult)
    actf(P["mag"], P["lrdt"], AF.Exp, [rP["lrdt"]], [rP["mag"]])
    actf(P["imag"], P["lrdt"], AF.Exp, [rP["lrdt"]], [rP["imag"]], scale=-1.0)
    actf(P["r8"], P["lrdt"], AF.Exp, [rP["lrdt"]], [rP["r8"]], scale=8.0)
    ptt("ang", "lami", "dt", ALU.mult)
    wrap("angw", "ang")
    actf(P["sn"], P["angw"], AF.Sin, [rP["angw"]], [rP["sn"]])
    actf(P["tA"], P["angw"], AF.Abs, [rP["angw"]], [rP["tA"]])
    actf(P["cs"], P["tA"], AF.Sin, [rP["tA"]], [rP["cs"]], scale=-1.0, bias=math.pi / 2)
    ptt("ar", "mag", "cs", ALU.mult)
    ptt("ai", "mag", "sn", ALU.mult)
    ptt("air", "imag", "cs", ALU.mult)
    stt(P["aii"], P["imag"], -1.0, P["sn"], ALU.mult, ALU.mult, [rP["imag"], rP["sn"]], [rP["aii"]])
    ts(P["tB"], P["angw"], 8.0, None, ALU.mult, None, [rP["angw"]], [rP["tB"]])
    wrap("phi8", "tB")
    ts(P["p8s"], P["phi8"], 1.0 / TWO_PI, None, ALU.mult, None, [rP["phi8"]], [rP["p8s"]])
    ptt("den", "lamr", "lamr", ALU.mult)
    ptt("tA", "lami", "lami", ALU.mult)
    ptt("den", "den", "tA", ALU.add)
    B.dve(lambda e: e.reciprocal(out=P["rden"], in_=P["den"]), reads=[rP["den"]], writes=[rP["rden"]])
    ts(P["am1"], P["ar"], -1.0, None, ALU.add, None, [rP["ar"]], [rP["am1"]])
    ptt("tA", "am1", "lamr", ALU.mult)
    ptt("tB", "ai", "lami", ALU.mult)
    ptt("tA", "tA", "tB", ALU.add)
    ptt("fr", "tA", "rden", ALU.mult)
    ptt("tA", "ai", "lamr", ALU.mult)
    ptt("tB", "am1", "lami", ALU.mult)
    ptt("tA", "tA", "tB", ALU.subtract)
    ptt("fi", "tA", "rden", ALU.mult)
    frb = P["fr"].unsqueeze(2).to_broadcast([128, 32, 16])
    fib = P["fi"].unsqueeze(2).to_broadcast([128, 32, 16])
    t16a = balloc([32, 16])
    t16b = balloc([32, 16])
    r_t16 = R("t16")
    tt(t16a, bnr, frb, ALU.mult, [r_bn, rP["fr"]], [r_t16])
    tt(t16b, bni, fib, ALU.mult, [r_bn, rP["fi"]], [r_t16])
    tt(Bbr, t16a, t16b, ALU.subtract, [r_t16], [r_Bb])
    tt(t16a, bni, frb, ALU.mult, [r_bn, rP["fr"]], [r_t16])
    tt(t16b, bnr, fib, ALU.mult, [r_bn, rP["fi"]], [r_t16])
    tt(Bbi, t16a, t16b, ALU.add, [r_t16], [r_Bb])

    def powers(Pr, Pi, r_P, nk, br, bi):
        B.dve(lambda e: e.memset(Pr[:, 0, :], 1.0), writes=[r_P])
        B.dve(lambda e: e.memset(Pi[:, 0, :], 0.0), writes=[r_P])
        for k in range(1, nk):
            tt(P["tA"], Pr[:, k - 1, :], P[br], ALU.mult, [r_P, rP[br]], [rP["tA"]])
            tt(P["tB"], Pi[:, k - 1, :], P[bi], ALU.mult, [r_P, rP[bi]], [rP["tB"]])
            tt(Pr[:, k, :], P["tA"], P["tB"], ALU.subtract, [rP["tA"], rP["tB"]], [r_P])
            tt(P["tA"], Pr[:, k - 1, :], P[bi], ALU.mult, [r_P, rP[bi]], [rP["tA"]])
            tt(P["tB"], Pi[:, k - 1, :], P[br], ALU.mult, [r_P, rP[br]], [rP["tB"]])
            tt(Pi[:, k, :], P["tA"], P["tB"], ALU.add, [rP["tA"], rP["tB"]], [r_P])

    powers(Ppr, Ppi, r_Pp, 17, "ar", "ai")
    powers(Pmr, Pmi, r_Pm, 8, "air", "aii")

    def cmul_b(outr, outi, ar_, ai_, br_, bi_, t1, t2, reads, r_t, w_r, w_i, neg_i=False):
        tt(t1, ar_, br_, ALU.mult, reads, [r_t])
        tt(t2, ai_, bi_, ALU.mult, reads, [r_t])
        tt(outr, t1, t2, ALU.subtract, [r_t], w_r)
        tt(t1, ar_, bi_, ALU.mult, reads, [r_t])
        tt(t2, ai_, br_, ALU.mult, reads, [r_t])
        if neg_i:
            stt(outi, t1, -1.0, t2, ALU.mult, ALU.subtract, [r_t], w_i)
        else:
            tt(outi, t1, t2, ALU.add, [r_t], w_i)

    t16c, t16d, r_t16b = t16a, t16b, r_t16
    a7r = Pmr[:, 7, :].unsqueeze(2).to_broadcast([128, 32, 16])
    a7i = Pmi[:, 7, :].unsqueeze(2).to_broadcast([128, 32, 16])
    cmul_b(Spr, Spi, SinTr, SinTi, a7r, a7i, t16c, t16d, [r_SinT, r_Pm], r_t16b, [r_Sp], [r_Sp])

    def stage_G(t):
        sl = t % 2
        for gl in range(4):
            gp = 4 * t + gl
            pmr = Pmr[:, :, gp:gp + 1].to_broadcast([128, 8, 16])
            pmi = Pmi[:, :, gp:gp + 1].to_broadcast([128, 8, 16])
            bbr = Bbr[:, gp:gp + 1, :].to_broadcast([128, 8, 16])
            bbi = Bbi[:, gp:gp + 1, :].to_broadcast([128, 8, 16])
            v3 = lambda x: x.rearrange("p (k c) -> p k c", k=8)
            cmul_b(v3(Wnr), v3(Wni), pmr, pmi, bbr, bbi, v3(gt1), v3(gt2), [r_Pm, r_Bb], r_gt, [r_Wn], [r_Wn])
            ppr = Ppr[:, 0:8, gp:gp + 1].to_broadcast([128, 8, 16])
            ppi = Ppi[:, 0:8, gp:gp + 1].to_broadcast([128, 8, 16])
            cr = cnr[:, gp:gp + 1, :].to_broadcast([128, 8, 16])
            ci = cni[:, gp:gp + 1, :].to_broadcast([128, 8, 16])
            cmul_b(v3(CSnr), v3(CSni), cr, ci, ppr, ppi, v3(gt1), v3(gt2), [r_cn, r_Pp], r_gt, [r_CSn], [r_CSn],
                   neg_i=True)
            ppr2 = Ppr[:, 8:16, gp:gp + 1].to_broadcast([128, 8, 16])
            ppi2 = Ppi[:, 8:16, gp:gp + 1].to_broadcast([128, 8, 16])
            cmul_b(v3(T2r), v3(T2i), cr, ci, ppr2, ppi2, v3(gt1), v3(gt2),
                   [r_cn, r_Pp], r_gt, [r_T2], [r_T2], neg_i=True)
            for g2 in range(2):
                actf(CS2r[sl][:, 2 * gl + g2, :], T2r, AF.Copy, [r_T2, r_const], [r_CS2[sl]], scale=hm[:, g2:g2 + 1])
                actf(CS2i[sl][:, 2 * gl + g2, :], T2i, AF.Copy, [r_T2, r_const], [r_CS2[sl]], scale=hm[:, g2:g2 + 1])
            if stop == "G1":
                return
            bk = 6 + gl % 2
            B.pe(lambda e, bk=bk: e.transpose(out=pall[:, bk, 0:128], in_=Wnr, identity=identf[:]),
                 reads=[r_Wn, r_ident], writes=[r_pb[bk]])
            B.pe(lambda e, bk=bk: e.transpose(out=pall[:, bk, 128:256], in_=Wni, identity=identf[:]),
                 reads=[r_Wn, r_ident], writes=[r_pb[bk]])
            if stop == "G2":
                return
            for g2 in range(2):
                oc = 256 + g2 * 128
                actf(CSzr, CSnr, AF.Copy, [r_CSn, r_const], [r_CSz], scale=hm[:, g2:g2 + 1])
                actf(CSzi, CSni, AF.Copy, [r_CSn, r_const], [r_CSz], scale=hm[:, g2:g2 + 1])
                B.pe(mm(pall[:, bk, oc:oc + 128], Wnr, CSzr, True, False),
                     reads=[r_Wn, r_CSz], writes=[r_pb[bk]])
                B.pe(mm(pall[:, bk, oc:oc + 128], Wni, CSzi, False, True),
                     reads=[r_Wn, r_CSz], writes=[r_pb[bk]])
            if stop == "G3":
                return
            B.alt(lambda e, bk=bk, sl=sl, gl=gl: cp(e, WTr[sl][:, 2 * gl:2 * gl + 2, :],
                                                   pall[:, bk, 0:128].rearrange("p (a b) -> p a b", a=2)),
                  reads=[r_pb[bk]], writes=[r_WT[sl]])
            B.alt(lambda e, bk=bk, sl=sl, gl=gl: cp(e, WTi[sl][:, 2 * gl:2 * gl + 2, :],
                                                   pall[:, bk, 128:256].rearrange("p (a b) -> p a b", a=2)),
                  reads=[r_pb[bk]], writes=[r_WT[sl]])
            if stop == "G4":
                return
            for g2 in range(2):
                g = 2 * gp + g2
                tt(Mg[sl][:, 2 * gl + g2, :], pall[:, bk, 256 + g2 * 128:384 + g2 * 128], mask1, ALU.mult,
                   [r_pb[bk], r_const], [r_Mg[sl]])
                stt(Mg[sl][:, 2 * gl + g2, :], identf[:], Dcol[:, g:g + 1], Mg[sl][:, 2 * gl + g2, :], ALU.mult, ALU.add,
                    [r_ident, r_const, r_Mg[sl]], [r_Mg[sl]])

    def stage_A(t):
        sl = t % 2
        wv, r_w = load_w([w_in[:, C_SU + t * 128:C_SU + (t + 1) * 128]], 16)
        s = project(wv, r_w, 0, 16, hT, r_hT, SEG_MS)
        B.act(lambda e, s=s: cp(e, suTb[:, 0:1024], pset(s)[:, 0:1024]), reads=rset(s), writes=[r_suTb])
        B.act(lambda e, s=s: cp(e, suTb[:, 1024:1088].rearrange("p (k s) -> p s k", k=4),
                                pset(s)[:, 1024:1088].rearrange("p (s k) -> p s k", k=4)), reads=rset(s), writes=[r_suTb2])
        wv, r_w = load_w([w_in[:, C_SZ + t * 128:C_SZ + (t + 1) * 128]], 16)
        s = project(wv, r_w, 0, 16, hT, r_hT, SEG_MS)
        actf(ysT[:, t, :], pset(s)[:, 0:1088], AF.Silu, rset(s), [r_ys[t]])
        sc = scr1[sl]
        r_sc = r_scr1[sl]
        r_f = fence("act", [r_suTb, r_suTb2])
        B.dma(sc[:, 0:1024], suTp[:, t, :], reads=[r_suTp[t]], writes=[r_sc[0]], stream=f"shA{sl}")
        B.dma(sc[:, 1024:2112], suTb, reads=[r_suTb, r_suTb2, r_f], writes=[r_sc[1]], stream=f"shA{sl}")
        di = 0
        for slab in range(2):
            for k in range(8):
                B.dma(Ub[sl][k * 16:(k + 1) * 16, :, slab * 128:(slab + 1) * 128],
                      DAP(sc, slab * 1024 + k * 128, [[SCR1_W, 16], [16 * SCR1_W, 8], [1, 128]]),
                      reads=r_sc, writes=[r_Ub[sl][di]], stream=f"shB{sl}")
                di += 1
        for k in range(4):
            B.dma(Ub[sl][k * 16:(k + 1) * 16, :, 256:272],
                  DAP(sc, 2048 + k * 16, [[SCR1_W, 16], [16 * SCR1_W, 8], [1, 16]]),
                  reads=r_sc, writes=[r_Ub[sl][di]], stream=f"shB{sl}")
            di += 1

    def stage_B(t):
        sl = t % 2
        for gl in range(4):
            gp = 4 * t + gl
            for (WT_, bk) in ((WTr[sl], 6), (WTi[sl], 7)):
                for g2 in range(2):
                    g8 = 2 * gl + g2
                    rows = slice(g2 * 64, (g2 + 1) * 64)
                    B.pe(mm(pall[rows, bk, 0:256], WT_[:, g8, :], Ub[sl][:, g8, 0:256], True, True),
                         reads=[r_WT[sl]] + r_Ub[sl], writes=[r_pb[bk]])
                    B.pe(mm(pall[rows, bk, 256:272], WT_[:, g8, :], Ub[sl][:, g8, 256:272], True, True),
                         reads=[r_WT[sl]] + r_Ub[sl], writes=[r_pb[bk]])
            zr_ps = pall[:, 6, 0:256]
            zi_ps = pall[:, 7, 0:256]
            p8s = P["p8s"][:, gp:gp + 1]
            ph8 = P["phi8"][:, gp:gp + 1]
            actf(rt1, jtab, AF.Identity, [r_const, rP["p8s"]], [r_rot], scale=p8s, bias=MAGIC)
            actf(rt1, rt1, AF.Identity, [r_rot], [r_rot], bias=-MAGIC)
            actf(rt2, jtab, AF.Copy, [r_const, rP["phi8"]], [r_rot], scale=ph8)
            stt(rt2, rt1, -TWO_PI, rt2, ALU.mult, ALU.add, [r_rot], [r_rot])
            ts(rt2, rt2, -PI_S, PI_S, ALU.max, ALU.min, [r_rot], [r_rot])
            actf(rsin, rt2, AF.Sin, [r_rot], [r_rot])
            actf(rt1, rt2, AF.Abs, [r_rot], [r_rot])
            actf(rcos, rt1, AF.Sin, [r_rot], [r_rot], scale=-1.0, bias=math.pi / 2)
            tt(rt1, zr_ps, rcos, ALU.mult, [r_pb[6], r_rot], [r_rot])
            tt(rt2, zi_ps, rsin, ALU.mult, [r_pb[7], r_rot], [r_rot])
            tt(rzr, rt1, rt2, ALU.add, [r_rot], [r_rot])
            tt(rt1, zi_ps, rcos, ALU.mult, [r_pb[7], r_rot], [r_rot])
            tt(rt2, zr_ps, rsin, ALU.mult, [r_pb[6], r_rot], [r_rot])
            tt(rzi, rt1, rt2, ALU.subtract, [r_rot], [r_rot])
            B.dve(lambda e, gp=gp: e.tensor_copy(out=ZSr[:, gp, :], in_=pall[:, 6, 256:272]), reads=[r_pb[6]],
                  writes=[r_ZS[t]])
            B.dve(lambda e, gp=gp: e.tensor_copy(out=ZSi[:, gp, :], in_=pall[:, 7, 256:272]), reads=[r_pb[7]],
                  writes=[r_ZS[t]])
            r8b = P["r8"][:, gp:gp + 1].to_broadcast([128, 256])
            B.dve(lambda e, r8b=r8b: e.tensor_tensor_scan(out=rt1, data0=r8b, data1=rzr, initial=0.0, op0=ALU.mult,
                                                         op1=ALU.add), reads=[r_rot, rP["r8"]], writes=[r_rot])
            B.dve(lambda e, r8b=r8b: e.tensor_tensor_scan(out=rt2, data0=r8b, data1=rzi, initial=0.0, op0=ALU.mult,
                                                         op1=ALU.add), reads=[r_rot, rP["r8"]], writes=[r_rot])
            c_ = rcos[:, 127:256]
            s_ = rsin[:, 127:256]
            tt(rzr[:, 0:129], rt1[:, 127:256], c_, ALU.mult, [r_rot], [r_rot])
            tt(rzi[:, 0:129], rt2[:, 127:256], s_, ALU.mult, [r_rot], [r_rot])
            tt(Spvr[sl][:, gl, 0:129], rzr[:, 0:129], rzi[:, 0:129], ALU.subtract, [r_rot], [r_Spv[sl]])
            B.dve(lambda e, gp=gp: e.tensor_tensor(out=SFr[:, gp:gp + 1], in0=rzr[:, 128:129], in1=rzi[:, 128:129],
                                                   op=ALU.subtract), reads=[r_rot], writes=[r_SF[t]])
            tt(rzr[:, 0:129], rt2[:, 127:256], c_, ALU.mult, [r_rot], [r_rot])
            tt(rzi[:, 0:129], rt1[:, 127:256], s_, ALU.mult, [r_rot], [r_rot])
            tt(Spvi[sl][:, gl, 0:129], rzr[:, 0:129], rzi[:, 0:129], ALU.add, [r_rot], [r_Spv[sl]])
            B.dve(lambda e, gp=gp: e.tensor_tensor(out=SFi[:, gp:gp + 1], in0=rzr[:, 128:129], in1=rzi[:, 128:129],
                                                   op=ALU.add), reads=[r_rot], writes=[r_SF[t]])

    def stage_C(t):
        sl = t % 2
        for g8 in range(8):
            gl = g8 // 2
            bk = 6 + g8 % 2
            c0 = (g8 // 2) * 128
            B.pe(mm(pall[:, bk, c0:c0 + 128], Mg[sl][:, g8, :], Ub[sl][:, g8, 128:256], True, False),
                 reads=[r_Mg[sl]] + r_Ub[sl], writes=[r_pb[bk]], drain=True)
            B.pe(mm(pall[:, bk, c0:c0 + 128], CS2r[sl][:, g8, :], Spvr[sl][:, gl, 0:128], False, False),
                 reads=[r_CS2[sl], r_Spv[sl]], writes=[r_pb[bk]])
            B.pe(mm(pall[:, bk, c0:c0 + 128], CS2i[sl][:, g8, :], Spvi[sl][:, gl, 0:128], False, True),
                 reads=[r_CS2[sl], r_Spv[sl]], writes=[r_pb[bk]])
        for b2 in range(2):
            B.act(lambda e, b2=b2: cp(e, Ysb[:, b2:8:2, 0:128], pall[:, 6 + b2, :].rearrange("p (a b) -> p a b", a=4)),
                  reads=[r_pb[6 + b2]], writes=[r_Ysb])
        for g8 in range(8):
            gl = g8 // 2
            gp = 4 * t + gl
            bk = 6 + g8 % 2
            c0 = (g8 // 2) * 16
            B.pe(mm(pall[:, bk, c0:c0 + 16], Mg[sl][:, g8, :], Ub[sl][:, g8, 256:272], True, False),
                 reads=[r_Mg[sl]] + r_Ub[sl], writes=[r_pb[bk]], drain=True)
            B.pe(mm(pall[:, bk, c0:c0 + 16], CS2r[sl][:, g8, :], Spr[:, gp, :], False, False),
                 reads=[r_CS2[sl], r_Sp], writes=[r_pb[bk]])
            B.pe(mm(pall[:, bk, c0:c0 + 16], CS2i[sl][:, g8, :], Spi[:, gp, :], False, True),
                 reads=[r_CS2[sl], r_Sp], writes=[r_pb[bk]])
        for b2 in range(2):
            B.act(lambda e, b2=b2: cp(e, Ysb[0:64, b2:8:2, 128:144],
                                      pall[0:64, 6 + b2, 0:64].rearrange("p (a b) -> p a b", a=4)),
                  reads=[r_pb[6 + b2]], writes=[r_Ysb])
        sc = scr2[sl]
        r_sc = r_scr2[sl]
        r_f = fence("act", [r_Ysb])
        for k in range(8):
            B.dma(DAP(sc, k * YW, [[8 * YW, 16], [16 * 8 * YW, 8], [1, 128]]), Ysb[k * 16:(k + 1) * 16, :, 0:128],
                  reads=[r_Ysb, r_f], writes=[r_sc[k]], stream=f"shC{sl}")
        for k in range(4):
            B.dma(DAP(sc, k * YW + 128, [[8 * YW, 16], [16 * 8 * YW, 8], [1, 16]]), Ysb[k * 16:(k + 1) * 16, :, 128:144],
                  reads=[r_Ysb, r_f], writes=[r_sc[8 + k]], stream=f"shC{sl}")
        B.dma(acc[:, 0:1024].rearrange("p (k j) -> p k j", k=8), DAP(sc, 0, [[8 * YW, 128], [YW, 8], [1, 128]]),
              reads=r_sc, writes=[r_acc], stream="shD0")
        B.dma(acc[:, 1024:1088].rearrange("p (k s) -> p k s", k=4), DAP(sc, 128, [[8 * YW, 128], [YW, 4], [1, 16]]),
              reads=r_sc, writes=[r_acc2], stream="shD1")
        acs = acc[:, 1024:1088].rearrange("p (k s) -> p s k", k=4)
        yss = ysT[:, t, 1024:1088].rearrange("p (s k) -> p s k", k=4)
        tt(acc[:, 0:1024], acc[:, 0:1024], ysT[:, t, 0:1024], ALU.mult, [r_acc, r_ys[t]], [r_acc])
        tt(acs, acs, yss, ALU.mult, [r_acc2, r_ys[t]], [r_acc2])
        actf(ysT[:, t, 0:1024], acc[:, 0:1024], AF.Gelu_apprx_tanh, [r_acc], [r_ys[t]])
        actf(yss, acs, AF.Gelu_apprx_tanh, [r_acc2, r_ys[t]], [r_ys[t]])

    def stage_conv(t):
        wv, r_w = load_w([w_in[:, C_CC + t * 128:C_CC + (t + 1) * 128]], 16)
        s = project(wv, r_w, 0, 16, hT, r_hT, SEG_MSH)
        actf(tmp_cc, pset(s)[:, 0:NX], AF.Copy, rset(s), [r_tmpcc])
        wv, r_w = load_w([w_in[:, C_CH + t * 128:C_CH + (t + 1) * 128]], 16)
        s = project(wv, r_w, 0, 16, hT, r_hT, SEG_MSH)
        ps = pset(s)
        tt(vext[:, 2:10, :], ps[:, 0:1024].rearrange("p (k j) -> p k j", k=8),
           tmp_cc[:, 0:1024].rearrange("p (k j) -> p k j", k=8), ALU.mult, rset(s) + [r_tmpcc], [r_vext])
        tt(vs[:, :, 2:6], ps[:, 1024:1088].rearrange("p (s k) -> p s k", k=4),
           tmp_cc[:, 1024:1088].rearrange("p (s k) -> p s k", k=4), ALU.mult, rset(s) + [r_tmpcc], [r_vext])
        tt(vh, ps[:, 1088:1090], tmp_cc[:, 1088:1090], ALU.mult, rset(s) + [r_tmpcc], [r_vext])
        cpv = lambda o, i, rd=[r_vext], wr=[r_vext]: B.dve(lambda e: e.tensor_copy(out=o, in_=i), reads=rd, writes=wr)
        cpv(vext[:, 1, 1:128], vext[:, 9, 0:127])
        cpv(vext[:, 0, 1:128], vext[:, 8, 0:127])
        cpv(vext[:, 1, 0:1], vh[:, 1:2])
        cpv(vext[:, 0, 0:1], vh[:, 0:1])
        cpv(vs[:, :, 0:2], sconvT[:, t, :, :], [r_sconvT, r_vext], [r_vext])
        cpv(ncp[:, t, 0:1], vext[:, 8, 127:128], [r_vext], [r_ncp])
        cpv(ncp[:, t, 1:2], vext[:, 9, 127:128], [r_vext], [r_ncp])
        cpv(ncs[:, t, :, :], vs[:, :, 4:6], [r_vext], [r_ncs])
        accm = acc[:, 0:1024].rearrange("p (k j) -> p k j", k=8)
        accs = acc[:, 1024:1088].rearrange("p (s k) -> p s k", k=4)
        ts(accm, vext[:, 2:10, :], cw[:, t, 2:3], None, ALU.mult, None, [r_vext, r_const], [r_acc])
        stt(accm, vext[:, 1:9, :], cw[:, t, 1:2], accm, ALU.mult, ALU.add, [r_vext, r_const, r_acc], [r_acc])
        stt(accm, vext[:, 0:8, :], cw[:, t, 0:1], accm, ALU.mult, ALU.add, [r_vext, r_const, r_acc], [r_acc])
        ts(accs, vs[:, :, 2:6], cw[:, t, 2:3], None, ALU.mult, None, [r_vext, r_const], [r_acc2])
        stt(accs, vs[:, :, 1:5], cw[:, t, 1:2], accs, ALU.mult, ALU.add, [r_vext, r_const, r_acc2], [r_acc2])
        stt(accs, vs[:, :, 0:4], cw[:, t, 0:1], accs, ALU.mult, ALU.add, [r_vext, r_const, r_acc2], [r_acc2])
        wv, r_w = load_w([w_in[:, C_CZ + t * 128:C_CZ + (t + 1) * 128]], 16)
        s = project(wv, r_w, 0, 16, hT, r_hT, SEG_MS)
        actf(tmp_s, pset(s)[:, 0:1088], AF.Silu, rset(s), [r_tmps])
        wv, r_w = load_w([w_in[:, C_CB + t * 128:C_CB + (t + 1) * 128]], 16)
        s = project(wv, r_w, 0, 16, hT, r_hT, SEG_MS)
        tt(tmp_s, pset(s)[:, 0:1088], tmp_s, ALU.mult, rset(s) + [r_tmps], [r_tmps])
        tt(cact[:, t, :], acc, tmp_s, ALU.mult, [r_acc, r_acc2, r_tmps], [r_cact[t]])

    if stop == "prep":
        return finish()
    for i in range(nloop + 1):
        if i < nloop:
            stage_A(i)
            stage_conv(i)
            stage_G(i)
        if i >= 1:
            stage_C(i - 1)
            if stop == "C":
                return finish()
        if i < nloop:
            stage_B(i)
            if stop == "B" and i == 1:
                return finish()
    if stop == "loop":
        return finish()

    fr_ = P["tA"]
    fi_ = P["tB"]
    r_fin = R("fin")
    t32a = P["am1"]
    t32b = P["den"]
    r_t32 = R("t32")
    cmul_b(fr_, fi_, SFr, SFi, Ppr[:, 7, :], Ppi[:, 7, :], t32a, t32b, r_SF + [r_Pp], r_t32, [r_fin], [r_fin])
    r_f = fence("dve", [r_fin])
    B.dma(DAP(spr_o, 0, [[1, 128], [128, 32]]), fr_, reads=[r_fin, r_f], stream="out", allow_slow_non_contiguous=True)
    B.dma(DAP(spi_o, 0, [[1, 128], [128, 32]]), fi_, reads=[r_fin, r_f], stream="out", allow_slow_non_contiguous=True)
    a4r = Ppr[:, 4, :].unsqueeze(2).to_broadcast([128, 32, 16])
    a4i = Ppi[:, 4, :].unsqueeze(2).to_broadcast([128, 32, 16])
    a3r = Ppr[:, 3, :].unsqueeze(2).to_broadcast([128, 32, 16])
    a3i = Ppi[:, 3, :].unsqueeze(2).to_broadcast([128, 32, 16])
    n1r, n1i = bnr, bni
    n2r = tmp_s[:, 0:512].rearrange("p (a b) -> p a b", a=32)
    n2i = tmp_s[:, 512:1024].rearrange("p (a b) -> p a b", a=32)
    r_n1 = R("n1")
    r_n2 = R("n2")
    cmul_b(n1r, n1i, SinTr, SinTi, a4r, a4i, t16c, t16d, [r_SinT, r_Pp, r_bn], r_t16b, [r_n1, r_bn], [r_n1, r_bn])
    cmul_b(n2r, n2i, ZSr, ZSi, a3r, a3i, t16c, t16d, r_ZS + [r_Pp, r_tmps], r_t16b, [r_n2, r_tmps], [r_n2, r_tmps])
    tt(n1r, n1r, n2r, ALU.add, [r_n1, r_n2], [r_n1])
    tt(n1i, n1i, n2i, ALU.add, [r_n1, r_n2], [r_n1])
    r_f = fence("dve", [r_n1, r_ncp, r_ncs])
    ost = [tmp_cc[0:16, 0:512], tmp_cc[0:16, 512:1024], vext[0:16, :, :].rearrange("p a b -> p (a b)")[:, 0:512],
           vext[0:16, :, :].rearrange("p a b -> p (a b)")[:, 512:1024]]
    r_ost = [R(f"ost{i}") for i in range(4)]
    oi = 0
    for q8 in range(8):
        for (src3, dsto, bk) in ((n1r, snr_o, 6), (n1i, sni_o, 7)):
            for gpl in range(4):
                B.pe(lambda e, src3=src3, bk=bk, gpl=gpl, q8=q8: e.transpose(
                    out=pall[0:16, bk, gpl * 128:(gpl + 1) * 128], in_=src3[:, 4 * q8 + gpl, :], identity=identf[:, :]),
                    reads=[r_n1, r_ident], writes=[r_pb[bk]])
            o_ = ost[oi % 4]
            r_o = r_ost[oi % 4]
            oi += 1
            B.act(lambda e, o_=o_, bk=bk: cp(e, o_, pall[0:16, bk, :]), reads=[r_pb[bk]], writes=[r_o, r_tmpcc, r_vext])
            r_f2 = fence("act", [r_o])
            B.dma(dsto[:, q8 * 512:(q8 + 1) * 512], o_, reads=[r_o, r_f2], stream="out")
    nst = acc[0:32, 0:1024]
    for t in range(8):
        bk = 6 + t // 4
        B.pe(lambda e, t=t, bk=bk: e.transpose(out=pall[0:32, bk, (t % 4) * 128:(t % 4 + 1) * 128],
                                               in_=ncs[:, t, :, :].rearrange("p s r -> p (s r)"), identity=identf[:, :]),
             reads=[r_ncs, r_ident], writes=[r_pb[bk]])
    for b2 in range(2):
        B.act(lambda e, b2=b2: cp(e, nst[:, b2 * 512:(b2 + 1) * 512], pall[0:32, 6 + b2, :]), reads=[r_pb[6 + b2]],
              writes=[r_acc])
    r_f3 = fence("act", [r_acc])
    B.dma(ncs_o.rearrange("s r c -> (s r) c"), nst, reads=[r_acc, r_f3], stream="out")
    for r in range(2):
        B.dma(DAP(ncp_o, r * 1024, [[1, 128], [128, 8]]), ncp[:, :, r], reads=[r_ncp, r_f], stream="out",
              allow_slow_non_contiguous=True)
    if "cact" in B.dbg:
        o = B.dout("dbg_cact", [128, 8, 1088], BF16)
        B.dma(o, cact[:, :, :], reads=r_cact, stream="dbg")
        o = B.dout("dbg_ys", [128, 8, 1088], BF16)
        B.dma(o, ysT[:, :, :], reads=r_ys, stream="dbg")
    B.S.barrier()
    if stop == "loopout":
        return finish()

    AR.top = 0
    tg = AR.alloc([1088])
    tb = AR.alloc([1088])
    r_tg = R("tg")
    r_tb = R("tb")
    r_mg = [R(f"mg{i}") for i in range(16)]

    def gate_sig(col, dst, r_dst):
        wv, r_w = load_w([w_in[:, col:col + 128]], 16)
        s = project(wv, r_w, 0, 16, hT, r_hT, SEG_MS)
        actf(dst, pset(s)[:, 0:1088], AF.Sigmoid, rset(s), [r_dst])

    def branch_out(wsrc, j, actbuf, r_actbuf):
        wv, r_w = load_w([wsrc[:, j * 128:(j + 1) * 128]], 8)
        return project(wv, r_w, 0, 8, actbuf, r_actbuf, SEG_MS)

    for j in range(16):
        gate_sig(C_G + 0 * D + j * 128, tg, r_tg)
        s = branch_out(w_conv_out, j, cact, r_cact)
        tt(merged[:, j, :], pset(s)[:, 0:1088], tg, ALU.mult, rset(s) + [r_tg], [r_mg[j]])
    for j in range(16):
        gate_sig(C_G + 1 * D + j * 128, tg, r_tg)
        s = branch_out(w_glu_b, j, ysT, r_ys)
        actf(tb, pset(s)[:, 0:1088], AF.Sigmoid, rset(s), [r_tb])
        tt(tg, tg, tb, ALU.mult, [r_tg, r_tb], [r_tg])
        s = branch_out(w_glu_a, j, ysT, r_ys)
        tt(tb, pset(s)[:, 0:1088], tg, ALU.mult, rset(s) + [r_tg], [r_tb])
        tt(merged[:, j, :], merged[:, j, :], tb, ALU.add, [r_mg[j], r_tb], [r_mg[j]])
    if "m2" in B.dbg:
        o = B.dout("dbg_m2", [128, 16, 1088], BF16)
        B.dma(o, merged, reads=r_mg, stream="dbg")
    B.S.barrier()
    if stop == "post":
        return finish()

    AR.top = 0
    KT = AR.alloc([8, 256], BF16)
    Vb = AR.alloc([2, 1024], BF16)
    a_top = AR.top
    memT = AR.alloc([16, 256], BF16)
    kst = AR.alloc([1024])
    vst = AR.alloc([1024])
    r_memT = [R(f"memT{i}") for i in range(16)]
    r_KT = R("KT")
    r_Vb = R("Vb")
    r_kst = R("kst")
    r_vst = R("vst")
    norm_transpose([(mem[0:128, :], 128, memT, r_memT, 0), (mem[128:256, :], 128, memT, r_memT, 128)], mem_norm_g, AR)
    if stop == "at0":
        return finish()
    for c4 in range(8):
        wv, r_w = load_w([w_mem_k[:, c4 * 128:(c4 + 1) * 128]], 16)
        for kt in range(16):
            B.pe(mm(pall[:, 0, 0:256], wv[:, kt, 0:128], memT[:, kt, :], kt == 0, kt == 15),
                 reads=[r_w, r_memT[kt]], writes=[r_pb[0]])
        B.alt(lambda e, c4=c4: cp(e, KT[:, c4, :], pall[:, 0, 0:256]), reads=[r_pb[0]], writes=[r_KT])
        for kt in range(16):
            B.pe(mm(pall[:, 1, 0:128], memT[:, kt, 0:128], wv[:, kt, 0:128], kt == 0, kt == 15),
                 reads=[r_w, r_memT[kt]], writes=[r_pb[1]])
        B.alt(lambda e, c4=c4: cp(e, kst[:, c4 * 128:(c4 + 1) * 128], pall[:, 1, 0:128]), reads=[r_pb[1]], writes=[r_kst])
    r_fa = fence("act", [r_kst])
    r_fd = fence("dve", [r_kst])
    B.dma(mk_o, kst, reads=[r_kst, r_fa, r_fd], stream="out")
    if stop == "at0k":
        return finish()
    for c4 in range(8):
        wv, r_w = load_w([w_mem_v[:, c4 * 128:(c4 + 1) * 128]], 16)
        for mt in range(1 if stop == "at0v1" else 2):
            bk = 2 + mt
            for kt in range(16):
                B.pe(mm(pall[:, bk, 0:128], memT[:, kt, mt * 128:(mt + 1) * 128], wv[:, kt, 0:128], kt == 0, kt == 15),
                     reads=[r_w, r_memT[kt]], writes=[r_pb[bk]])
            if mt == 0:
                B.alt(lambda e, c4=c4, bk=bk: cp(e, vst[:, c4 * 128:(c4 + 1) * 128], pall[:, bk, 0:128]),
                      reads=[r_pb[bk]], writes=[r_vst])
                B.alt(lambda e, c4=c4: cp(e, Vb[:, 0, c4 * 128:(c4 + 1) * 128], vst[:, c4 * 128:(c4 + 1) * 128]),
                      reads=[r_vst], writes=[r_Vb])
            else:
                B.alt(lambda e, c4=c4, mt=mt, bk=bk: cp(e, Vb[:, mt, c4 * 128:(c4 + 1) * 128], pall[:, bk, 0:128]),
                      reads=[r_pb[bk]], writes=[r_Vb])
    if stop == "at0v":
        return finish()
    r_fa = fence("act", [r_vst])
    r_fd = fence("dve", [r_vst])
    B.dma(mv_o, vst, reads=[r_vst, r_fa, r_fd], stream="out")
    if stop == "at0w":
        return finish()
    B.S.barrier()
    if stop == "at1":
        return finish()
    AR.top = a_top
    qT = AR.alloc([8, 1088], BF16)
    azs = AR.alloc([8, 1088], BF16)
    r_qT = [R(f"qT{i}") for i in range(8)]
    r_azs = [R(f"azs{i}") for i in range(8)]
    for f in range(8):
        wv, r_w = load_w([w_in[:, C_AQ + f * 128:C_AQ + (f + 1) * 128]], 16)
        s = project(wv, r_w, 0, 16, hT, r_hT, SEG_MS)
        B.alt(lambda e, s=s, f=f: cp(e, qT[:, f, :], pset(s)[:, 0:1088]), reads=rset(s), writes=[r_qT[f]])
        wv, r_w = load_w([w_in[:, C_AZ + f * 128:C_AZ + (f + 1) * 128]], 16)
        s = project(wv, r_w, 0, 16, hT, r_hT, SEG_MS)
        actf(azs[:, f, :], pset(s)[:, 0:1088], AF.Silu, rset(s), [r_azs[f]])
    if stop == "at2":
        return finish()
    att = cact
    r_att = [R(f"att{i}") for i in range(8)]
    B.S.barrier()
    PT = [AR.alloc([2, 512], BF16) for _ in range(2)]
    r_PT = [R("PT0"), R("PT1")]
    rsum = AR.alloc([512])
    r_rsum = R("rsum")
    otmp = AR.alloc([512])
    r_otmp = R("otmp")
    SC = 1.0 / 16.0
    it = 0
    for half in range(2):
        cs_ = slice(half * 512, (half + 1) * 512)
        for h in range(4):
            pi_ = it % 2
            it += 1
            for mt in range(2):
                for dt_ in range(2):
                    B.pe(mm(pall[:, mt, :], KT[:, 2 * h + dt_, mt * 128:(mt + 1) * 128], qT[:, 2 * h + dt_, cs_],
                            dt_ == 0, dt_ == 1), reads=[r_KT, r_qT[2 * h + dt_]], writes=[r_pb[mt]])
            for mt in range(2):
                actf(PT[pi_][:, mt, :], pall[:, mt, :], AF.Exp, [r_pb[mt]], [r_PT[pi_]], scale=SC)
            for mt in range(2):
                B.pe(mm(pall[:, 2, :], onesb[:], PT[pi_][:, mt, :], mt == 0, mt == 1), reads=[r_ident, r_PT[pi_]],
                     writes=[r_pb[2]])
            B.dve(lambda e: e.reciprocal(out=rsum, in_=pall[:, 2, :]), reads=[r_pb[2]], writes=[r_rsum])
            for dt_ in range(2):
                bk = 3 + dt_
                f = 2 * h + dt_
                for mt in range(2):
                    B.pe(mm(pall[:, bk, :], Vb[:, mt, f * 128:(f + 1) * 128], PT[pi_][:, mt, :], mt == 0, mt == 1),
                         reads=[r_Vb, r_PT[pi_]], writes=[r_pb[bk]])
                tt(otmp, pall[:, bk, :], rsum, ALU.mult, [r_pb[bk], r_rsum], [r_otmp])
                tt(att[:, f, cs_], otmp, azs[:, f, cs_], ALU.mult, [r_otmp, r_azs[f]], [r_att[f]])
    if stop == "at3":
        return finish()
    Kb = [AR.alloc([2, 1024], BF16) for _ in range(2)]
    Vs = [AR.alloc([2, 1024], BF16) for _ in range(2)]
    KTs = [AR.alloc([8, 256], BF16) for _ in range(2)]
    PTs = AR.alloc([2, 4, 4], BF16)
    rs16 = AR.alloc([4, 4])
    o32 = AR.alloc([8, 4])
    r_Kb = [R("Kb0"), R("Kb1")]
    r_Vs = [R("Vs0"), R("Vs1")]
    r_KTs = [R("KTs0"), R("KTs1")]
    r_PTs = R("PTs")
    r_rs16 = R("rs16")
    r_o32 = R("o32")
    for sq in range(16):
        sl = sq % 2
        B.dma(Kb[sl], ck[sq].rearrange("(a p) c -> p a c", p=128), writes=[r_Kb[sl]], stream=f"kvk{sl}", q="pool")
        B.dma(Vs[sl], cv[sq].rearrange("(a p) c -> p a c", p=128), writes=[r_Vs[sl]], stream=f"kvv{sl}", q="pool")
        for mt in range(2):
            bk = mt
            pv = pall[:, bk, :].bitcast(BF16)
            for f in range(8):
                B.pe(lambda e, pv=pv, sl=sl, mt=mt, f=f: e.transpose(out=pv[:, f * 128:(f + 1) * 128],
                                                                     in_=Kb[sl][:, mt, f * 128:(f + 1) * 128],
                                                                     identity=identb[:]),
                     reads=[r_Kb[sl], r_ident], writes=[r_pb[bk]])
            B.alt(lambda e, pv=pv, sl=sl, mt=mt: cp(e, KTs[sl][:, :, mt * 128:(mt + 1) * 128],
                                                   pv.rearrange("p (a b) -> p a b", a=8)),
                  reads=[r_pb[bk]], writes=[r_KTs[sl]])
        qc = slice(S0 + 4 * sq, S0 + 4 * sq + 4)
        for mt in range(2):
            for h in range(4):
                c0 = (mt * 4 + h) * 4
                for dt_ in range(2):
                    B.pe(mm(pall[:, 7, c0:c0 + 4], KTs[sl][:, 2 * h + dt_, mt * 128:(mt + 1) * 128], qT[:, 2 * h + dt_, qc],
                            dt_ == 0, dt_ == 1), reads=[r_KTs[sl], r_qT[2 * h + dt_]], writes=[r_pb[7]])
        actf(PTs, pall[:, 7, 0:32].rearrange("p (a b c) -> p a b c", a=2, b=4), AF.Exp, [r_pb[7]], [r_PTs], scale=SC)
        for h in range(4):
            for mt in range(2):
                B.pe(mm(pall[:, 7, 64 + h * 4:64 + h * 4 + 4], onesb[:], PTs[:, mt, h, :], mt == 0, mt == 1),
                     reads=[r_ident, r_PTs], writes=[r_pb[7]])
        for h in range(4):
            for dt_ in range(2):
                f = 2 * h + dt_
                for mt in range(2):
                    B.pe(mm(pall[:, 7, 128 + f * 4:128 + f * 4 + 4], Vs[sl][:, mt, f * 128:(f + 1) * 128], PTs[:, mt, h, :],
                            mt == 0, mt == 1), reads=[r_Vs[sl], r_PTs], writes=[r_pb[7]])
        B.dve(lambda e: e.reciprocal(out=rs16, in_=pall[:, 7, 64:80].rearrange("p (a b) -> p a b", a=4)),
              reads=[r_pb[7]], writes=[r_rs16])
        tt(o32.rearrange("p (h d) c -> p h d c", d=2), pall[:, 7, 128:160].rearrange("p (h d c) -> p h d c", h=4, d=2),
           rs16.unsqueeze(2).to_broadcast([128, 4, 2, 4]), ALU.mult, [r_pb[7], r_rs16], [r_o32])
        tt(att[:, :, qc], o32, azs[:, :, qc], ALU.mult, [r_o32] + r_azs, r_att)
    if stop == "at4":
        return finish()
    AR2 = AR.top
    tg2 = AR.alloc([1088])
    tb2 = AR.alloc([1088])
    for j in range(16):
        gate_sig(C_G + 2 * D + j * 128, tg2, r_tg)
        s = branch_out(w_attn_out, j, att, r_att)
        tt(tb2, pset(s)[:, 0:1088], tg2, ALU.mult, rset(s) + [r_tg], [r_tb])
        tt(merged[:, j, :], merged[:, j, :], tb2, ALU.add, [r_mg[j], r_tb], [r_mg[j]])
    if "m3" in B.dbg:
        o = B.dout("dbg_m3", [128, 16, 1088], BF16)
        B.dma(o, merged, reads=r_mg, stream="dbg")
        o = B.dout("dbg_att", [128, 8, 1088], BF16)
        B.dma(o, att[:, :, :], reads=r_att, stream="dbg")
    B.S.barrier()
    if stop == "attn":
        return finish()

    AR.top = 0
    wo = AR.alloc([16, 2048], BF16)
    r_wo = [R(f"wo{i}") for i in range(16)]
    hTf = hT[:, :, :].rearrange("p a b -> p (a b)").bitcast(F32)
    xr = [hTf[:, 0:2048], hTf[:, 2048:4096]]
    fg = hTf[:, 4096:6144]
    sqj = hTf[:, 6144:8192]
    r_xr = [R("xr0"), R("xr1")]
    r_fg = R("fg")
    r_sqj = R("sqj")
    B.dma(fg, final_g.partition_broadcast(128), writes=[r_fg], stream="fg")
    for c in range(16):
        B.dma(wo[:, :, c * 128:(c + 1) * 128], w_out[:, c * 128:(c + 1) * 128].rearrange("(a p) c -> p a c", p=128),
              writes=[r_wo[c]], stream=f"wo{c}", q="pool")
    ym_v = ym_o.rearrange("(j k) d -> k j d", k=8)
    ttiles = [(xm_v[k], ym_v[k], 128, k * 128) for k in range(8)] + [(xs, ys_o, NS, S0)]
    for ti, (xsrc, ydst, rows, col0) in enumerate(ttiles):
        sl = ti % 2
        B.dma(xr[sl][:rows, :], xsrc, writes=[r_xr[sl]], stream=f"xld{sl}")
        for cb_ in range(4):
            bk = (ti % 2) * 4 + cb_
            for kt in range(16):
                B.pe(mm(pall[:rows, bk, :], merged[:, kt, col0:col0 + rows], wo[:, kt, cb_ * 512:(cb_ + 1) * 512],
                        kt == 0, kt == 15), reads=[r_mg[kt]] + r_wo[cb_ * 4:(cb_ + 1) * 4], writes=[r_pb[bk]])
            tt(xr[sl][:rows, cb_ * 512:(cb_ + 1) * 512], xr[sl][:rows, cb_ * 512:(cb_ + 1) * 512], pall[:rows, bk, :], ALU.add,
               [r_xr[sl], r_pb[bk]], [r_xr[sl]])
        si = ti % 32
        sq_ = stat[:rows, 2 * si:2 * si + 1]
        rs_ = stat[:rows, 2 * si + 1:2 * si + 2]
        r_s = r_stat[si]
        B.act(lambda e, sl=sl, rows=rows, sq_=sq_: e.activation(out=sqj[:rows, :], in_=xr[sl][:rows, :], func=AF.Square,
                                                               accum_out=sq_), reads=[r_xr[sl]], writes=[r_sqj, r_s])
        B.act(lambda e, sq_=sq_, rs_=rs_: e.activation(out=rs_, in_=sq_, func=AF.Sqrt, scale=1.0 / D, bias=EPS),
              reads=[r_s], writes=[r_s])
        B.dve(lambda e, rs_=rs_: e.reciprocal(out=rs_, in_=rs_), reads=[r_s], writes=[r_s])
        stt(xr[sl][:rows, :], xr[sl][:rows, :], rs_, fg[:rows, :], ALU.mult, ALU.mult, [r_xr[sl], r_s, r_fg], [r_xr[sl]])
        r_f = fence("dve", [r_xr[sl]])
        B.dma(ydst, xr[sl][:rows, :], reads=[r_xr[sl], r_f], stream="yout")

    return finish()


IN_NAMES = ["xm", "xp", "xs", "mem", "ck", "cv", "sconv", "ssr", "ssi"]


def core_inputs(inp, c):
    b, hb = c // 2, c % 2
    f = np.float32
    A = np.ascontiguousarray
    memr = inp["mem_prompt"][b]
    return {
        "xm": A(inp["x_prompt"][b, hb * NM:(hb + 1) * NM]),
        "xp": A(inp["x_prompt"][b, 0:NM]) if hb == 1 else np.zeros((NM, D), f),
        "xs": A(inp["x_sample"][c * 16:(c + 1) * 16].reshape(NS, D)),
        "mem": A(np.concatenate([memr[hb * 128:(hb + 1) * 128], memr[(1 - hb) * 128:(2 - hb) * 128]], 0)),
        "ck": A(inp["cache_mem_k"][0, c * 16:(c + 1) * 16].reshape(16, 256, 1024)),
        "cv": A(inp["cache_mem_v"][0, c * 16:(c + 1) * 16].reshape(16, 256, 1024)),
        "sconv": A(inp["state_conv"][0, c * 16:(c + 1) * 16]),
        "ssr": A(inp["state_ssm_re"][0, c * 16:(c + 1) * 16].reshape(16, 4096)),
        "ssi": A(inp["state_ssm_im"][0, c * 16:(c + 1) * 16].reshape(16, 4096)),
    }


def shared_inputs(inp):
    A = np.ascontiguousarray
    return {
        "norm_g": A(inp["norm_g"][0]), "mem_norm_g": A(inp["mem_norm_g"][0]), "final_g": A(inp["final_norm_g"]),
        "w_in": A(inp["w_in"][0]), "conv_w": A(inp["conv_w"][0]), "w_conv_out": A(inp["w_conv_out"][0]),
        "lam_re": A(inp["ssm_lambda_re"][0]), "lam_im": A(inp["ssm_lambda_im"][0]), "log_dt": A(inp["ssm_log_dt"][0]),
        "b_re": A(inp["ssm_b_re"][0]), "b_im": A(inp["ssm_b_im"][0]), "c_re": A(inp["ssm_c_re"][0]),
        "c_im": A(inp["ssm_c_im"][0]), "ssm_d": A(inp["ssm_d"][0]), "w_glu_a": A(inp["w_glu_a"][0]),
        "w_glu_b": A(inp["w_glu_b"][0]), "w_mem_k": A(inp["w_mem_k"][0]), "w_mem_v": A(inp["w_mem_v"][0]),
        "w_attn_out": A(inp["w_attn_out"][0]), "w_out": A(inp["w_out"][0]),
    }


_CACHE = {}


def run(inp, dbg=(), ncores=NCORE, stop=None, nloop=8):
    key = (tuple(sorted(dbg)), stop, nloop)
    if key not in _CACHE:
        _CACHE[key] = build(dbg, stop, nloop)
    B = _CACHE[key]
    sh = shared_inputs(inp)
    in_maps = []
    for c in range(ncores):
        m = dict(sh)
        m.update(core_inputs(inp, c))
        in_maps.append({k: v for k, v in m.items() if k in B.ins})
    res = run_bass_kernel_spmd(B.nc, in_maps, core_ids=list(range(ncores)))
    return res.results


def assemble(res):
    f = np.float32
    y_prompt = np.zeros((4, 2048, D), f)
    y_sample = np.zeros((128, 4, D), f)
    mk = np.zeros((1, 4, 256, 4, 256), f)
    mv = np.zeros((1, 4, 256, 4, 256), f)
    ncp = np.zeros((1, 4, 2, 1024), f)
    spr = np.zeros((1, 4, 64, 64), f)
    spi = np.zeros((1, 4, 64, 64), f)
    ncs = np.zeros((1, 128, 2, 1024), f)
    snr = np.zeros((1, 128, 64, 64), f)
    sni = np.zeros((1, 128, 64, 64), f)
    for c in range(NCORE):
        b, hb = c // 2, c % 2
        r = res[c]
        y_prompt[b, hb * NM:(hb + 1) * NM] = r["ym"]
        y_sample[c * 16:(c + 1) * 16] = r["ysm"].reshape(16, 4, D)
        mk[0, b, hb * 128:(hb + 1) * 128] = r["mk"].reshape(128, 4, 256)
        mv[0, b, hb * 128:(hb + 1) * 128] = r["mv"].reshape(128, 4, 256)
        if hb == 1:
            ncp[0, b] = r["ncp"]
            spr[0, b] = r["spr"].reshape(64, 64)
            spi[0, b] = r["spi"].reshape(64, 64)
        ncs[0, c * 16:(c + 1) * 16] = r["ncs"]
        snr[0, c * 16:(c + 1) * 16] = r["snr"].reshape(16, 64, 64)
        sni[0, c * 16:(c + 1) * 16] = r["sni"].reshape(16, 64, 64)
    return (y_prompt, y_sample, mk, mv, ncp, spr, spi, ncs, snr, sni)


def kernel(**inputs):
    inp = {k: np.asarray(v) for k, v in inputs.items()}
    res = run(inp)
    return assemble(res)
```

```python
import math
import numpy as np
import concourse.bass as bass
import concourse.mybir as mybir
from concourse.bass_utils import run_bass_kernel_spmd
from contextlib import ExitStack

F32 = mybir.dt.float32
BF16 = mybir.dt.bfloat16
I32 = mybir.dt.int32
AF = mybir.ActivationFunctionType
ALU = mybir.AluOpType

D = 2048
NIN = 14336
NCORE = 8
EPS = 1e-6
NM = 1024
NS = 64
NX = NM + NS + 2
S0 = NM
H0 = NM + NS
C_CB, C_CC, C_CH, C_CZ, C_SU, C_SZ, C_AQ, C_AZ, C_G = 0, 1024, 2048, 3072, 4096, 5120, 6144, 7168, 8192


class Res:
    __slots__ = ("w", "r", "name")

    def __init__(self, name=""):
        self.w = None
        self.r = []
        self.name = name


class Op:
    __slots__ = ("q", "fn", "deps", "flag", "val", "dma", "idx", "alldeps", "cost", "lat", "seq", "seg", "waits")


DEF_COST = {"pe": 0.15, "act": 0.8, "dve": 0.8, "pool": 1.0, "sp": 0.1}
RESCHED = True
KEEP_ORDER = ("pe",)
PE_FREE_SEGS = ()


class Sched:
    QS = ["pe", "act", "dve", "pool", "sp"]

    def __init__(self):
        self.all_ops = []
        self.streams = {}

    def op(self, q, fn, reads=(), writes=(), dma=None, after=(), cost=None, lat=0.0):
        o = Op()
        o.q = q
        o.fn = fn
        o.flag = False
        o.dma = dma
        o.val = 0
        o.idx = 0
        o.cost = cost if cost is not None else getattr(fn, "cost", None)
        if o.cost is None:
            o.cost = DEF_COST[q]
        o.lat = lat
        deps = []
        seen = set()

        def add(d):
            if d is None or id(d) in seen:
                return
            seen.add(id(d))
            deps.append(d)

        for r in reads:
            add(r.w)
        for w in writes:
            add(w.w)
            for x in w.r:
                add(x)
        for d in after:
            add(d)
        o.alldeps = deps
        for r in reads:
            r.r.append(o)
        for w in writes:
            w.w = o
            w.r = []
        if dma is not None:
            self.streams[dma] = True
        o.seq = len(self.all_ops)
        self.all_ops.append(o)
        return o

    def barrier(self):
        self.all_ops.append(None)

    def _schedule_segment(self, seg):
        import heapq
        inseg = {id(o) for o in seg}
        preds = {id(o): [d for d in o.alldeps if id(d) in inseg] for o in seg}
        last_stream = {}
        for o in seg:
            if o.dma is not None:
                p = last_stream.get(o.dma)
                if p is not None and all(id(p) != id(x) for x in preds[id(o)]):
                    preds[id(o)].append(p)
                last_stream[o.dma] = o
        last_q = {}
        for o in seg:
            if o.q in KEEP_ORDER and not (o.q == "pe" and self.cur_seg in PE_FREE_SEGS):
                p = last_q.get(o.q)
                if p is not None and all(id(p) != id(x) for x in preds[id(o)]):
                    preds[id(o)].append(p)
                last_q[o.q] = o
        if not RESCHED:
            out = {q: [] for q in self.QS}
            for o in seg:
                out[o.q].append(o)
            return out
        succ = {id(o): [] for o in seg}
        indeg = {}
        for o in seg:
            indeg[id(o)] = len(preds[id(o)])
            for d in preds[id(o)]:
                succ[id(d)].append(o)
        done = {}
        ready = {q: [] for q in self.QS}
        for o in seg:
            if indeg[id(o)] == 0:
                heapq.heappush(ready[o.q], (0.0, o.seq, o))
        qfree = {q: 0.0 for q in self.QS}
        out = {q: [] for q in self.QS}
        n = 0
        while n < len(seg):
            best = None
            for q in self.QS:
                hp = ready[q]
                if not hp:
                    continue
                cands = []
                while hp and hp[0][0] <= qfree[q]:
                    cands.append(heapq.heappop(hp))
                if cands:
                    c = min(cands, key=lambda x: x[1])
                    for x in cands:
                        if x is not c:
                            heapq.heappush(hp, x)
                    heapq.heappush(hp, c)
                    st = qfree[q]
                    key = (st, c[1])
                    pick = c
                else:
                    pick = hp[0]
                    st = pick[0]
                    key = (st, pick[1])
                if best is None or key < best[0]:
                    best = (key, q, pick, st)
            _, q, pick, st = best
            hp = ready[q]
            hp.remove(pick)
            heapq.heapify(hp)
            o = pick[2]
            out[q].append(o)
            qfree[q] = st + o.cost
            done[id(o)] = st + o.cost + o.lat
            n += 1
            for s_ in succ[id(o)]:
                indeg[id(s_)] -= 1
                if indeg[id(s_)] == 0:
                    rt = max(done[id(d)] for d in preds[id(s_)])
                    heapq.heappush(ready[s_.q], (rt, s_.seq, s_))
        self.est_time = getattr(self, "est_time", 0.0) + max(qfree.values())
        return out

    def finalize(self):
        self.queues = {q: [] for q in self.QS}
        segs = [[]]
        for o in self.all_ops:
            if o is None:
                segs.append([])
            else:
                segs[-1].append(o)
        for si, seg in enumerate(segs):
            for o in seg:
                o.seg = si
            self.cur_seg = si
            out = self._schedule_segment(seg)
            for q in self.QS:
                self.queues[q].extend(out[q])
            if si < len(segs) - 1:
                lasts = []
                for q in self.QS:
                    for o in reversed(self.queues[q]):
                        if o.dma is None and o.fn is not None:
                            lasts.append(o)
                            break
                lastdma = {}
                for q in self.QS:
                    for o in self.queues[q]:
                        if o.dma is not None:
                            lastdma[o.dma] = o
                lasts += list(lastdma.values())
                for q in self.QS:
                    b = Op()
                    b.q = q
                    b.fn = None
                    b.flag = False
                    b.dma = None
                    b.val = 0
                    b.idx = 0
                    b.seg = -1
                    b.alldeps = lasts
                    self.queues[q].append(b)
        cnt = {}
        for q in self.QS:
            for o in self.queues[q]:
                if o.dma is not None:
                    cnt[o.dma] = cnt.get(o.dma, 0) + 1
                    o.idx = cnt[o.dma]
        for q in self.QS:
            for o in self.queues[q]:
                o.waits = []
                for d in o.alldeps:
                    if o.seg != -1 and d.seg != o.seg:
                        continue
                    if d.dma is None and d.q == "pe" and q == "pe":
                        continue
                    o.waits.append(d)
                    if d.dma is None:
                        d.flag = True
        for q in self.QS:
            c = 0
            for o in self.queues[q]:
                if o.dma is None and o.flag:
                    c += 1
                    o.val = c

    def emit(self, nc, stack):
        self.finalize()
        qsem = {q: stack.enter_context(nc.semaphore("q_" + q)) for q in self.QS}
        ssem = {s: stack.enter_context(nc.semaphore("s_" + s)) for s in self.streams}
        block = stack.enter_context(nc.Block())

        def tok(d):
            if d.dma is not None:
                return ssem[d.dma], 16 * d.idx
            return qsem[d.q], d.val

        def run(q, eng):
            waited = {}
            for o in self.queues[q]:
                for d in o.waits:
                    sem, val = tok(d)
                    if waited.get(id(sem), 0) >= val:
                        continue
                    eng.wait_ge(sem, val)
                    waited[id(sem)] = val
                if o.fn is None:
                    continue
                ins = o.fn(eng)
                if o.dma is not None:
                    ins.then_inc(ssem[o.dma], 16)
                elif o.flag:
                    ins.then_inc(qsem[q], 1)
            fin = {}
            for o in self.queues[q]:
                if o.dma is not None:
                    fin[o.dma] = max(fin.get(o.dma, 0), o.idx)
            for s, n in fin.items():
                eng.wait_ge(ssem[s], 16 * n)

        @block.tensor
        def _(e):
            run("pe", e)

        @block.scalar
        def _(e):
            run("act", e)

        @block.vector
        def _(e):
            run("dve", e)

        @block.gpsimd
        def _(e):
            run("pool", e)

        @block.sync
        def _(e):
            run("sp", e)


class Builder:
    def __init__(self, dbg=()):
        self.dbg = set(dbg)
        self.nc = bass.Bass("TRN2", target_bir_lowering=False)
        self.S = Sched()
        self.stack = ExitStack()
        self.ins = {}
        self.outs = {}
        self.rr = 0

    def din(self, name, shape, dt=F32):
        t = self.nc.dram_tensor(name, list(shape), dt, kind="ExternalInput").ap()
        self.ins[name] = t
        return t

    def dout(self, name, shape, dt=F32):
        t = self.nc.dram_tensor(name, list(shape), dt, kind="ExternalOutput").ap()
        self.outs[name] = t
        return t

    def sb(self, name, shape, dt=F32):
        return self.stack.enter_context(self.nc.sbuf_tensor(name, list(shape), dt))

    def ps(self, name, shape, dt=F32):
        return self.stack.enter_context(self.nc.psum_tensor(name, list(shape), dt))

    def pe(self, fn, reads=(), writes=(), drain=False):
        after = [self.last_pe] if (drain and getattr(self, "last_pe", None) is not None) else []
        o = self.S.op("pe", fn, reads, writes, after=after)
        self.last_pe = o
        return o

    def act(self, fn, reads=(), writes=()):
        return self.S.op("act", fn, reads, writes)

    def dve(self, fn, reads=(), writes=()):
        return self.S.op("dve", fn, reads, writes)

    def pool(self, fn, reads=(), writes=()):
        return self.S.op("pool", fn, reads, writes)

    def alt(self, fn, reads=(), writes=()):
        self.rr ^= 1
        return self.S.op("act" if self.rr else "dve", fn, reads, writes)

    def dma(self, out, in_, reads=(), writes=(), stream="ld", q="sp", **kw):
        n = 1
        for x in out.shape:
            n *= x
        lat = 2.5 + n * 4 / 200e3
        cost = 1.0 if q == "pool" else 0.15
        if kw.get("allow_slow_non_contiguous"):
            lat += n * 0.004
        return self.S.op(q, lambda e: e.dma_start(out=out, in_=in_, **kw), reads, writes, dma=stream, cost=cost, lat=lat)

    def dump(self, name, ap_sb, res, shape, dt=F32):
        if name not in self.dbg:
            return
        o = self.dout("dbg_" + name, shape, dt)
        self.dma(o, ap_sb, reads=[res], stream="dbg")


class Arena:
    def __init__(self, B, words):
        self.t = B.sb("arena", [128, words], F32)
        self.words = words
        self.top = 0

    def alloc(self, shape, dt=F32):
        n = 1
        for x in shape:
            n *= x
        w = n if dt == F32 else (n + 1) // 2
        w = (w + 7) // 8 * 8
        off = self.top
        self.top += w
        assert self.top <= self.words, f"arena overflow {self.top} > {self.words}"
        v = self.t[:, off:off + w]
        if dt != F32:
            v = v.bitcast(dt)
        v = v[:, 0:n]
        if len(shape) == 2:
            v = v.rearrange("p (a b) -> p a b", a=shape[0])
        elif len(shape) == 3:
            v = v.rearrange("p (a b c) -> p a b c", a=shape[0], b=shape[1])
        elif len(shape) == 4:
            v = v.rearrange("p (a b c d) -> p a b c d", a=shape[0], b=shape[1], c=shape[2])
        return v


def cp(e, out, in_):
    if hasattr(e, "tensor_copy"):
        return e.tensor_copy(out=out, in_=in_)
    return e.activation(out=out, in_=in_, func=AF.Copy)


def mm(out, lhsT, rhs, start, stop):
    f = lambda e: e.matmul(out, lhsT, rhs, start=start, stop=stop)
    n = rhs.shape[-1]
    f.cost = 0.03 + max(n, 64) * (4 if rhs.dtype == F32 else 1) / 2000.0
    return f


MAGIC = 12582912.0
TWO_PI = 2.0 * math.pi
PI_S = 3.1415925
BIGW = 8704
ARENA_W = 23300
NJ = 272
SCR1_W = 2112
YW = 144


def build(dbg=(), stop=None, nloop=8):
    B = Builder(dbg)

    def finish():
        B.S.emit(B.nc, B.stack)
        B.stack.close()
        return B

    nc = B.nc
    R = Res
    xm = B.din("xm", [NM, D])
    xp = B.din("xp", [NM, D])
    xs = B.din("xs", [NS, D])
    mem = B.din("mem", [256, D])
    ck = B.din("ck", [16, 256, 1024])
    cv = B.din("cv", [16, 256, 1024])
    sconv = B.din("sconv", [16, 2, 1024])
    ssr = B.din("ssr", [16, 4096])
    ssi = B.din("ssi", [16, 4096])
    norm_g = B.din("norm_g", [D])
    mem_norm_g = B.din("mem_norm_g", [D])
    final_g = B.din("final_g", [D])
    w_in = B.din("w_in", [D, NIN])
    conv_w = B.din("conv_w", [3, 1024])
    w_conv_out = B.din("w_conv_out", [1024, D])
    lam_re = B.din("lam_re", [64, 64])
    lam_im = B.din("lam_im", [64, 64])
    log_dt = B.din("log_dt", [64])
    b_re = B.din("b_re", [64, 64, 16])
    b_im = B.din("b_im", [64, 64, 16])
    c_re = B.din("c_re", [64, 16, 64])
    c_im = B.din("c_im", [64, 16, 64])
    ssm_d = B.din("ssm_d", [1024])
    w_glu_a = B.din("w_glu_a", [1024, D])
    w_glu_b = B.din("w_glu_b", [1024, D])
    w_mem_k = B.din("w_mem_k", [D, 1024])
    w_mem_v = B.din("w_mem_v", [D, 1024])
    w_attn_out = B.din("w_attn_out", [1024, D])
    w_out = B.din("w_out", [D, D])

    ym_o = B.dout("ym", [NM, D])
    ys_o = B.dout("ysm", [NS, D])
    mk_o = B.dout("mk", [128, 1024])
    mv_o = B.dout("mv", [128, 1024])
    ncp_o = B.dout("ncp", [2, 1024])
    spr_o = B.dout("spr", [4096])
    spi_o = B.dout("spi", [4096])
    ncs_o = B.dout("ncs", [16, 2, 1024])
    snr_o = B.dout("snr", [16, 4096])
    sni_o = B.dout("sni", [16, 4096])

    scr1 = [nc.dram_tensor(f"scr1_{i}", [128, SCR1_W], BF16).ap() for i in range(2)]
    scr2 = [nc.dram_tensor(f"scr2_{i}", [128, 8 * YW], F32).ap() for i in range(2)]

    def DAP(t, off, pat):
        return bass.AP(t.tensor, off, pat)

    hT = B.sb("hT", [128, 16, NX], BF16)
    BIG = B.sb("BIG", [128, BIGW], F32)
    wbs = [B.sb(f"wb{i}", [128, 2048], BF16) for i in range(3)]
    NWB = 3
    identb = B.sb("identb", [128, 128], BF16)
    identf = B.sb("identf", [128, 128], F32)
    onesb = B.sb("onesb", [128, 128], BF16)
    cact = B.sb("cact", [128, 8, 1088], BF16)
    ysT = B.sb("ysT", [128, 8, 1088], BF16)
    stat = B.sb("stat", [128, 64], F32)
    pall = B.ps("pall", [128, 8, 512], F32)
    AR = Arena(B, ARENA_W)

    hTp = BIG[:, 0:8192].bitcast(BF16).rearrange("p (a b) -> p a b", a=16)
    merged = BIG[:, 0:BIGW].bitcast(BF16).rearrange("p (a b) -> p a b", a=16)

    r_pb = [R(f"pb{i}") for i in range(8)]
    r_hT = [R(f"hT{i}") for i in range(16)]
    r_ident = R("ident")
    r_wb = [R(f"wb{i}") for i in range(3)]
    r_stat = [R(f"st{i}") for i in range(32)]
    r_cact = [R(f"cact{i}") for i in range(8)]
    r_ys = [R(f"ys{i}") for i in range(8)]

    fctr = [0]

    def fence(q, reads):
        r = R("fence")
        i = fctr[0] % 2
        fctr[0] += 1
        if q == "act":
            B.act(lambda e: e.activation(out=stat[0:1, 56 + i:57 + i], in_=stat[0:1, 58:59], func=AF.Copy), reads=reads,
                  writes=[r])
        else:
            B.dve(lambda e: e.tensor_copy(out=stat[0:1, 60 + i:61 + i], in_=stat[0:1, 59:60]), reads=reads, writes=[r])
        return r

    def bank(b, n=512):
        return pall[:, b, 0:n]

    def pset(s):
        return pall[:, 3 * s:3 * s + 3, :].rearrange("p b c -> p (b c)")

    B.pool(lambda e: e.memset(identf[:], 0.0), writes=[r_ident])
    B.pool(lambda e: e.affine_select(out=identf[:], in_=identf[:], pattern=[[-1, 128]],
                                     compare_op=ALU.not_equal, fill=1.0, base=0,
                                     channel_multiplier=1), reads=[r_ident], writes=[r_ident])
    B.dve(lambda e: e.tensor_copy(out=identb[:], in_=identf[:]), reads=[r_ident], writes=[r_ident])
    B.dve(lambda e: e.memset(onesb[:], 1.0), writes=[r_ident])
    B.dve(lambda e: e.memset(stat[:], 0.0), writes=r_stat)

    wcount = [0]

    def load_w(parts, kt):
        assert len(parts) == 1
        p = parts[0]
        sl = wcount[0] % NWB
        wcount[0] += 1
        assert p.shape[1] == 128 and kt * 128 <= 2048
        v = wbs[sl][:, 0:kt * 128].rearrange("p (a c) -> p a c", a=kt)
        B.dma(v, p.rearrange("(a p) c -> p a c", p=128), writes=[r_wb[sl]], stream=f"w{sl}", q="pool")
        return v, r_wb[sl]

    setctr = [0]

    def project(wv, r_w, col, nkt, act, r_act, segs, s=None):
        if s is None:
            s = setctr[0] % 2
            setctr[0] += 1
        for kt in range(nkt):
            for (c0, n, bi) in segs:
                bk = 3 * s + bi
                B.pe(mm(pall[:, bk, 0:n], wv[:, kt, col:col + 128], act[:, kt, c0:c0 + n], kt == 0, kt == nkt - 1),
                     reads=[r_w, r_act[kt]], writes=[r_pb[bk]])
        return s

    SEG_MS = [(0, 512, 0), (512, 512, 1), (1024, NS, 2)]
    SEG_MSH = [(0, 512, 0), (512, 512, 1), (1024, NS + 2, 2)]

    def rset(s):
        return [r_pb[3 * s], r_pb[3 * s + 1], r_pb[3 * s + 2]]

    def norm_transpose(tiles, gsrc, AR_):
        xst = [AR_.alloc([D]) for _ in range(2)]
        xbf = [AR_.alloc([D], BF16) for _ in range(2)]
        junk = AR_.alloc([D], BF16)
        gbc = AR_.alloc([D])
        r_xst = [R(), R()]
        r_xbf = [R(), R()]
        r_junk = R()
        r_gbc = R()
        B.dma(gbc, gsrc.partition_broadcast(128), writes=[r_gbc], stream="gbc")
        for ti, (src, rows, dstT, r_dst, col0) in enumerate(tiles):
            sl = ti % 2
            xt, xb = xst[sl], xbf[sl]
            B.dma(xt[:rows, :], src, writes=[r_xst[sl]], stream=f"xld{sl}")
            si = ti % 32
            sq = stat[:rows, 2 * si:2 * si + 1]
            rs = stat[:rows, 2 * si + 1:2 * si + 2]
            r_s = r_stat[si]
            B.act(lambda e, xt=xt, rows=rows, sq=sq: e.activation(out=junk[:rows, :], in_=xt[:rows, :], func=AF.Square,
                                                                 accum_out=sq),
                  reads=[r_xst[sl]], writes=[r_junk, r_s])
            B.act(lambda e, sq=sq, rs=rs: e.activation(out=rs, in_=sq, func=AF.Sqrt, scale=1.0 / D, bias=EPS),
                  reads=[r_s], writes=[r_s])
            B.dve(lambda e, rs=rs: e.reciprocal(out=rs, in_=rs), reads=[r_s], writes=[r_s])
            B.dve(lambda e, xt=xt, xb=xb, rows=rows, rs=rs: e.scalar_tensor_tensor(
                out=xb[:rows, :], in0=xt[:rows, :], scalar=rs, in1=gbc[:rows, :], op0=ALU.mult, op1=ALU.mult),
                reads=[r_xst[sl], r_s, r_gbc], writes=[r_xbf[sl]])
            for g4 in range(4):
                pbk = 6 + (ti * 4 + g4) % 2
                pv = pall[:, pbk, :].bitcast(BF16)
                for i in range(4):
                    dt_ = g4 * 4 + i
                    B.pe(lambda e, pv=pv, xb=xb, rows=rows, dt_=dt_, i=i: e.transpose(
                        out=pv[:, i * 128:i * 128 + rows], in_=xb[:rows, dt_ * 128:(dt_ + 1) * 128],
                        identity=identb[:rows, :rows]),
                        reads=[r_xbf[sl], r_ident], writes=[r_pb[pbk]])
                B.alt(lambda e, pv=pv, dstT=dstT, g4=g4, rows=rows, col0=col0: cp(
                    e, dstT[:, g4 * 4:(g4 + 1) * 4, col0:col0 + rows],
                    pv[:, 0:512].rearrange("p (a b) -> p a b", a=4)[:, :, 0:rows]),
                    reads=[r_pb[pbk]], writes=r_dst[g4 * 4:(g4 + 1) * 4])

    suTp = AR.alloc([8, 1024], BF16)
    r_suTp = [R(f"suTp{i}") for i in range(8)]
    names = ["lamr", "lami", "dtl", "dt", "lrdt", "mag", "imag", "ang", "angw", "sn", "cs", "ar", "ai", "air", "aii",
             "den", "rden", "am1", "fr", "fi", "tA", "tB", "phi8", "p8s", "r8", "tC"]
    P = {n: AR.alloc([32]) for n in names}
    rP = {n: R(n) for n in names}
    Ppr = AR.alloc([17, 32])
    Ppi = AR.alloc([17, 32])
    Pmr = AR.alloc([8, 32])
    Pmi = AR.alloc([8, 32])
    r_Pp = R("Pp")
    r_Pm = R("Pm")
    Bbr = AR.alloc([32, 16])
    Bbi = AR.alloc([32, 16])
    cnr = AR.alloc([32, 16])
    cni = AR.alloc([32, 16])
    r_Bb = R("Bb")
    r_cn = R("cn")
    Dcol = AR.alloc([64])
    cw = AR.alloc([8, 3])
    jtab = AR.alloc([256])
    mask1 = AR.alloc([128])
    r_const = R("const")
    sconvT = AR.alloc([8, 16, 2])
    ncp = AR.alloc([8, 2])
    ncs = AR.alloc([8, 16, 2])
    r_sconvT = R("sconvT")
    r_ncp = R("ncp")
    r_ncs = R("ncs")
    SinTr = AR.alloc([32, 16])
    SinTi = AR.alloc([32, 16])
    Spr = AR.alloc([32, 16], BF16)
    Spi = AR.alloc([32, 16], BF16)
    ZSr = AR.alloc([32, 16])
    ZSi = AR.alloc([32, 16])
    SFr = AR.alloc([32])
    SFi = AR.alloc([32])
    r_SinT = R("SinT")
    r_Sp = R("Sp")
    r_ZS = [R(f"ZS{i}") for i in range(8)]
    r_SF = [R(f"SF{i}") for i in range(8)]
    ld_state = {"stream": "par0", "ops": [], "res": []}

    def ld(dst, src, res, q="act", **kw):
        o = B.dma(dst, src, writes=[res], stream=ld_state["stream"], q=q, **kw)
        ld_state["ops"].append(o)
        ld_state["res"].append(res)

    def ld_group_done():
        last = ld_state["ops"][-1]
        for r_ in ld_state["res"]:
            r_.w = last
        ld_state["ops"], ld_state["res"] = [], []

    nce = dict(allow_slow_non_contiguous=True)
    stf = ysT[:, :, :].rearrange("p a b -> p (a b)").bitcast(F32)
    stc = cact[:, :, :].rearrange("p a b -> p (a b)").bitcast(F32)
    r_stf = R("stf")
    r_stc = R("stc")
    ld_state["stream"] = "par0"
    q0 = "sp"
    ld(stf[0:32, 0:128], lam_re.rearrange("(a b) p -> a (b p)", b=2), r_stf, q=q0)
    ld(stf[0:32, 128:256], lam_im.rearrange("(a b) p -> a (b p)", b=2), r_stf, q=q0)
    ld(stf[0:32, 384:386], log_dt.rearrange("(a b) -> a b", b=2), r_stf, q=q0)
    ld(stf[0:64, 512:528], ssm_d.rearrange("(g c) -> g c", c=16), r_stf, q=q0)
    for i in range(3):
        ld(cw[:, :, i], DAP(conv_w, i * 1024, [[1, 128], [128, 8]]), r_const, q=q0, **nce)
    ld(stf[0:32, 2048:3072], sconv.rearrange("s r c -> (s r) c"), r_stf, q=q0)
    for (src_, base) in ((c_re, 0), (c_im, 1024)):
        st3 = stc[0:64, base:base + 1024].rearrange("p (t x) -> p t x", t=8)
        for t in range(8):
            for gpl in range(4):
                ld(st3[gpl * 16:(gpl + 1) * 16, t, :].rearrange("p (a b) -> p a b", a=2),
                   DAP(src_, (8 * t + 2 * gpl) * 1024, [[64, 16], [1024, 2], [1, 64]]), r_stc, q=q0)
    ld_group_done()
    B.dve(lambda e: e.tensor_copy(out=stf[0:32, 256:384].rearrange("p (a b) -> p a b", a=2),
                                  in_=stf[0:32, 384:386].unsqueeze(2).to_broadcast([32, 2, 64])), reads=[r_stf], writes=[r_stf])
    B.dve(lambda e: e.tensor_copy(out=stf[0:64, 640:768].rearrange("p (a b) -> p a b", a=8),
                                  in_=stf[0:64, 512:528].unsqueeze(1).to_broadcast([64, 8, 16])), reads=[r_stf], writes=[r_stf])

    def ptr(out_ps, in_sb, rows, res_in, bk):
        B.pe(lambda e: e.transpose(out=out_ps, in_=in_sb, identity=identf[0:rows, 0:rows]), reads=[res_in, r_ident],
             writes=[r_pb[bk]])

    for i, nm in enumerate(("lamr", "lami", "dtl")):
        ptr(pall[:, 6, i * 32:(i + 1) * 32], stf[0:32, i * 128:(i + 1) * 128], 32, r_stf, 6)
    ptr(pall[:, 6, 128:192], stf[0:64, 640:768], 64, r_stf, 6)
    for i, nm in enumerate(("lamr", "lami", "dtl")):
        B.alt(lambda e, i=i, nm=nm: cp(e, P[nm], pall[:, 6, i * 32:(i + 1) * 32]), reads=[r_pb[6]], writes=[rP[nm]])
    B.alt(lambda e: cp(e, Dcol, pall[:, 6, 128:192]), reads=[r_pb[6]], writes=[r_const])
    if stop == "e1":
        return finish()
    for t in range(8):
        ptr(pall[:, 7, 64 + t * 32:64 + (t + 1) * 32], stf[0:32, 2048 + t * 128:2048 + (t + 1) * 128], 32, r_stf, 7)
    B.alt(lambda e: cp(e, sconvT.rearrange("p t s r -> p t (s r)"), pall[:, 7, 64:320].rearrange("p (a b) -> p a b", a=8)),
          reads=[r_pb[7]], writes=[r_sconvT])
    if stop == "e2":
        return finish()
    for (dstC, base, bk) in ((cnr, 0, 6), (cni, 1024, 7)):
        for t in range(8):
            ptr(pall[:, bk, t * 64:(t + 1) * 64], stc[0:64, base + t * 128:base + (t + 1) * 128], 64, r_stc, bk)
        B.alt(lambda e, dstC=dstC, bk=bk: cp(e, dstC.rearrange("p a b -> p (a b)"), pall[:, bk, :]), reads=[r_pb[bk]],
              writes=[r_cn])
    if stop == "e3":
        return finish()
    p1_top = AR.top
    r_hTp = [R(f"hTp{i}") for i in range(16)]
    xm_v = xm.rearrange("(j k) d -> k j d", k=8)
    xp_v = xp.rearrange("(j k) d -> k j d", k=8)
    tiles = []
    for k in range(8):
        tiles.append((xp_v[k], 128, hTp, r_hTp, k * 128))
    for k in range(8):
        tiles.append((xm_v[k], 128, hT, r_hT, k * 128))
    tiles.append((xs, NS, hT, r_hT, S0))
    norm_transpose(tiles, norm_g, AR)
    B.dve(lambda e: e.tensor_copy(out=hT[:, :, H0:H0 + 2], in_=hTp[:, :, 6 * 128 + 127:8 * 128:128]),
          reads=r_hTp, writes=r_hT)
    for t in range(8):
        wv, r_w = load_w([w_in[:, C_SU + t * 128:C_SU + (t + 1) * 128]], 16)
        s = project(wv, r_w, 0, 16, hTp, r_hTp, [(0, 512, 0), (512, 512, 1)])
        B.alt(lambda e, s=s, t=t: cp(e, suTp[:, t, :], pset(s)[:, 0:1024]), reads=rset(s)[0:2], writes=[r_suTp[t]])
    B.S.barrier()
    if stop == "p1":
        return finish()

    AR.top = p1_top
    BT = [0]

    def balloc(shape, dt=F32):
        n = 1
        for x in shape:
            n *= x
        w = n if dt == F32 else (n + 1) // 2
        w = (w + 7) // 8 * 8
        off = BT[0]
        BT[0] += w
        assert BT[0] <= BIGW, "BIG overflow"
        v = BIG[:, off:off + w]
        if dt != F32:
            v = v.bitcast(dt)
        v = v[:, 0:n]
        if len(shape) == 2:
            v = v.rearrange("p (a b) -> p a b", a=shape[0])
        elif len(shape) == 3:
            v = v.rearrange("p (a b c) -> p a b c", a=shape[0], b=shape[1])
        return v

    WTr = [AR.alloc([8, 64], BF16) for _ in range(2)]
    WTi = [AR.alloc([8, 64], BF16) for _ in range(2)]
    CS2r = [AR.alloc([8, 128], BF16) for _ in range(2)]
    CS2i = [AR.alloc([8, 128], BF16) for _ in range(2)]
    Mg = [AR.alloc([8, 128], BF16) for _ in range(2)]
    r_WT = [R("WT0"), R("WT1")]
    r_CS2 = [R("CS20"), R("CS21")]
    r_Mg = [R("Mg0"), R("Mg1")]
    Wnr = AR.alloc([128])
    Wni = AR.alloc([128])
    CSnr = AR.alloc([128])
    CSni = AR.alloc([128])
    gt1 = AR.alloc([128])
    gt2 = AR.alloc([128])
    T2r = AR.alloc([128])
    T2i = AR.alloc([128])
    CSzr = AR.alloc([128])
    CSzi = AR.alloc([128])
    hm = AR.alloc([2])
    r_T2 = R("T2")
    r_CSz = R("CSz")
    r_Wn = R("Wn")
    r_CSn = R("CSn")
    r_gt = R("gt")
    rcos = AR.alloc([256])
    rsin = AR.alloc([256])
    rt1 = AR.alloc([256])
    rt2 = AR.alloc([256])
    rzr = AR.alloc([256])
    rzi = AR.alloc([256])
    r_rot = R("rot")
    suTb = AR.alloc([1088], BF16)
    r_suTb = R("suTb")
    r_suTb2 = R("suTb2")
    Ub = [AR.alloc([8, NJ], BF16) for _ in range(2)]
    r_Ub = [[R(f"Ub{a}_{i}") for i in range(20)] for a in range(2)]
    r_scr1 = [[R(), R()] for _ in range(2)]
    r_scr2 = [[R() for _ in range(12)] for _ in range(2)]
    Spvr = [AR.alloc([4, 132], BF16) for _ in range(2)]
    Spvi = [AR.alloc([4, 132], BF16) for _ in range(2)]
    r_Spv = [R("Spv0"), R("Spv1")]
    r_Ysb = R("Ysb")
    print("arena top after S5 allocs", AR.top, "of", ARENA_W)
    Ysb = balloc([8, YW])
    tmp_cc = balloc([NX])
    vext = balloc([10, 128])
    vs = balloc([16, 6])
    vh = balloc([2])
    acc = balloc([1088])
    tmp_s = balloc([1088])
    bnr = balloc([32, 16])
    bni = balloc([32, 16])
    r_tmpcc = R("tmpcc")
    r_vext = R("vext")
    r_acc = R("acc")
    r_acc2 = R("acc2")
    r_tmps = R("tmps")
    r_bn = R("bn")
    print("BIG top", BT[0], "of", BIGW)

    ld_state["stream"] = "par1"
    ld(bnr, DAP(b_re, 0, [[16, 128], [2048, 32], [1, 16]]), r_bn)
    ld(bni, DAP(b_im, 0, [[16, 128], [2048, 32], [1, 16]]), r_bn)
    ld_group_done()
    for q4 in range(4):
        B.dma(acc[0:16, 0:1024], ssr[:, q4 * 1024:(q4 + 1) * 1024], writes=[r_acc], stream="sin0")
        B.dma(tmp_s[0:16, 0:1024], ssi[:, q4 * 1024:(q4 + 1) * 1024], writes=[r_tmps], stream="sin1")
        for gpl in range(8):
            B.pe(lambda e, gpl=gpl: e.transpose(out=pall[:, 6, gpl * 16:(gpl + 1) * 16], in_=acc[0:16, gpl * 128:(gpl + 1) * 128],
                                                identity=identf[0:16, 0:16]), reads=[r_acc, r_ident], writes=[r_pb[6]])
            B.pe(lambda e, gpl=gpl: e.transpose(out=pall[:, 7, gpl * 16:(gpl + 1) * 16], in_=tmp_s[0:16, gpl * 128:(gpl + 1) * 128],
                                                identity=identf[0:16, 0:16]), reads=[r_tmps, r_ident], writes=[r_pb[7]])
        B.act(lambda e, q4=q4: cp(e, SinTr[:, q4 * 8:(q4 + 1) * 8, :].rearrange("p a b -> p (a b)"), pall[:, 6, 0:128]),
              reads=[r_pb[6]], writes=[r_SinT])
        B.dve(lambda e, q4=q4: cp(e, SinTi[:, q4 * 8:(q4 + 1) * 8, :].rearrange("p a b -> p (a b)"), pall[:, 7, 0:128]),
              reads=[r_pb[7]], writes=[r_SinT])
    B.pool(lambda e: e.iota(jtab, [[1, 256]], base=0, channel_multiplier=0, allow_small_or_imprecise_dtypes=True),
           writes=[r_const])
    B.pool(lambda e: e.memset(hm[0:64, 0:1], 1.0), writes=[r_const])
    B.pool(lambda e: e.memset(hm[64:128, 0:1], 0.0), writes=[r_const])
    B.pool(lambda e: e.memset(hm[0:64, 1:2], 0.0), writes=[r_const])
    B.pool(lambda e: e.memset(hm[64:128, 1:2], 1.0), writes=[r_const])
    for sl_ in range(2):
        B.pool(lambda e, sl_=sl_: e.memset(Ub[sl_][64:128, :, 256:272], 0.0), writes=[r_Ub[sl_][19]])
    B.pool(lambda e: e.memset(mask1, 1.0), writes=[r_const])
    B.pool(lambda e: e.affine_select(out=mask1, in_=mask1, pattern=[[16, 8], [0, 16]],
                                     compare_op=ALU.is_ge, fill=0.0, base=15, channel_multiplier=-1),
           reads=[r_const], writes=[r_const])

    def tt(out, a, b, op, reads, writes):
        B.dve(lambda e: e.tensor_tensor(out=out, in0=a, in1=b, op=op), reads=reads, writes=writes)

    def ts(out, a, s1, s2, op0, op1, reads, writes):
        if s2 is None:
            B.dve(lambda e: e.tensor_scalar(out=out, in0=a, scalar1=s1, scalar2=None, op0=op0), reads=reads, writes=writes)
        else:
            B.dve(lambda e: e.tensor_scalar(out=out, in0=a, scalar1=s1, scalar2=s2, op0=op0, op1=op1), reads=reads,
                  writes=writes)

    def stt(out, a, s, b, op0, op1, reads, writes):
        B.dve(lambda e: e.scalar_tensor_tensor(out=out, in0=a, scalar=s, in1=b, op0=op0, op1=op1), reads=reads,
              writes=writes)

    def actf(out, a, func, reads, writes, scale=1.0, bias=0.0):
        B.act(lambda e: e.activation(out=out, in_=a, func=func, scale=scale, bias=bias), reads=reads, writes=writes)

    def pp(n):
        return P[n], rP[n]

    def ptt(o, a, b, op):
        tt(P[o], P[a], P[b], op, [rP[a], rP[b]], [rP[o]])

    def wrap(o, a):
        ts(P["tC"], P[a], 1.0 / TWO_PI, MAGIC, ALU.mult, ALU.add, [rP[a]], [rP["tC"]])
        ts(P["tC"], P["tC"], MAGIC, None, ALU.subtract, None, [rP["tC"]], [rP["tC"]])
        stt(P[o], P["tC"], -TWO_PI, P[a], ALU.mult, ALU.add, [rP["tC"], rP[a]], [rP[o]])
        ts(P[o], P[o], -PI_S, PI_S, ALU.max, ALU.min, [rP[o]], [rP[o]])

    actf(P["dt"], P["dtl"], AF.Exp, [rP["dtl"]], [rP["dt"]])
    ptt("lrdt", "lamr", "dt", ALU.mult)
    actf(P["mag"], P["lrdt"], AF.Exp, [rP["lrdt"]], [rP["mag"]])
    actf(P["imag"], P["lrdt"], AF.Exp, [rP["lrdt"]], [rP["imag"]], scale=-1.0)
    actf(P["r8"], P["lrdt"], AF.Exp, [rP["lrdt"]], [rP["r8"]], scale=8.0)
    ptt("ang", "lami", "dt", ALU.mult)
    wrap("angw", "ang")
    actf(P["sn"], P["angw"], AF.Sin, [rP["angw"]], [rP["sn"]])
    actf(P["tA"], P["angw"], AF.Abs, [rP["angw"]], [rP["tA"]])
    actf(P["cs"], P["tA"], AF.Sin, [rP["tA"]], [rP["cs"]], scale=-1.0, bias=math.pi / 2)
    ptt("ar", "mag", "cs", ALU.mult)
    ptt("ai", "mag", "sn", ALU.mult)
    ptt("air", "imag", "cs", ALU.mult)
    stt(P["aii"], P["imag"], -1.0, P["sn"], ALU.mult, ALU.mult, [rP["imag"], rP["sn"]], [rP["aii"]])
    ts(P["tB"], P["angw"], 8.0, None, ALU.mult, None, [rP["angw"]], [rP["tB"]])
    wrap("phi8", "tB")
    ts(P["p8s"], P["phi8"], 1.0 / TWO_PI, None, ALU.mult, None, [rP["phi8"]], [rP["p8s"]])
    ptt("den", "lamr", "lamr", ALU.mult)
    ptt("tA", "lami", "lami", ALU.mult)
    ptt("den", "den", "tA", ALU.add)
    B.dve(lambda e: e.reciprocal(out=P["rden"], in_=P["den"]), reads=[rP["den"]], writes=[rP["rden"]])
    ts(P["am1"], P["ar"], -1.0, None, ALU.add, None, [rP["ar"]], [rP["am1"]])
    ptt("tA", "am1", "lamr", ALU.mult)
    ptt("tB", "ai", "lami", ALU.mult)
    ptt("tA", "tA", "tB", ALU.add)
    ptt("fr", "tA", "rden", ALU.mult)
    ptt("tA", "ai", "lamr", ALU.mult)
    ptt("tB", "am1", "lami", ALU.mult)
    ptt("tA", "tA", "tB", ALU.subtract)
    ptt("fi", "tA", "rden", ALU.mult)
    frb = P["fr"].unsqueeze(2).to_broadcast([128, 32, 16])
    fib = P["fi"].unsqueeze(2).to_broadcast([128, 32, 16])
    t16a = balloc([32, 16])
    t16b = balloc([32, 16])
    r_t16 = R("t16")
    tt(t16a, bnr, frb, ALU.mult, [r_bn, rP["fr"]], [r_t16])
    tt(t16b, bni, fib, ALU.mult, [r_bn, rP["fi"]], [r_t16])
    tt(Bbr, t16a, t16b, ALU.subtract, [r_t16], [r_Bb])
    tt(t16a, bni, frb, ALU.mult, [r_bn, rP["fr"]], [r_t16])
    tt(t16b, bnr, fib, ALU.mult, [r_bn, rP["fi"]], [r_t16])
    tt(Bbi, t16a, t16b, ALU.add, [r_t16], [r_Bb])

    def powers(Pr, Pi, r_P, nk, br, bi):
        B.dve(lambda e: e.memset(Pr[:, 0, :], 1.0), writes=[r_P])
        B.dve(lambda e: e.memset(Pi[:, 0, :], 0.0), writes=[r_P])
        for k in range(1, nk):
            tt(P["tA"], Pr[:, k - 1, :], P[br], ALU.mult, [r_P, rP[br]], [rP["tA"]])
            tt(P["tB"], Pi[:, k - 1, :], P[bi], ALU.mult, [r_P, rP[bi]], [rP["tB"]])
            tt(Pr[:, k, :], P["tA"], P["tB"], ALU.subtract, [rP["tA"], rP["tB"]], [r_P])
            tt(P["tA"], Pr[:, k - 1, :], P[bi], ALU.mult, [r_P, rP[bi]], [rP["tA"]])
            tt(P["tB"], Pi[:, k - 1, :], P[br], ALU.mult, [r_P, rP[br]], [rP["tB"]])
            tt(Pi[:, k, :], P["tA"], P["tB"], ALU.add, [rP["tA"], rP["tB"]], [r_P])

    powers(Ppr, Ppi, r_Pp, 17, "ar", "ai")
    powers(Pmr, Pmi, r_Pm, 8, "air", "aii")

    def cmul_b(outr, outi, ar_, ai_, br_, bi_, t1, t2, reads, r_t, w_r, w_i, neg_i=False):
        tt(t1, ar_, br_, ALU.mult, reads, [r_t])
        tt(t2, ai_, bi_, ALU.mult, reads, [r_t])
        tt(outr, t1, t2, ALU.subtract, [r_t], w_r)
        tt(t1, ar_, bi_, ALU.mult, reads, [r_t])
        tt(t2, ai_, br_, ALU.mult, reads, [r_t])
        if neg_i:
            stt(outi, t1, -1.0, t2, ALU.mult, ALU.subtract, [r_t], w_i)
        else:
            tt(outi, t1, t2, ALU.add, [r_t], w_i)

    t16c, t16d, r_t16b = t16a, t16b, r_t16
    a7r = Pmr[:, 7, :].unsqueeze(2).to_broadcast([128, 32, 16])
    a7i = Pmi[:, 7, :].unsqueeze(2).to_broadcast([128, 32, 16])
    cmul_b(Spr, Spi, SinTr, SinTi, a7r, a7i, t16c, t16d, [r_SinT, r_Pm], r_t16b, [r_Sp], [r_Sp])

    def stage_G(t):
        sl = t % 2
        for gl in range(4):
            gp = 4 * t + gl
            pmr = Pmr[:, :, gp:gp + 1].to_broadcast([128, 8, 16])
            pmi = Pmi[:, :, gp:gp + 1].to_broadcast([128, 8, 16])
            bbr = Bbr[:, gp:gp + 1, :].to_broadcast([128, 8, 16])
            bbi = Bbi[:, gp:gp + 1, :].to_broadcast([128, 8, 16])
            v3 = lambda x: x.rearrange("p (k c) -> p k c", k=8)
            cmul_b(v3(Wnr), v3(Wni), pmr, pmi, bbr, bbi, v3(gt1), v3(gt2), [r_Pm, r_Bb], r_gt, [r_Wn], [r_Wn])
            ppr = Ppr[:, 0:8, gp:gp + 1].to_broadcast([128, 8, 16])
            ppi = Ppi[:, 0:8, gp:gp + 1].to_broadcast([128, 8, 16])
            cr = cnr[:, gp:gp + 1, :].to_broadcast([128, 8, 16])
            ci = cni[:, gp:gp + 1, :].to_broadcast([128, 8, 16])
            cmul_b(v3(CSnr), v3(CSni), cr, ci, ppr, ppi, v3(gt1), v3(gt2), [r_cn, r_Pp], r_gt, [r_CSn], [r_CSn],
                   neg_i=True)
            ppr2 = Ppr[:, 8:16, gp:gp + 1].to_broadcast([128, 8, 16])
            ppi2 = Ppi[:, 8:16, gp:gp + 1].to_broadcast([128, 8, 16])
            cmul_b(v3(T2r), v3(T2i), cr, ci, ppr2, ppi2, v3(gt1), v3(gt2),
                   [r_cn, r_Pp], r_gt, [r_T2], [r_T2], neg_i=True)
            for g2 in range(2):
                actf(CS2r[sl][:, 2 * gl + g2, :], T2r, AF.Copy, [r_T2, r_const], [r_CS2[sl]], scale=hm[:, g2:g2 + 1])
                actf(CS2i[sl][:, 2 * gl + g2, :], T2i, AF.Copy, [r_T2, r_const], [r_CS2[sl]], scale=hm[:, g2:g2 + 1])
            if stop == "G1":
                return
            bk = 6 + gl % 2
            B.pe(lambda e, bk=bk: e.transpose(out=pall[:, bk, 0:128], in_=Wnr, identity=identf[:]),
                 reads=[r_Wn, r_ident], writes=[r_pb[bk]])
            B.pe(lambda e, bk=bk: e.transpose(out=pall[:, bk, 128:256], in_=Wni, identity=identf[:]),
                 reads=[r_Wn, r_ident], writes=[r_pb[bk]])
            if stop == "G2":
                return
            for g2 in range(2):
                oc = 256 + g2 * 128
                actf(CSzr, CSnr, AF.Copy, [r_CSn, r_const], [r_CSz], scale=hm[:, g2:g2 + 1])
                actf(CSzi, CSni, AF.Copy, [r_CSn, r_const], [r_CSz], scale=hm[:, g2:g2 + 1])
                B.pe(mm(pall[:, bk, oc:oc + 128], Wnr, CSzr, True, False),
                     reads=[r_Wn, r_CSz], writes=[r_pb[bk]])
                B.pe(mm(pall[:, bk, oc:oc + 128], Wni, CSzi, False, True),
                     reads=[r_Wn, r_CSz], writes=[r_pb[bk]])
            if stop == "G3":
                return
            B.alt(lambda e, bk=bk, sl=sl, gl=gl: cp(e, WTr[sl][:, 2 * gl:2 * gl + 2, :],
                                                   pall[:, bk, 0:128].rearrange("p (a b) -> p a b", a=2)),
                  reads=[r_pb[bk]], writes=[r_WT[sl]])
            B.alt(lambda e, bk=bk, sl=sl, gl=gl: cp(e, WTi[sl][:, 2 * gl:2 * gl + 2, :],
                                                   pall[:, bk, 128:256].rearrange("p (a b) -> p a b", a=2)),
                  reads=[r_pb[bk]], writes=[r_WT[sl]])
            if stop == "G4":
                return
            for g2 in range(2):
                g = 2 * gp + g2
                tt(Mg[sl][:, 2 * gl + g2, :], pall[:, bk, 256 + g2 * 128:384 + g2 * 128], mask1, ALU.mult,
                   [r_pb[bk], r_const], [r_Mg[sl]])
                stt(Mg[sl][:, 2 * gl + g2, :], identf[:], Dcol[:, g:g + 1], Mg[sl][:, 2 * gl + g2, :], ALU.mult, ALU.add,
                    [r_ident, r_const, r_Mg[sl]], [r_Mg[sl]])

    def stage_A(t):
        sl = t % 2
        wv, r_w = load_w([w_in[:, C_SU + t * 128:C_SU + (t + 1) * 128]], 16)
        s = project(wv, r_w, 0, 16, hT, r_hT, SEG_MS)
        B.act(lambda e, s=s: cp(e, suTb[:, 0:1024], pset(s)[:, 0:1024]), reads=rset(s), writes=[r_suTb])
        B.act(lambda e, s=s: cp(e, suTb[:, 1024:1088].rearrange("p (k s) -> p s k", k=4),
                                pset(s)[:, 1024:1088].rearrange("p (s k) -> p s k", k=4)), reads=rset(s), writes=[r_suTb2])
        wv, r_w = load_w([w_in[:, C_SZ + t * 128:C_SZ + (t + 1) * 128]], 16)
        s = project(wv, r_w, 0, 16, hT, r_hT, SEG_MS)
        actf(ysT[:, t, :], pset(s)[:, 0:1088], AF.Silu, rset(s), [r_ys[t]])
        sc = scr1[sl]
        r_sc = r_scr1[sl]
        r_f = fence("act", [r_suTb, r_suTb2])
        B.dma(sc[:, 0:1024], suTp[:, t, :], reads=[r_suTp[t]], writes=[r_sc[0]], stream=f"shA{sl}")
        B.dma(sc[:, 1024:2112], suTb, reads=[r_suTb, r_suTb2, r_f], writes=[r_sc[1]], stream=f"shA{sl}")
        di = 0
        for slab in range(2):
            for k in range(8):
                B.dma(Ub[sl][k * 16:(k + 1) * 16, :, slab * 128:(slab + 1) * 128],
                      DAP(sc, slab * 1024 + k * 128, [[SCR1_W, 16], [16 * SCR1_W, 8], [1, 128]]),
                      reads=r_sc, writes=[r_Ub[sl][di]], stream=f"shB{sl}")
                di += 1
        for k in range(4):
            B.dma(Ub[sl][k * 16:(k + 1) * 16, :, 256:272],
                  DAP(sc, 2048 + k * 16, [[SCR1_W, 16], [16 * SCR1_W, 8], [1, 16]]),
                  reads=r_sc, writes=[r_Ub[sl][di]], stream=f"shB{sl}")
            di += 1

    def stage_B(t):
        sl = t % 2
        for gl in range(4):
            gp = 4 * t + gl
            for (WT_, bk) in ((WTr[sl], 6), (WTi[sl], 7)):
                for g2 in range(2):
                    g8 = 2 * gl + g2
                    rows = slice(g2 * 64, (g2 + 1) * 64)
                    B.pe(mm(pall[rows, bk, 0:256], WT_[:, g8, :], Ub[sl][:, g8, 0:256], True, True),
                         reads=[r_WT[sl]] + r_Ub[sl], writes=[r_pb[bk]])
                    B.pe(mm(pall[rows, bk, 256:272], WT_[:, g8, :], Ub[sl][:, g8, 256:272], True, True),
                         reads=[r_WT[sl]] + r_Ub[sl], writes=[r_pb[bk]])
            zr_ps = pall[:, 6, 0:256]
            zi_ps = pall[:, 7, 0:256]
            p8s = P["p8s"][:, gp:gp + 1]
            ph8 = P["phi8"][:, gp:gp + 1]
            actf(rt1, jtab, AF.Identity, [r_const, rP["p8s"]], [r_rot], scale=p8s, bias=MAGIC)
            actf(rt1, rt1, AF.Identity, [r_rot], [r_rot], bias=-MAGIC)
            actf(rt2, jtab, AF.Copy, [r_const, rP["phi8"]], [r_rot], scale=ph8)
            stt(rt2, rt1, -TWO_PI, rt2, ALU.mult, ALU.add, [r_rot], [r_rot])
            ts(rt2, rt2, -PI_S, PI_S, ALU.max, ALU.min, [r_rot], [r_rot])
            actf(rsin, rt2, AF.Sin, [r_rot], [r_rot])
            actf(rt1, rt2, AF.Abs, [r_rot], [r_rot])
            actf(rcos, rt1, AF.Sin, [r_rot], [r_rot], scale=-1.0, bias=math.pi / 2)
            tt(rt1, zr_ps, rcos, ALU.mult, [r_pb[6], r_rot], [r_rot])
            tt(rt2, zi_ps, rsin, ALU.mult, [r_pb[7], r_rot], [r_rot])
            tt(rzr, rt1, rt2, ALU.add, [r_rot], [r_rot])
            tt(rt1, zi_ps, rcos, ALU.mult, [r_pb[7], r_rot], [r_rot])
            tt(rt2, zr_ps, rsin, ALU.mult, [r_pb[6], r_rot], [r_rot])
            tt(rzi, rt1, rt2, ALU.subtract, [r_rot], [r_rot])
            B.dve(lambda e, gp=gp: e.tensor_copy(out=ZSr[:, gp, :], in_=pall[:, 6, 256:272]), reads=[r_pb[6]],
                  writes=[r_ZS[t]])
            B.dve(lambda e, gp=gp: e.tensor_copy(out=ZSi[:, gp, :], in_=pall[:, 7, 256:272]), reads=[r_pb[7]],
                  writes=[r_ZS[t]])
            r8b = P["r8"][:, gp:gp + 1].to_broadcast([128, 256])
            B.dve(lambda e, r8b=r8b: e.tensor_tensor_scan(out=rt1, data0=r8b, data1=rzr, initial=0.0, op0=ALU.mult,
                                                         op1=ALU.add), reads=[r_rot, rP["r8"]], writes=[r_rot])
            B.dve(lambda e, r8b=r8b: e.tensor_tensor_scan(out=rt2, data0=r8b, data1=rzi, initial=0.0, op0=ALU.mult,
                                                         op1=ALU.add), reads=[r_rot, rP["r8"]], writes=[r_rot])
            c_ = rcos[:, 127:256]
            s_ = rsin[:, 127:256]
            tt(rzr[:, 0:129], rt1[:, 127:256], c_, ALU.mult, [r_rot], [r_rot])
            tt(rzi[:, 0:129], rt2[:, 127:256], s_, ALU.mult, [r_rot], [r_rot])
            tt(Spvr[sl][:, gl, 0:129], rzr[:, 0:129], rzi[:, 0:129], ALU.subtract, [r_rot], [r_Spv[sl]])
            B.dve(lambda e, gp=gp: e.tensor_tensor(out=SFr[:, gp:gp + 1], in0=rzr[:, 128:129], in1=rzi[:, 128:129],
                                                   op=ALU.subtract), reads=[r_rot], writes=[r_SF[t]])
            tt(rzr[:, 0:129], rt2[:, 127:256], c_, ALU.mult, [r_rot], [r_rot])
            tt(rzi[:, 0:129], rt1[:, 127:256], s_, ALU.mult, [r_rot], [r_rot])
            tt(Spvi[sl][:, gl, 0:129], rzr[:, 0:129], rzi[:, 0:129], ALU.add, [r_rot], [r_Spv[sl]])
            B.dve(lambda e, gp=gp: e.tensor_tensor(out=SFi[:, gp:gp + 1], in0=rzr[:, 128:129], in1=rzi[:, 128:129],
                                                   op=ALU.add), reads=[r_rot], writes=[r_SF[t]])

    def stage_C(t):
        sl = t % 2
        for g8 in range(8):
            gl = g8 // 2
            bk = 6 + g8 % 2
            c0 = (g8 // 2) * 128
            B.pe(mm(pall[:, bk, c0:c0 + 128], Mg[sl][:, g8, :], Ub[sl][:, g8, 128:256], True, False),
                 reads=[r_Mg[sl]] + r_Ub[sl], writes=[r_pb[bk]], drain=True)
            B.pe(mm(pall[:, bk, c0:c0 + 128], CS2r[sl][:, g8, :], Spvr[sl][:, gl, 0:128], False, False),
                 reads=[r_CS2[sl], r_Spv[sl]], writes=[r_pb[bk]])
            B.pe(mm(pall[:, bk, c0:c0 + 128], CS2i[sl][:, g8, :], Spvi[sl][:, gl, 0:128], False, True),
                 reads=[r_CS2[sl], r_Spv[sl]], writes=[r_pb[bk]])
        for b2 in range(2):
            B.act(lambda e, b2=b2: cp(e, Ysb[:, b2:8:2, 0:128], pall[:, 6 + b2, :].rearrange("p (a b) -> p a b", a=4)),
                  reads=[r_pb[6 + b2]], writes=[r_Ysb])
        for g8 in range(8):
            gl = g8 // 2
            gp = 4 * t + gl
            bk = 6 + g8 % 2
            c0 = (g8 // 2) * 16
            B.pe(mm(pall[:, bk, c0:c0 + 16], Mg[sl][:, g8, :], Ub[sl][:, g8, 256:272], True, False),
                 reads=[r_Mg[sl]] + r_Ub[sl], writes=[r_pb[bk]], drain=True)
            B.pe(mm(pall[:, bk, c0:c0 + 16], CS2r[sl][:, g8, :], Spr[:, gp, :], False, False),
                 reads=[r_CS2[sl], r_Sp], writes=[r_pb[bk]])
            B.pe(mm(pall[:, bk, c0:c0 + 16], CS2i[sl][:, g8, :], Spi[:, gp, :], False, True),
                 reads=[r_CS2[sl], r_Sp], writes=[r_pb[bk]])
        for b2 in range(2):
            B.act(lambda e, b2=b2: cp(e, Ysb[0:64, b2:8:2, 128:144],
                                      pall[0:64, 6 + b2, 0:64].rearrange("p (a b) -> p a b", a=4)),
                  reads=[r_pb[6 + b2]], writes=[r_Ysb])
        sc = scr2[sl]
        r_sc = r_scr2[sl]
        r_f = fence("act", [r_Ysb])
        for k in range(8):
            B.dma(DAP(sc, k * YW, [[8 * YW, 16], [16 * 8 * YW, 8], [1, 128]]), Ysb[k * 16:(k + 1) * 16, :, 0:128],
                  reads=[r_Ysb, r_f], writes=[r_sc[k]], stream=f"shC{sl}")
        for k in range(4):
            B.dma(DAP(sc, k * YW + 128, [[8 * YW, 16], [16 * 8 * YW, 8], [1, 16]]), Ysb[k * 16:(k + 1) * 16, :, 128:144],
                  reads=[r_Ysb, r_f], writes=[r_sc[8 + k]], stream=f"shC{sl}")
        B.dma(acc[:, 0:1024].rearrange("p (k j) -> p k j", k=8), DAP(sc, 0, [[8 * YW, 128], [YW, 8], [1, 128]]),
              reads=r_sc, writes=[r_acc], stream="shD0")
        B.dma(acc[:, 1024:1088].rearrange("p (k s) -> p k s", k=4), DAP(sc, 128, [[8 * YW, 128], [YW, 4], [1, 16]]),
              reads=r_sc, writes=[r_acc2], stream="shD1")
        acs = acc[:, 1024:1088].rearrange("p (k s) -> p s k", k=4)
        yss = ysT[:, t, 1024:1088].rearrange("p (s k) -> p s k", k=4)
        tt(acc[:, 0:1024], acc[:, 0:1024], ysT[:, t, 0:1024], ALU.mult, [r_acc, r_ys[t]], [r_acc])
        tt(acs, acs, yss, ALU.mult, [r_acc2, r_ys[t]], [r_acc2])
        actf(ysT[:, t, 0:1024], acc[:, 0:1024], AF.Gelu_apprx_tanh, [r_acc], [r_ys[t]])
        actf(yss, acs, AF.Gelu_apprx_tanh, [r_acc2, r_ys[t]], [r_ys[t]])

    def stage_conv(t):
        wv, r_w = load_w([w_in[:, C_CC + t * 128:C_CC + (t + 1) * 128]], 16)
        s = project(wv, r_w, 0, 16, hT, r_hT, SEG_MSH)
        actf(tmp_cc, pset(s)[:, 0:NX], AF.Copy, rset(s), [r_tmpcc])
        wv, r_w = load_w([w_in[:, C_CH + t * 128:C_CH + (t + 1) * 128]], 16)
        s = project(wv, r_w, 0, 16, hT, r_hT, SEG_MSH)
        ps = pset(s)
        tt(vext[:, 2:10, :], ps[:, 0:1024].rearrange("p (k j) -> p k j", k=8),
           tmp_cc[:, 0:1024].rearrange("p (k j) -> p k j", k=8), ALU.mult, rset(s) + [r_tmpcc], [r_vext])
        tt(vs[:, :, 2:6], ps[:, 1024:1088].rearrange("p (s k) -> p s k", k=4),
           tmp_cc[:, 1024:1088].rearrange("p (s k) -> p s k", k=4), ALU.mult, rset(s) + [r_tmpcc], [r_vext])
        tt(vh, ps[:, 1088:1090], tmp_cc[:, 1088:1090], ALU.mult, rset(s) + [r_tmpcc], [r_vext])
        cpv = lambda o, i, rd=[r_vext], wr=[r_vext]: B.act(lambda e: cp(e, o, i), reads=rd, writes=wr)
        cpv(vext[:, 1, 1:128], vext[:, 9, 0:127])
        cpv(vext[:, 0, 1:128], vext[:, 8, 0:127])
        cpv(vext[:, 1, 0:1], vh[:, 1:2])
        cpv(vext[:, 0, 0:1], vh[:, 0:1])
        cpv(vs[:, :, 0:2], sconvT[:, t, :, :], [r_sconvT, r_vext], [r_vext])
        cpv(ncp[:, t, 0:1], vext[:, 8, 127:128], [r_vext], [r_ncp])
        cpv(ncp[:, t, 1:2], vext[:, 9, 127:128], [r_vext], [r_ncp])
        cpv(ncs[:, t, :, :], vs[:, :, 4:6], [r_vext], [r_ncs])
        accm = acc[:, 0:1024].rearrange("p (k j) -> p k j", k=8)
        accs = acc[:, 1024:1088].rearrange("p (s k) -> p s k", k=4)
        actf(accm, vext[:, 2:10, :], AF.Copy, [r_vext, r_const], [r_acc], scale=cw[:, t, 2:3])
        stt(accm, vext[:, 1:9, :], cw[:, t, 1:2], accm, ALU.mult, ALU.add, [r_vext, r_const, r_acc], [r_acc])
        stt(accm, vext[:, 0:8, :], cw[:, t, 0:1], accm, ALU.mult, ALU.add, [r_vext, r_const, r_acc], [r_acc])
        actf(accs, vs[:, :, 2:6], AF.Copy, [r_vext, r_const], [r_acc2], scale=cw[:, t, 2:3])
        stt(accs, vs[:, :, 1:5], cw[:, t, 1:2], accs, ALU.mult, ALU.add, [r_vext, r_const, r_acc2], [r_acc2])
        stt(accs, vs[:, :, 0:4], cw[:, t, 0:1], accs, ALU.mult, ALU.add, [r_vext, r_const, r_acc2], [r_acc2])
        wv, r_w = load_w([w_in[:, C_CZ + t * 128:C_CZ + (t + 1) * 128]], 16)
        s = project(wv, r_w, 0, 16, hT, r_hT, SEG_MS)
        actf(tmp_s, pset(s)[:, 0:1088], AF.Silu, rset(s), [r_tmps])
        wv, r_w = load_w([w_in[:, C_CB + t * 128:C_CB + (t + 1) * 128]], 16)
        s = project(wv, r_w, 0, 16, hT, r_hT, SEG_MS)
        tt(tmp_s, pset(s)[:, 0:1088], tmp_s, ALU.mult, rset(s) + [r_tmps], [r_tmps])
        tt(cact[:, t, :], acc, tmp_s, ALU.mult, [r_acc, r_acc2, r_tmps], [r_cact[t]])

    if stop == "prep":
        return finish()
    for i in range(nloop + 1):
        if i < nloop:
            stage_A(i)
            stage_conv(i)
            stage_G(i)
        if i >= 1:
            stage_C(i - 1)
            if stop == "C":
                return finish()
        if i < nloop:
            stage_B(i)
            if stop == "B" and i == 1:
                return finish()
    if stop == "loop":
        return finish()

    fr_ = P["tA"]
    fi_ = P["tB"]
    r_fin = R("fin")
    t32a = P["am1"]
    t32b = P["den"]
    r_t32 = R("t32")
    cmul_b(fr_, fi_, SFr, SFi, Ppr[:, 7, :], Ppi[:, 7, :], t32a, t32b, r_SF + [r_Pp], r_t32, [r_fin], [r_fin])
    r_f = fence("dve", [r_fin])
    B.dma(DAP(spr_o, 0, [[1, 128], [128, 32]]), fr_, reads=[r_fin, r_f], stream="out", allow_slow_non_contiguous=True)
    B.dma(DAP(spi_o, 0, [[1, 128], [128, 32]]), fi_, reads=[r_fin, r_f], stream="out", allow_slow_non_contiguous=True)
    a4r = Ppr[:, 4, :].unsqueeze(2).to_broadcast([128, 32, 16])
    a4i = Ppi[:, 4, :].unsqueeze(2).to_broadcast([128, 32, 16])
    a3r = Ppr[:, 3, :].unsqueeze(2).to_broadcast([128, 32, 16])
    a3i = Ppi[:, 3, :].unsqueeze(2).to_broadcast([128, 32, 16])
    n1r, n1i = bnr, bni
    n2r = tmp_s[:, 0:512].rearrange("p (a b) -> p a b", a=32)
    n2i = tmp_s[:, 512:1024].rearrange("p (a b) -> p a b", a=32)
    r_n1 = R("n1")
    r_n2 = R("n2")
    cmul_b(n1r, n1i, SinTr, SinTi, a4r, a4i, t16c, t16d, [r_SinT, r_Pp, r_bn], r_t16b, [r_n1, r_bn], [r_n1, r_bn])
    cmul_b(n2r, n2i, ZSr, ZSi, a3r, a3i, t16c, t16d, r_ZS + [r_Pp, r_tmps], r_t16b, [r_n2, r_tmps], [r_n2, r_tmps])
    tt(n1r, n1r, n2r, ALU.add, [r_n1, r_n2], [r_n1])
    tt(n1i, n1i, n2i, ALU.add, [r_n1, r_n2], [r_n1])
    r_f = fence("dve", [r_n1, r_ncp, r_ncs])
    ost = [tmp_cc[0:16, 0:512], tmp_cc[0:16, 512:1024], vext[0:16, :, :].rearrange("p a b -> p (a b)")[:, 0:512],
           vext[0:16, :, :].rearrange("p a b -> p (a b)")[:, 512:1024]]
    r_ost = [R(f"ost{i}") for i in range(4)]
    oi = 0
    for q8 in range(8):
        for (src3, dsto, bk) in ((n1r, snr_o, 6), (n1i, sni_o, 7)):
            for gpl in range(4):
                B.pe(lambda e, src3=src3, bk=bk, gpl=gpl, q8=q8: e.transpose(
                    out=pall[0:16, bk, gpl * 128:(gpl + 1) * 128], in_=src3[:, 4 * q8 + gpl, :], identity=identf[:, :]),
                    reads=[r_n1, r_ident], writes=[r_pb[bk]])
            o_ = ost[oi % 4]
            r_o = r_ost[oi % 4]
            oi += 1
            B.act(lambda e, o_=o_, bk=bk: cp(e, o_, pall[0:16, bk, :]), reads=[r_pb[bk]], writes=[r_o, r_tmpcc, r_vext])
            r_f2 = fence("act", [r_o])
            B.dma(dsto[:, q8 * 512:(q8 + 1) * 512], o_, reads=[r_o, r_f2], stream="out")
    nst = acc[0:32, 0:1024]
    for t in range(8):
        bk = 6 + t // 4
        B.pe(lambda e, t=t, bk=bk: e.transpose(out=pall[0:32, bk, (t % 4) * 128:(t % 4 + 1) * 128],
                                               in_=ncs[:, t, :, :].rearrange("p s r -> p (s r)"), identity=identf[:, :]),
             reads=[r_ncs, r_ident], writes=[r_pb[bk]])
    for b2 in range(2):
        B.act(lambda e, b2=b2: cp(e, nst[:, b2 * 512:(b2 + 1) * 512], pall[0:32, 6 + b2, :]), reads=[r_pb[6 + b2]],
              writes=[r_acc])
    r_f3 = fence("act", [r_acc])
    B.dma(ncs_o.rearrange("s r c -> (s r) c"), nst, reads=[r_acc, r_f3], stream="out")
    for r in range(2):
        B.dma(DAP(ncp_o, r * 1024, [[1, 128], [128, 8]]), ncp[:, :, r], reads=[r_ncp, r_f], stream="out",
              allow_slow_non_contiguous=True)
    if "cact" in B.dbg:
        o = B.dout("dbg_cact", [128, 8, 1088], BF16)
        B.dma(o, cact[:, :, :], reads=r_cact, stream="dbg")
        o = B.dout("dbg_ys", [128, 8, 1088], BF16)
        B.dma(o, ysT[:, :, :], reads=r_ys, stream="dbg")
    B.S.barrier()
    if stop == "loopout":
        return finish()

    AR.top = 0
    tg = AR.alloc([1088])
    tb = AR.alloc([1088])
    r_tg = R("tg")
    r_tb = R("tb")
    r_mg = [R(f"mg{i}") for i in range(16)]

    def gate_sig(col, dst, r_dst):
        wv, r_w = load_w([w_in[:, col:col + 128]], 16)
        s = project(wv, r_w, 0, 16, hT, r_hT, SEG_MS)
        actf(dst, pset(s)[:, 0:1088], AF.Sigmoid, rset(s), [r_dst])

    def branch_out(wsrc, j, actbuf, r_actbuf):
        wv, r_w = load_w([wsrc[:, j * 128:(j + 1) * 128]], 8)
        return project(wv, r_w, 0, 8, actbuf, r_actbuf, SEG_MS)

    for j in range(16):
        gate_sig(C_G + 0 * D + j * 128, tg, r_tg)
        s = branch_out(w_conv_out, j, cact, r_cact)
        tt(merged[:, j, :], pset(s)[:, 0:1088], tg, ALU.mult, rset(s) + [r_tg], [r_mg[j]])
    for j in range(16):
        gate_sig(C_G + 1 * D + j * 128, tg, r_tg)
        s = branch_out(w_glu_b, j, ysT, r_ys)
        actf(tb, pset(s)[:, 0:1088], AF.Sigmoid, rset(s), [r_tb])
        tt(tg, tg, tb, ALU.mult, [r_tg, r_tb], [r_tg])
        s = branch_out(w_glu_a, j, ysT, r_ys)
        tt(tb, pset(s)[:, 0:1088], tg, ALU.mult, rset(s) + [r_tg], [r_tb])
        tt(merged[:, j, :], merged[:, j, :], tb, ALU.add, [r_mg[j], r_tb], [r_mg[j]])
    if "m2" in B.dbg:
        o = B.dout("dbg_m2", [128, 16, 1088], BF16)
        B.dma(o, merged, reads=r_mg, stream="dbg")
    B.S.barrier()
    if stop == "post":
        return finish()

    AR.top = 0
    KT = AR.alloc([8, 256], BF16)
    Vb = AR.alloc([2, 1024], BF16)
    a_top = AR.top
    memT = AR.alloc([16, 256], BF16)
    kst = AR.alloc([1024])
    vst = AR.alloc([1024])
    r_memT = [R(f"memT{i}") for i in range(16)]
    r_KT = R("KT")
    r_Vb = R("Vb")
    r_kst = R("kst")
    r_vst = R("vst")
    norm_transpose([(mem[0:128, :], 128, memT, r_memT, 0), (mem[128:256, :], 128, memT, r_memT, 128)], mem_norm_g, AR)
    if stop == "at0":
        return finish()
    for c4 in range(8):
        wv, r_w = load_w([w_mem_k[:, c4 * 128:(c4 + 1) * 128]], 16)
        for kt in range(16):
            B.pe(mm(pall[:, 0, 0:256], wv[:, kt, 0:128], memT[:, kt, :], kt == 0, kt == 15),
                 reads=[r_w, r_memT[kt]], writes=[r_pb[0]])
        B.alt(lambda e, c4=c4: cp(e, KT[:, c4, :], pall[:, 0, 0:256]), reads=[r_pb[0]], writes=[r_KT])
        for kt in range(16):
            B.pe(mm(pall[:, 1, 0:128], memT[:, kt, 0:128], wv[:, kt, 0:128], kt == 0, kt == 15),
                 reads=[r_w, r_memT[kt]], writes=[r_pb[1]])
        B.alt(lambda e, c4=c4: cp(e, kst[:, c4 * 128:(c4 + 1) * 128], pall[:, 1, 0:128]), reads=[r_pb[1]], writes=[r_kst])
    r_fa = fence("act", [r_kst])
    r_fd = fence("dve", [r_kst])
    B.dma(mk_o, kst, reads=[r_kst, r_fa, r_fd], stream="out")
    if stop == "at0k":
        return finish()
    for c4 in range(8):
        wv, r_w = load_w([w_mem_v[:, c4 * 128:(c4 + 1) * 128]], 16)
        for mt in range(1 if stop == "at0v1" else 2):
            bk = 2 + mt
            for kt in range(16):
                B.pe(mm(pall[:, bk, 0:128], memT[:, kt, mt * 128:(mt + 1) * 128], wv[:, kt, 0:128], kt == 0, kt == 15),
                     reads=[r_w, r_memT[kt]], writes=[r_pb[bk]])
            if mt == 0:
                B.alt(lambda e, c4=c4, bk=bk: cp(e, vst[:, c4 * 128:(c4 + 1) * 128], pall[:, bk, 0:128]),
                      reads=[r_pb[bk]], writes=[r_vst])
                B.alt(lambda e, c4=c4: cp(e, Vb[:, 0, c4 * 128:(c4 + 1) * 128], vst[:, c4 * 128:(c4 + 1) * 128]),
                      reads=[r_vst], writes=[r_Vb])
            else:
                B.alt(lambda e, c4=c4, mt=mt, bk=bk: cp(e, Vb[:, mt, c4 * 128:(c4 + 1) * 128], pall[:, bk, 0:128]),
                      reads=[r_pb[bk]], writes=[r_Vb])
    if stop == "at0v":
        return finish()
    r_fa = fence("act", [r_vst])
    r_fd = fence("dve", [r_vst])
    B.dma(mv_o, vst, reads=[r_vst, r_fa, r_fd], stream="out")
    if stop == "at0w":
        return finish()
    B.S.barrier()
    if stop == "at1":
        return finish()
    AR.top = a_top
    qT = AR.alloc([8, 1088], BF16)
    azs = AR.alloc([8, 1088], BF16)
    r_qT = [R(f"qT{i}") for i in range(8)]
    r_azs = [R(f"azs{i}") for i in range(8)]
    for f in range(8):
        wv, r_w = load_w([w_in[:, C_AQ + f * 128:C_AQ + (f + 1) * 128]], 16)
        s = project(wv, r_w, 0, 16, hT, r_hT, SEG_MS)
        B.alt(lambda e, s=s, f=f: cp(e, qT[:, f, :], pset(s)[:, 0:1088]), reads=rset(s), writes=[r_qT[f]])
        wv, r_w = load_w([w_in[:, C_AZ + f * 128:C_AZ + (f + 1) * 128]], 16)
        s = project(wv, r_w, 0, 16, hT, r_hT, SEG_MS)
        actf(azs[:, f, :], pset(s)[:, 0:1088], AF.Silu, rset(s), [r_azs[f]])
    if stop == "at2":
        return finish()
    att = cact
    r_att = [R(f"att{i}") for i in range(8)]
    B.S.barrier()
    PT = [AR.alloc([2, 512], BF16) for _ in range(2)]
    r_PT = [R("PT0"), R("PT1")]
    rsum = AR.alloc([512])
    r_rsum = R("rsum")
    otmp = AR.alloc([512])
    r_otmp = R("otmp")
    SC = 1.0 / 16.0
    it = 0
    for half in range(2):
        cs_ = slice(half * 512, (half + 1) * 512)
        for h in range(4):
            pi_ = it % 2
            it += 1
            for mt in range(2):
                for dt_ in range(2):
                    B.pe(mm(pall[:, mt, :], KT[:, 2 * h + dt_, mt * 128:(mt + 1) * 128], qT[:, 2 * h + dt_, cs_],
                            dt_ == 0, dt_ == 1), reads=[r_KT, r_qT[2 * h + dt_]], writes=[r_pb[mt]])
            for mt in range(2):
                actf(PT[pi_][:, mt, :], pall[:, mt, :], AF.Exp, [r_pb[mt]], [r_PT[pi_]], scale=SC)
            for mt in range(2):
                B.pe(mm(pall[:, 2, :], onesb[:], PT[pi_][:, mt, :], mt == 0, mt == 1), reads=[r_ident, r_PT[pi_]],
                     writes=[r_pb[2]])
            B.dve(lambda e: e.reciprocal(out=rsum, in_=pall[:, 2, :]), reads=[r_pb[2]], writes=[r_rsum])
            for dt_ in range(2):
                bk = 3 + dt_
                f = 2 * h + dt_
                for mt in range(2):
                    B.pe(mm(pall[:, bk, :], Vb[:, mt, f * 128:(f + 1) * 128], PT[pi_][:, mt, :], mt == 0, mt == 1),
                         reads=[r_Vb, r_PT[pi_]], writes=[r_pb[bk]])
                tt(otmp, pall[:, bk, :], rsum, ALU.mult, [r_pb[bk], r_rsum], [r_otmp])
                tt(att[:, f, cs_], otmp, azs[:, f, cs_], ALU.mult, [r_otmp, r_azs[f]], [r_att[f]])
    if stop == "at3":
        return finish()
    Kb = [AR.alloc([2, 1024], BF16) for _ in range(2)]
    Vs = [AR.alloc([2, 1024], BF16) for _ in range(2)]
    KTs = [AR.alloc([8, 256], BF16) for _ in range(2)]
    PTs = AR.alloc([2, 4, 4], BF16)
    rs16 = AR.alloc([4, 4])
    o32 = AR.alloc([8, 4])
    r_Kb = [R("Kb0"), R("Kb1")]
    r_Vs = [R("Vs0"), R("Vs1")]
    r_KTs = [R("KTs0"), R("KTs1")]
    r_PTs = R("PTs")
    r_rs16 = R("rs16")
    r_o32 = R("o32")
    for sq in range(16):
        sl = sq % 2
        B.dma(Kb[sl], ck[sq].rearrange("(a p) c -> p a c", p=128), writes=[r_Kb[sl]], stream=f"kvk{sl}", q="pool")
        B.dma(Vs[sl], cv[sq].rearrange("(a p) c -> p a c", p=128), writes=[r_Vs[sl]], stream=f"kvv{sl}", q="pool")
        for mt in range(2):
            bk = mt
            pv = pall[:, bk, :].bitcast(BF16)
            for f in range(8):
                B.pe(lambda e, pv=pv, sl=sl, mt=mt, f=f: e.transpose(out=pv[:, f * 128:(f + 1) * 128],
                                                                     in_=Kb[sl][:, mt, f * 128:(f + 1) * 128],
                                                                     identity=identb[:]),
                     reads=[r_Kb[sl], r_ident], writes=[r_pb[bk]])
            B.alt(lambda e, pv=pv, sl=sl, mt=mt: cp(e, KTs[sl][:, :, mt * 128:(mt + 1) * 128],
                                                   pv.rearrange("p (a b) -> p a b", a=8)),
                  reads=[r_pb[bk]], writes=[r_KTs[sl]])
        qc = slice(S0 + 4 * sq, S0 + 4 * sq + 4)
        for mt in range(2):
            for h in range(4):
                c0 = (mt * 4 + h) * 4
                for dt_ in range(2):
                    B.pe(mm(pall[:, 7, c0:c0 + 4], KTs[sl][:, 2 * h + dt_, mt * 128:(mt + 1) * 128], qT[:, 2 * h + dt_, qc],
                            dt_ == 0, dt_ == 1), reads=[r_KTs[sl], r_qT[2 * h + dt_]], writes=[r_pb[7]])
        actf(PTs, pall[:, 7, 0:32].rearrange("p (a b c) -> p a b c", a=2, b=4), AF.Exp, [r_pb[7]], [r_PTs], scale=SC)
        for h in range(4):
            for mt in range(2):
                B.pe(mm(pall[:, 7, 64 + h * 4:64 + h * 4 + 4], onesb[:], PTs[:, mt, h, :], mt == 0, mt == 1),
                     reads=[r_ident, r_PTs], writes=[r_pb[7]])
        for h in range(4):
            for dt_ in range(2):
                f = 2 * h + dt_
                for mt in range(2):
                    B.pe(mm(pall[:, 7, 128 + f * 4:128 + f * 4 + 4], Vs[sl][:, mt, f * 128:(f + 1) * 128], PTs[:, mt, h, :],
                            mt == 0, mt == 1), reads=[r_Vs[sl], r_PTs], writes=[r_pb[7]])
        B.dve(lambda e: e.reciprocal(out=rs16, in_=pall[:, 7, 64:80].rearrange("p (a b) -> p a b", a=4)),
              reads=[r_pb[7]], writes=[r_rs16])
        tt(o32.rearrange("p (h d) c -> p h d c", d=2), pall[:, 7, 128:160].rearrange("p (h d c) -> p h d c", h=4, d=2),
           rs16.unsqueeze(2).to_broadcast([128, 4, 2, 4]), ALU.mult, [r_pb[7], r_rs16], [r_o32])
        tt(att[:, :, qc], o32, azs[:, :, qc], ALU.mult, [r_o32] + r_azs, r_att)
    if stop == "at4":
        return finish()
    AR2 = AR.top
    tg2 = AR.alloc([1088])
    tb2 = AR.alloc([1088])
    for j in range(16):
        gate_sig(C_G + 2 * D + j * 128, tg2, r_tg)
        s = branch_out(w_attn_out, j, att, r_att)
        tt(tb2, pset(s)[:, 0:1088], tg2, ALU.mult, rset(s) + [r_tg], [r_tb])
        tt(merged[:, j, :], merged[:, j, :], tb2, ALU.add, [r_mg[j], r_tb], [r_mg[j]])
    if "m3" in B.dbg:
        o = B.dout("dbg_m3", [128, 16, 1088], BF16)
        B.dma(o, merged, reads=r_mg, stream="dbg")
        o = B.dout("dbg_att", [128, 8, 1088], BF16)
        B.dma(o, att[:, :, :], reads=r_att, stream="dbg")
    B.S.barrier()
    if stop == "attn":
        return finish()

    AR.top = 0
    wo = AR.alloc([16, 2048], BF16)
    r_wo = [R(f"wo{i}") for i in range(16)]
    hTf = hT[:, :, :].rearrange("p a b -> p (a b)").bitcast(F32)
    xr = [hTf[:, 0:2048], hTf[:, 2048:4096]]
    fg = hTf[:, 4096:6144]
    sqj = hTf[:, 6144:8192]
    r_xr = [R("xr0"), R("xr1")]
    r_fg = R("fg")
    r_sqj = R("sqj")
    B.dma(fg, final_g.partition_broadcast(128), writes=[r_fg], stream="fg")
    for c in range(16):
        B.dma(wo[:, :, c * 128:(c + 1) * 128], w_out[:, c * 128:(c + 1) * 128].rearrange("(a p) c -> p a c", p=128),
              writes=[r_wo[c]], stream=f"wo{c}", q="pool")
    ym_v = ym_o.rearrange("(j k) d -> k j d", k=8)
    ttiles = [(xm_v[k], ym_v[k], 128, k * 128) for k in range(8)] + [(xs, ys_o, NS, S0)]
    for ti, (xsrc, ydst, rows, col0) in enumerate(ttiles):
        sl = ti % 2
        B.dma(xr[sl][:rows, :], xsrc, writes=[r_xr[sl]], stream=f"xld{sl}")
        for cb_ in range(4):
            bk = (ti % 2) * 4 + cb_
            for kt in range(16):
                B.pe(mm(pall[:rows, bk, :], merged[:, kt, col0:col0 + rows], wo[:, kt, cb_ * 512:(cb_ + 1) * 512],
                        kt == 0, kt == 15), reads=[r_mg[kt]] + r_wo[cb_ * 4:(cb_ + 1) * 4], writes=[r_pb[bk]])
            tt(xr[sl][:rows, cb_ * 512:(cb_ + 1) * 512], xr[sl][:rows, cb_ * 512:(cb_ + 1) * 512], pall[:rows, bk, :], ALU.add,
               [r_xr[sl], r_pb[bk]], [r_xr[sl]])
        si = ti % 32
        sq_ = stat[:rows, 2 * si:2 * si + 1]
        rs_ = stat[:rows, 2 * si + 1:2 * si + 2]
        r_s = r_stat[si]
        B.act(lambda e, sl=sl, rows=rows, sq_=sq_: e.activation(out=sqj[:rows, :], in_=xr[sl][:rows, :], func=AF.Square,
                                                               accum_out=sq_), reads=[r_xr[sl]], writes=[r_sqj, r_s])
        B.act(lambda e, sq_=sq_, rs_=rs_: e.activation(out=rs_, in_=sq_, func=AF.Sqrt, scale=1.0 / D, bias=EPS),
              reads=[r_s], writes=[r_s])
        B.dve(lambda e, rs_=rs_: e.reciprocal(out=rs_, in_=rs_), reads=[r_s], writes=[r_s])
        stt(xr[sl][:rows, :], xr[sl][:rows, :], rs_, fg[:rows, :], ALU.mult, ALU.mult, [r_xr[sl], r_s, r_fg], [r_xr[sl]])
        r_f = fence("dve", [r_xr[sl]])
        B.dma(ydst, xr[sl][:rows, :], reads=[r_xr[sl], r_f], stream="yout")

    return finish()


IN_NAMES = ["xm", "xp", "xs", "mem", "ck", "cv", "sconv", "ssr", "ssi"]


def core_inputs(inp, c):
    b, hb = c // 2, c % 2
    f = np.float32
    A = np.ascontiguousarray
    memr = inp["mem_prompt"][b]
    return {
        "xm": A(inp["x_prompt"][b, hb * NM:(hb + 1) * NM]),
        "xp": A(inp["x_prompt"][b, 0:NM]) if hb == 1 else np.zeros((NM, D), f),
        "xs": A(inp["x_sample"][c * 16:(c + 1) * 16].reshape(NS, D)),
        "mem": A(np.concatenate([memr[hb * 128:(hb + 1) * 128], memr[(1 - hb) * 128:(2 - hb) * 128]], 0)),
        "ck": A(inp["cache_mem_k"][0, c * 16:(c + 1) * 16].reshape(16, 256, 1024)),
        "cv": A(inp["cache_mem_v"][0, c * 16:(c + 1) * 16].reshape(16, 256, 1024)),
        "sconv": A(inp["state_conv"][0, c * 16:(c + 1) * 16]),
        "ssr": A(inp["state_ssm_re"][0, c * 16:(c + 1) * 16].reshape(16, 4096)),
        "ssi": A(inp["state_ssm_im"][0, c * 16:(c + 1) * 16].reshape(16, 4096)),
    }


def shared_inputs(inp):
    A = np.ascontiguousarray
    return {
        "norm_g": A(inp["norm_g"][0]), "mem_norm_g": A(inp["mem_norm_g"][0]), "final_g": A(inp["final_norm_g"]),
        "w_in": A(inp["w_in"][0]), "conv_w": A(inp["conv_w"][0]), "w_conv_out": A(inp["w_conv_out"][0]),
        "lam_re": A(inp["ssm_lambda_re"][0]), "lam_im": A(inp["ssm_lambda_im"][0]), "log_dt": A(inp["ssm_log_dt"][0]),
        "b_re": A(inp["ssm_b_re"][0]), "b_im": A(inp["ssm_b_im"][0]), "c_re": A(inp["ssm_c_re"][0]),
        "c_im": A(inp["ssm_c_im"][0]), "ssm_d": A(inp["ssm_d"][0]), "w_glu_a": A(inp["w_glu_a"][0]),
        "w_glu_b": A(inp["w_glu_b"][0]), "w_mem_k": A(inp["w_mem_k"][0]), "w_mem_v": A(inp["w_mem_v"][0]),
        "w_attn_out": A(inp["w_attn_out"][0]), "w_out": A(inp["w_out"][0]),
    }


_CACHE = {}


def run(inp, dbg=(), ncores=NCORE, stop=None, nloop=8):
    key = (tuple(sorted(dbg)), stop, nloop)
    if key not in _CACHE:
        _CACHE[key] = build(dbg, stop, nloop)
    B = _CACHE[key]
    sh = shared_inputs(inp)
    in_maps = []
    for c in range(ncores):
        m = dict(sh)
        m.update(core_inputs(inp, c))
        in_maps.append({k: v for k, v in m.items() if k in B.ins})
    res = run_bass_kernel_spmd(B.nc, in_maps, core_ids=list(range(ncores)))
    return res.results


def assemble(res):
    f = np.float32
    y_prompt = np.zeros((4, 2048, D), f)
    y_sample = np.zeros((128, 4, D), f)
    mk = np.zeros((1, 4, 256, 4, 256), f)
    mv = np.zeros((1, 4, 256, 4, 256), f)
    ncp = np.zeros((1, 4, 2, 1024), f)
    spr = np.zeros((1, 4, 64, 64), f)
    spi = np.zeros((1, 4, 64, 64), f)
    ncs = np.zeros((1, 128, 2, 1024), f)
    snr = np.zeros((1, 128, 64, 64), f)
    sni = np.zeros((1, 128, 64, 64), f)
    for c in range(NCORE):
        b, hb = c // 2, c % 2
        r = res[c]
        y_prompt[b, hb * NM:(hb + 1) * NM] = r["ym"]
        y_sample[c * 16:(c + 1) * 16] = r["ysm"].reshape(16, 4, D)
        mk[0, b, hb * 128:(hb + 1) * 128] = r["mk"].reshape(128, 4, 256)
        mv[0, b, hb * 128:(hb + 1) * 128] = r["mv"].reshape(128, 4, 256)
        if hb == 1:
            ncp[0, b] = r["ncp"]
            spr[0, b] = r["spr"].reshape(64, 64)
            spi[0, b] = r["spi"].reshape(64, 64)
        ncs[0, c * 16:(c + 1) * 16] = r["ncs"]
        snr[0, c * 16:(c + 1) * 16] = r["snr"].reshape(16, 64, 64)
        sni[0, c * 16:(c + 1) * 16] = r["sni"].reshape(16, 64, 64)
    return (y_prompt, y_sample, mk, mv, ncp, spr, spi, ncs, snr, sni)


def kernel(**inputs):
    inp = {k: np.asarray(v) for k, v in inputs.items()}
    res = run(inp)
    return assemble(res)
```

```python
import math
import numpy as np
import concourse.bass as bass
import concourse.mybir as mybir
from concourse.bass_utils import run_bass_kernel_spmd
from contextlib import ExitStack

F32 = mybir.dt.float32
BF16 = mybir.dt.bfloat16
I32 = mybir.dt.int32
AF = mybir.ActivationFunctionType
ALU = mybir.AluOpType

D = 2048
NIN = 14336
NCORE = 8
EPS = 1e-6
NM = 1024
NS = 64
NX = NM + NS + 2
S0 = NM
H0 = NM + NS
C_CB, C_CC, C_CH, C_CZ, C_SU, C_SZ, C_AQ, C_AZ, C_G = 0, 1024, 2048, 3072, 4096, 5120, 6144, 7168, 8192


class Res:
    __slots__ = ("w", "r", "name")

    def __init__(self, name=""):
        self.w = None
        self.r = []
        self.name = name


class Op:
    __slots__ = ("q", "fn", "deps", "flag", "val", "dma", "idx", "alldeps", "cost", "lat", "seq", "seg", "waits")


DEF_COST = {"pe": 0.15, "act": 0.8, "dve": 0.8, "pool": 1.0, "sp": 0.1}
RESCHED = True
KEEP_ORDER = ("pe",)
PE_FREE_SEGS = ()


class Sched:
    QS = ["pe", "act", "dve", "pool", "sp"]

    def __init__(self):
        self.all_ops = []
        self.streams = {}

    def op(self, q, fn, reads=(), writes=(), dma=None, after=(), cost=None, lat=0.0):
        o = Op()
        o.q = q
        o.fn = fn
        o.flag = False
        o.dma = dma
        o.val = 0
        o.idx = 0
        o.cost = cost if cost is not None else getattr(fn, "cost", None)
        if o.cost is None:
            o.cost = DEF_COST[q]
        o.lat = lat
        deps = []
        seen = set()

        def add(d):
            if d is None or id(d) in seen:
                return
            seen.add(id(d))
            deps.append(d)

        for r in reads:
            add(r.w)
        for w in writes:
            add(w.w)
            for x in w.r:
                add(x)
        for d in after:
            add(d)
        o.alldeps = deps
        for r in reads:
            r.r.append(o)
        for w in writes:
            w.w = o
            w.r = []
        if dma is not None:
            self.streams[dma] = True
        o.seq = len(self.all_ops)
        self.all_ops.append(o)
        return o

    def barrier(self):
        self.all_ops.append(None)

    def _schedule_segment(self, seg):
        import heapq
        inseg = {id(o) for o in seg}
        preds = {id(o): [d for d in o.alldeps if id(d) in inseg] for o in seg}
        last_stream = {}
        for o in seg:
            if o.dma is not None:
                p = last_stream.get(o.dma)
                if p is not None and all(id(p) != id(x) for x in preds[id(o)]):
                    preds[id(o)].append(p)
                last_stream[o.dma] = o
        last_q = {}
        for o in seg:
            if o.q in KEEP_ORDER and not (o.q == "pe" and self.cur_seg in PE_FREE_SEGS):
                p = last_q.get(o.q)
                if p is not None and all(id(p) != id(x) for x in preds[id(o)]):
                    preds[id(o)].append(p)
                last_q[o.q] = o
        if not RESCHED:
            out = {q: [] for q in self.QS}
            for o in seg:
                out[o.q].append(o)
            return out
        succ = {id(o): [] for o in seg}
        indeg = {}
        for o in seg:
            indeg[id(o)] = len(preds[id(o)])
            for d in preds[id(o)]:
                succ[id(d)].append(o)
        done = {}
        ready = {q: [] for q in self.QS}
        for o in seg:
            if indeg[id(o)] == 0:
                heapq.heappush(ready[o.q], (0.0, o.seq, o))
        qfree = {q: 0.0 for q in self.QS}
        out = {q: [] for q in self.QS}
        n = 0
        while n < len(seg):
            best = None
            for q in self.QS:
                hp = ready[q]
                if not hp:
                    continue
                cands = []
                while hp and hp[0][0] <= qfree[q]:
                    cands.append(heapq.heappop(hp))
                if cands:
                    c = min(cands, key=lambda x: x[1])
                    for x in cands:
                        if x is not c:
                            heapq.heappush(hp, x)
                    heapq.heappush(hp, c)
                    st = qfree[q]
                    key = (st, c[1])
                    pick = c
                else:
                    pick = hp[0]
                    st = pick[0]
                    key = (st, pick[1])
                if best is None or key < best[0]:
                    best = (key, q, pick, st)
            _, q, pick, st = best
            hp = ready[q]
            hp.remove(pick)
            heapq.heapify(hp)
            o = pick[2]
            out[q].append(o)
            qfree[q] = st + o.cost
            done[id(o)] = st + o.cost + o.lat
            n += 1
            for s_ in succ[id(o)]:
                indeg[id(s_)] -= 1
                if indeg[id(s_)] == 0:
                    rt = max(done[id(d)] for d in preds[id(s_)])
                    heapq.heappush(ready[s_.q], (rt, s_.seq, s_))
        self.est_time = getattr(self, "est_time", 0.0) + max(qfree.values())
        return out

    def finalize(self):
        self.queues = {q: [] for q in self.QS}
        segs = [[]]
        for o in self.all_ops:
            if o is None:
                segs.append([])
            else:
                segs[-1].append(o)
        for si, seg in enumerate(segs):
            for o in seg:
                o.seg = si
            self.cur_seg = si
            out = self._schedule_segment(seg)
            for q in self.QS:
                self.queues[q].extend(out[q])
            if si < len(segs) - 1:
                lasts = []
                for q in self.QS:
                    for o in reversed(self.queues[q]):
                        if o.dma is None and o.fn is not None:
                            lasts.append(o)
                            break
                lastdma = {}
                for q in self.QS:
                    for o in self.queues[q]:
                        if o.dma is not None:
                            lastdma[o.dma] = o
                lasts += list(lastdma.values())
                for q in self.QS:
                    b = Op()
                    b.q = q
                    b.fn = None
                    b.flag = False
                    b.dma = None
                    b.val = 0
                    b.idx = 0
                    b.seg = -1
                    b.alldeps = lasts
                    self.queues[q].append(b)
        cnt = {}
        for q in self.QS:
            for o in self.queues[q]:
                if o.dma is not None:
                    cnt[o.dma] = cnt.get(o.dma, 0) + 1
                    o.idx = cnt[o.dma]
        pos = {}
        for q in self.QS:
            for i_, o in enumerate(self.queues[q]):
                pos[id(o)] = i_
        for q in self.QS:
            for o in self.queues[q]:
                best = {}
                for d in o.alldeps:
                    if o.seg != -1 and d.seg != o.seg:
                        continue
                    if d.dma is None and d.q == "pe" and q == "pe":
                        continue
                    key = d.q if d.dma is None else ("dma", d.dma)
                    if key not in best or pos[id(d)] > pos[id(best[key])]:
                        best[key] = d
                o.waits = list(best.values())
                for d in o.waits:
                    if d.dma is None:
                        d.flag = True
        for q in self.QS:
            c = 0
            for o in self.queues[q]:
                if o.dma is None and o.flag:
                    c += 1
                    o.val = c

    def emit(self, nc, stack):
        self.finalize()
        qsem = {q: stack.enter_context(nc.semaphore("q_" + q)) for q in self.QS}
        ssem = {s: stack.enter_context(nc.semaphore("s_" + s)) for s in self.streams}
        block = stack.enter_context(nc.Block())

        def tok(d):
            if d.dma is not None:
                return ssem[d.dma], 16 * d.idx
            return qsem[d.q], d.val

        def run(q, eng):
            waited = {}
            for o in self.queues[q]:
                for d in o.waits:
                    sem, val = tok(d)
                    if waited.get(id(sem), 0) >= val:
                        continue
                    eng.wait_ge(sem, val)
                    waited[id(sem)] = val
                if o.fn is None:
                    continue
                ins = o.fn(eng)
                if o.dma is not None:
                    ins.then_inc(ssem[o.dma], 16)
                elif o.flag:
                    ins.then_inc(qsem[q], 1)
            fin = {}
            for o in self.queues[q]:
                if o.dma is not None:
                    fin[o.dma] = max(fin.get(o.dma, 0), o.idx)
            for s, n in fin.items():
                eng.wait_ge(ssem[s], 16 * n)

        @block.tensor
        def _(e):
            run("pe", e)

        @block.scalar
        def _(e):
            run("act", e)

        @block.vector
        def _(e):
            run("dve", e)

        @block.gpsimd
        def _(e):
            run("pool", e)

        @block.sync
        def _(e):
            run("sp", e)


class Builder:
    def __init__(self, dbg=()):
        self.dbg = set(dbg)
        self.nc = bass.Bass("TRN2", target_bir_lowering=False)
        self.S = Sched()
        self.stack = ExitStack()
        self.ins = {}
        self.outs = {}
        self.rr = 0

    def din(self, name, shape, dt=F32):
        t = self.nc.dram_tensor(name, list(shape), dt, kind="ExternalInput").ap()
        self.ins[name] = t
        return t

    def dout(self, name, shape, dt=F32):
        t = self.nc.dram_tensor(name, list(shape), dt, kind="ExternalOutput").ap()
        self.outs[name] = t
        return t

    def sb(self, name, shape, dt=F32):
        return self.stack.enter_context(self.nc.sbuf_tensor(name, list(shape), dt))

    def ps(self, name, shape, dt=F32):
        return self.stack.enter_context(self.nc.psum_tensor(name, list(shape), dt))

    def pe(self, fn, reads=(), writes=(), drain=False):
        after = [self.last_pe] if (drain and getattr(self, "last_pe", None) is not None) else []
        o = self.S.op("pe", fn, reads, writes, after=after)
        self.last_pe = o
        return o

    def act(self, fn, reads=(), writes=()):
        return self.S.op("act", fn, reads, writes)

    def dve(self, fn, reads=(), writes=()):
        return self.S.op("dve", fn, reads, writes)

    def pool(self, fn, reads=(), writes=()):
        return self.S.op("pool", fn, reads, writes)

    def alt(self, fn, reads=(), writes=()):
        self.rr ^= 1
        return self.S.op("act" if self.rr else "dve", fn, reads, writes)

    def dma(self, out, in_, reads=(), writes=(), stream="ld", q="sp", **kw):
        n = 1
        for x in out.shape:
            n *= x
        lat = 2.5 + n * 4 / 200e3
        cost = 1.0 if q == "pool" else 0.15
        if kw.get("allow_slow_non_contiguous"):
            lat += n * 0.004
        return self.S.op(q, lambda e: e.dma_start(out=out, in_=in_, **kw), reads, writes, dma=stream, cost=cost, lat=lat)

    def dump(self, name, ap_sb, res, shape, dt=F32):
        if name not in self.dbg:
            return
        o = self.dout("dbg_" + name, shape, dt)
        self.dma(o, ap_sb, reads=[res], stream="dbg")


class Arena:
    def __init__(self, B, words):
        self.t = B.sb("arena", [128, words], F32)
        self.words = words
        self.top = 0

    def alloc(self, shape, dt=F32):
        n = 1
        for x in shape:
            n *= x
        w = n if dt == F32 else (n + 1) // 2
        w = (w + 7) // 8 * 8
        off = self.top
        self.top += w
        assert self.top <= self.words, f"arena overflow {self.top} > {self.words}"
        v = self.t[:, off:off + w]
        if dt != F32:
            v = v.bitcast(dt)
        v = v[:, 0:n]
        if len(shape) == 2:
            v = v.rearrange("p (a b) -> p a b", a=shape[0])
        elif len(shape) == 3:
            v = v.rearrange("p (a b c) -> p a b c", a=shape[0], b=shape[1])
        elif len(shape) == 4:
            v = v.rearrange("p (a b c d) -> p a b c d", a=shape[0], b=shape[1], c=shape[2])
        return v


def cp(e, out, in_):
    if hasattr(e, "tensor_copy"):
        return e.tensor_copy(out=out, in_=in_)
    return e.activation(out=out, in_=in_, func=AF.Copy)


def mm(out, lhsT, rhs, start, stop):
    f = lambda e: e.matmul(out, lhsT, rhs, start=start, stop=stop)
    n = rhs.shape[-1]
    f.cost = 0.03 + max(n, 64) * (4 if rhs.dtype == F32 else 1) / 2000.0
    return f


MAGIC = 12582912.0
TWO_PI = 2.0 * math.pi
PI_S = 3.1415925
BIGW = 8704
ARENA_W = 23300
NJ = 272
SCR1_W = 2112
YW = 144


def build(dbg=(), stop=None, nloop=8):
    B = Builder(dbg)

    def finish():
        B.S.emit(B.nc, B.stack)
        B.stack.close()
        return B

    nc = B.nc
    R = Res
    xm = B.din("xm", [NM, D])
    xp = B.din("xp", [NM, D])
    xs = B.din("xs", [NS, D])
    mem = B.din("mem", [256, D])
    ck = B.din("ck", [16, 256, 1024])
    cv = B.din("cv", [16, 256, 1024])
    sconv = B.din("sconv", [16, 2, 1024])
    ssr = B.din("ssr", [16, 4096])
    ssi = B.din("ssi", [16, 4096])
    norm_g = B.din("norm_g", [D])
    mem_norm_g = B.din("mem_norm_g", [D])
    final_g = B.din("final_g", [D])
    w_in = B.din("w_in", [D, NIN])
    conv_w = B.din("conv_w", [3, 1024])
    w_conv_out = B.din("w_conv_out", [1024, D])
    lam_re = B.din("lam_re", [64, 64])
    lam_im = B.din("lam_im", [64, 64])
    log_dt = B.din("log_dt", [64])
    b_re = B.din("b_re", [64, 64, 16])
    b_im = B.din("b_im", [64, 64, 16])
    c_re = B.din("c_re", [64, 16, 64])
    c_im = B.din("c_im", [64, 16, 64])
    ssm_d = B.din("ssm_d", [1024])
    w_glu_a = B.din("w_glu_a", [1024, D])
    w_glu_b = B.din("w_glu_b", [1024, D])
    w_mem_k = B.din("w_mem_k", [D, 1024])
    w_mem_v = B.din("w_mem_v", [D, 1024])
    w_attn_out = B.din("w_attn_out", [1024, D])
    w_out = B.din("w_out", [D, D])

    ym_o = B.dout("ym", [NM, D])
    ys_o = B.dout("ysm", [NS, D])
    mk_o = B.dout("mk", [128, 1024])
    mv_o = B.dout("mv", [128, 1024])
    ncp_o = B.dout("ncp", [2, 1024])
    spr_o = B.dout("spr", [4096])
    spi_o = B.dout("spi", [4096])
    ncs_o = B.dout("ncs", [16, 2, 1024])
    snr_o = B.dout("snr", [16, 4096])
    sni_o = B.dout("sni", [16, 4096])

    scr1 = [nc.dram_tensor(f"scr1_{i}", [128, SCR1_W], BF16).ap() for i in range(2)]
    scr2 = [nc.dram_tensor(f"scr2_{i}", [128, 8 * YW], F32).ap() for i in range(2)]

    def DAP(t, off, pat):
        return bass.AP(t.tensor, off, pat)

    hT = B.sb("hT", [128, 16, NX], BF16)
    BIG = B.sb("BIG", [128, BIGW], F32)
    wbs = [B.sb(f"wb{i}", [128, 2048], BF16) for i in range(3)]
    NWB = 3
    identb = B.sb("identb", [128, 128], BF16)
    identf = B.sb("identf", [128, 128], F32)
    onesb = B.sb("onesb", [128, 128], BF16)
    cact = B.sb("cact", [128, 8, 1088], BF16)
    ysT = B.sb("ysT", [128, 8, 1088], BF16)
    stat = B.sb("stat", [128, 64], F32)
    pall = B.ps("pall", [128, 8, 512], F32)
    AR = Arena(B, ARENA_W)

    hTp = BIG[:, 0:8192].bitcast(BF16).rearrange("p (a b) -> p a b", a=16)
    merged = BIG[:, 0:BIGW].bitcast(BF16).rearrange("p (a b) -> p a b", a=16)

    r_pb = [R(f"pb{i}") for i in range(8)]
    r_hT = [R(f"hT{i}") for i in range(16)]
    r_ident = R("ident")
    r_wb = [R(f"wb{i}") for i in range(3)]
    r_stat = [R(f"st{i}") for i in range(32)]
    r_cact = [R(f"cact{i}") for i in range(8)]
    r_ys = [R(f"ys{i}") for i in range(8)]

    fctr = [0]

    def fence(q, reads):
        r = R("fence")
        i = fctr[0] % 2
        fctr[0] += 1
        if q == "act":
            B.act(lambda e: e.activation(out=stat[0:1, 56 + i:57 + i], in_=stat[0:1, 58:59], func=AF.Copy), reads=reads,
                  writes=[r])
        else:
            B.dve(lambda e: e.tensor_copy(out=stat[0:1, 60 + i:61 + i], in_=stat[0:1, 59:60]), reads=reads, writes=[r])
        return r

    def bank(b, n=512):
        return pall[:, b, 0:n]

    def pset(s):
        return pall[:, 3 * s:3 * s + 3, :].rearrange("p b c -> p (b c)")

    B.pool(lambda e: e.memset(identf[:], 0.0), writes=[r_ident])
    B.pool(lambda e: e.affine_select(out=identf[:], in_=identf[:], pattern=[[-1, 128]],
                                     compare_op=ALU.not_equal, fill=1.0, base=0,
                                     channel_multiplier=1), reads=[r_ident], writes=[r_ident])
    B.dve(lambda e: e.tensor_copy(out=identb[:], in_=identf[:]), reads=[r_ident], writes=[r_ident])
    B.dve(lambda e: e.memset(onesb[:], 1.0), writes=[r_ident])
    B.dve(lambda e: e.memset(stat[:], 0.0), writes=r_stat)

    wcount = [0]

    def load_w(parts, kt):
        assert len(parts) == 1
        p = parts[0]
        sl = wcount[0] % NWB
        wcount[0] += 1
        assert p.shape[1] == 128 and kt * 128 <= 2048
        v = wbs[sl][:, 0:kt * 128].rearrange("p (a c) -> p a c", a=kt)
        B.dma(v, p.rearrange("(a p) c -> p a c", p=128), writes=[r_wb[sl]], stream=f"w{sl}", q="pool")
        return v, r_wb[sl]

    setctr = [0]

    def project(wv, r_w, col, nkt, act, r_act, segs, s=None):
        if s is None:
            s = setctr[0] % 2
            setctr[0] += 1
        for kt in range(nkt):
            for (c0, n, bi) in segs:
                bk = 3 * s + bi
                B.pe(mm(pall[:, bk, 0:n], wv[:, kt, col:col + 128], act[:, kt, c0:c0 + n], kt == 0, kt == nkt - 1),
                     reads=[r_w, r_act[kt]], writes=[r_pb[bk]])
        return s

    SEG_MS = [(0, 512, 0), (512, 512, 1), (1024, NS, 2)]
    SEG_MSH = [(0, 512, 0), (512, 512, 1), (1024, NS + 2, 2)]

    def rset(s):
        return [r_pb[3 * s], r_pb[3 * s + 1], r_pb[3 * s + 2]]

    def norm_transpose(tiles, gsrc, AR_):
        xst = [AR_.alloc([D]) for _ in range(2)]
        xbf = [AR_.alloc([D], BF16) for _ in range(2)]
        junk = AR_.alloc([D], BF16)
        gbc = AR_.alloc([D])
        r_xst = [R(), R()]
        r_xbf = [R(), R()]
        r_junk = R()
        r_gbc = R()
        B.dma(gbc, gsrc.partition_broadcast(128), writes=[r_gbc], stream="gbc")
        for ti, (src, rows, dstT, r_dst, col0) in enumerate(tiles):
            sl = ti % 2
            xt, xb = xst[sl], xbf[sl]
            B.dma(xt[:rows, :], src, writes=[r_xst[sl]], stream=f"xld{sl}")
            si = ti % 32
            sq = stat[:rows, 2 * si:2 * si + 1]
            rs = stat[:rows, 2 * si + 1:2 * si + 2]
            r_s = r_stat[si]
            B.act(lambda e, xt=xt, rows=rows, sq=sq: e.activation(out=junk[:rows, :], in_=xt[:rows, :], func=AF.Square,
                                                                 accum_out=sq),
                  reads=[r_xst[sl]], writes=[r_junk, r_s])
            B.act(lambda e, sq=sq, rs=rs: e.activation(out=rs, in_=sq, func=AF.Sqrt, scale=1.0 / D, bias=EPS),
                  reads=[r_s], writes=[r_s])
            B.dve(lambda e, rs=rs: e.reciprocal(out=rs, in_=rs), reads=[r_s], writes=[r_s])
            B.dve(lambda e, xt=xt, xb=xb, rows=rows, rs=rs: e.scalar_tensor_tensor(
                out=xb[:rows, :], in0=xt[:rows, :], scalar=rs, in1=gbc[:rows, :], op0=ALU.mult, op1=ALU.mult),
                reads=[r_xst[sl], r_s, r_gbc], writes=[r_xbf[sl]])
            for g4 in range(4):
                pbk = 6 + (ti * 4 + g4) % 2
                pv = pall[:, pbk, :].bitcast(BF16)
                for i in range(4):
                    dt_ = g4 * 4 + i
                    B.pe(lambda e, pv=pv, xb=xb, rows=rows, dt_=dt_, i=i: e.transpose(
                        out=pv[:, i * 128:i * 128 + rows], in_=xb[:rows, dt_ * 128:(dt_ + 1) * 128],
                        identity=identb[:rows, :rows]),
                        reads=[r_xbf[sl], r_ident], writes=[r_pb[pbk]])
                B.alt(lambda e, pv=pv, dstT=dstT, g4=g4, rows=rows, col0=col0: cp(
                    e, dstT[:, g4 * 4:(g4 + 1) * 4, col0:col0 + rows],
                    pv[:, 0:512].rearrange("p (a b) -> p a b", a=4)[:, :, 0:rows]),
                    reads=[r_pb[pbk]], writes=r_dst[g4 * 4:(g4 + 1) * 4])

    suTp = AR.alloc([8, 1024], BF16)
    r_suTp = [R(f"suTp{i}") for i in range(8)]
    names = ["lamr", "lami", "dtl", "dt", "lrdt", "mag", "imag", "ang", "angw", "sn", "cs", "ar", "ai", "air", "aii",
             "den", "rden", "am1", "fr", "fi", "tA", "tB", "phi8", "p8s", "r8", "tC"]
    P = {n: AR.alloc([32]) for n in names}
    rP = {n: R(n) for n in names}
    Ppr = AR.alloc([17, 32])
    Ppi = AR.alloc([17, 32])
    Pmr = AR.alloc([8, 32])
    Pmi = AR.alloc([8, 32])
    r_Pp = R("Pp")
    r_Pm = R("Pm")
    Bbr = AR.alloc([32, 16])
    Bbi = AR.alloc([32, 16])
    cnr = AR.alloc([32, 16])
    cni = AR.alloc([32, 16])
    r_Bb = R("Bb")
    r_cn = R("cn")
    Dcol = AR.alloc([64])
    cw = AR.alloc([8, 3])
    jtab = AR.alloc([256])
    mask1 = AR.alloc([128])
    r_const = R("const")
    sconvT = AR.alloc([8, 16, 2])
    ncp = AR.alloc([8, 2])
    ncs = AR.alloc([8, 16, 2])
    r_sconvT = R("sconvT")
    r_ncp = R("ncp")
    r_ncs = R("ncs")
    SinTr = AR.alloc([32, 16])
    SinTi = AR.alloc([32, 16])
    Spr = AR.alloc([32, 16], BF16)
    Spi = AR.alloc([32, 16], BF16)
    ZSr = AR.alloc([32, 16])
    ZSi = AR.alloc([32, 16])
    SFr = AR.alloc([32])
    SFi = AR.alloc([32])
    r_SinT = R("SinT")
    r_Sp = R("Sp")
    r_ZS = [R(f"ZS{i}") for i in range(8)]
    r_SF = [R(f"SF{i}") for i in range(8)]
    ld_state = {"stream": "par0", "ops": [], "res": []}

    def ld(dst, src, res, q="act", **kw):
        o = B.dma(dst, src, writes=[res], stream=ld_state["stream"], q=q, **kw)
        ld_state["ops"].append(o)
        ld_state["res"].append(res)

    def ld_group_done():
        last = ld_state["ops"][-1]
        for r_ in ld_state["res"]:
            r_.w = last
        ld_state["ops"], ld_state["res"] = [], []

    nce = dict(allow_slow_non_contiguous=True)
    stf = ysT[:, :, :].rearrange("p a b -> p (a b)").bitcast(F32)
    stc = cact[:, :, :].rearrange("p a b -> p (a b)").bitcast(F32)
    r_stf = R("stf")
    r_stc = R("stc")
    ld_state["stream"] = "par0"
    q0 = "sp"
    ld(stf[0:32, 0:128], lam_re.rearrange("(a b) p -> a (b p)", b=2), r_stf, q=q0)
    ld(stf[0:32, 128:256], lam_im.rearrange("(a b) p -> a (b p)", b=2), r_stf, q=q0)
    ld(stf[0:32, 384:386], log_dt.rearrange("(a b) -> a b", b=2), r_stf, q=q0)
    ld(stf[0:64, 512:528], ssm_d.rearrange("(g c) -> g c", c=16), r_stf, q=q0)
    for i in range(3):
        ld(cw[:, :, i], DAP(conv_w, i * 1024, [[1, 128], [128, 8]]), r_const, q=q0, **nce)
    ld(stf[0:32, 2048:3072], sconv.rearrange("s r c -> (s r) c"), r_stf, q=q0)
    for (src_, base) in ((c_re, 0), (c_im, 1024)):
        st3 = stc[0:64, base:base + 1024].rearrange("p (t x) -> p t x", t=8)
        for t in range(8):
            for gpl in range(4):
                ld(st3[gpl * 16:(gpl + 1) * 16, t, :].rearrange("p (a b) -> p a b", a=2),
                   DAP(src_, (8 * t + 2 * gpl) * 1024, [[64, 16], [1024, 2], [1, 64]]), r_stc, q=q0)
    ld_group_done()
    B.dve(lambda e: e.tensor_copy(out=stf[0:32, 256:384].rearrange("p (a b) -> p a b", a=2),
                                  in_=stf[0:32, 384:386].unsqueeze(2).to_broadcast([32, 2, 64])), reads=[r_stf], writes=[r_stf])
    B.dve(lambda e: e.tensor_copy(out=stf[0:64, 640:768].rearrange("p (a b) -> p a b", a=8),
                                  in_=stf[0:64, 512:528].unsqueeze(1).to_broadcast([64, 8, 16])), reads=[r_stf], writes=[r_stf])

    def ptr(out_ps, in_sb, rows, res_in, bk):
        B.pe(lambda e: e.transpose(out=out_ps, in_=in_sb, identity=identf[0:rows, 0:rows]), reads=[res_in, r_ident],
             writes=[r_pb[bk]])

    for i, nm in enumerate(("lamr", "lami", "dtl")):
        ptr(pall[:, 6, i * 32:(i + 1) * 32], stf[0:32, i * 128:(i + 1) * 128], 32, r_stf, 6)
    ptr(pall[:, 6, 128:192], stf[0:64, 640:768], 64, r_stf, 6)
    for i, nm in enumerate(("lamr", "lami", "dtl")):
        B.alt(lambda e, i=i, nm=nm: cp(e, P[nm], pall[:, 6, i * 32:(i + 1) * 32]), reads=[r_pb[6]], writes=[rP[nm]])
    B.alt(lambda e: cp(e, Dcol, pall[:, 6, 128:192]), reads=[r_pb[6]], writes=[r_const])
    if stop == "e1":
        return finish()
    for t in range(8):
        ptr(pall[:, 7, 64 + t * 32:64 + (t + 1) * 32], stf[0:32, 2048 + t * 128:2048 + (t + 1) * 128], 32, r_stf, 7)
    B.alt(lambda e: cp(e, sconvT.rearrange("p t s r -> p t (s r)"), pall[:, 7, 64:320].rearrange("p (a b) -> p a b", a=8)),
          reads=[r_pb[7]], writes=[r_sconvT])
    if stop == "e2":
        return finish()
    for (dstC, base, bk) in ((cnr, 0, 6), (cni, 1024, 7)):
        for t in range(8):
            ptr(pall[:, bk, t * 64:(t + 1) * 64], stc[0:64, base + t * 128:base + (t + 1) * 128], 64, r_stc, bk)
        B.alt(lambda e, dstC=dstC, bk=bk: cp(e, dstC.rearrange("p a b -> p (a b)"), pall[:, bk, :]), reads=[r_pb[bk]],
              writes=[r_cn])
    if stop == "e3":
        return finish()
    p1_top = AR.top
    r_hTp = [R(f"hTp{i}") for i in range(16)]
    xm_v = xm.rearrange("(j k) d -> k j d", k=8)
    xp_v = xp.rearrange("(j k) d -> k j d", k=8)
    tiles = []
    for k in range(8):
        tiles.append((xp_v[k], 128, hTp, r_hTp, k * 128))
    for k in range(8):
        tiles.append((xm_v[k], 128, hT, r_hT, k * 128))
    tiles.append((xs, NS, hT, r_hT, S0))
    norm_transpose(tiles, norm_g, AR)
    B.dve(lambda e: e.tensor_copy(out=hT[:, :, H0:H0 + 2], in_=hTp[:, :, 6 * 128 + 127:8 * 128:128]),
          reads=r_hTp, writes=r_hT)
    for t in range(8):
        wv, r_w = load_w([w_in[:, C_SU + t * 128:C_SU + (t + 1) * 128]], 16)
        s = project(wv, r_w, 0, 16, hTp, r_hTp, [(0, 512, 0), (512, 512, 1)])
        B.alt(lambda e, s=s, t=t: cp(e, suTp[:, t, :], pset(s)[:, 0:1024]), reads=rset(s)[0:2], writes=[r_suTp[t]])
    B.S.barrier()
    if stop == "p1":
        return finish()

    AR.top = p1_top
    BT = [0]

    def balloc(shape, dt=F32):
        n = 1
        for x in shape:
            n *= x
        w = n if dt == F32 else (n + 1) // 2
        w = (w + 7) // 8 * 8
        off = BT[0]
        BT[0] += w
        assert BT[0] <= BIGW, "BIG overflow"
        v = BIG[:, off:off + w]
        if dt != F32:
            v = v.bitcast(dt)
        v = v[:, 0:n]
        if len(shape) == 2:
            v = v.rearrange("p (a b) -> p a b", a=shape[0])
        elif len(shape) == 3:
            v = v.rearrange("p (a b c) -> p a b c", a=shape[0], b=shape[1])
        return v

    WTr = [AR.alloc([8, 64], BF16) for _ in range(2)]
    WTi = [AR.alloc([8, 64], BF16) for _ in range(2)]
    CS2r = [AR.alloc([8, 128], BF16) for _ in range(2)]
    CS2i = [AR.alloc([8, 128], BF16) for _ in range(2)]
    Mg = [AR.alloc([8, 128], BF16) for _ in range(2)]
    r_WT = [R("WT0"), R("WT1")]
    r_CS2 = [R("CS20"), R("CS21")]
    r_Mg = [R("Mg0"), R("Mg1")]
    Wnr = AR.alloc([128])
    Wni = AR.alloc([128])
    CSnr = AR.alloc([128])
    CSni = AR.alloc([128])
    gt1 = AR.alloc([128])
    gt2 = AR.alloc([128])
    T2r = AR.alloc([128])
    T2i = AR.alloc([128])
    CSzr = AR.alloc([128])
    CSzi = AR.alloc([128])
    hm = AR.alloc([2])
    r_T2 = R("T2")
    r_CSz = R("CSz")
    r_Wn = R("Wn")
    r_CSn = R("CSn")
    r_gt = R("gt")
    rcos = AR.alloc([256])
    rsin = AR.alloc([256])
    rt1 = AR.alloc([256])
    rt2 = AR.alloc([256])
    rzr = AR.alloc([256])
    rzi = AR.alloc([256])
    r_rot = R("rot")
    suTb = AR.alloc([1088], BF16)
    r_suTb = R("suTb")
    r_suTb2 = R("suTb2")
    Ub = [AR.alloc([8, NJ], BF16) for _ in range(2)]
    r_Ub = [[R(f"Ub{a}_{i}") for i in range(20)] for a in range(2)]
    r_scr1 = [[R(), R()] for _ in range(2)]
    r_scr2 = [[R() for _ in range(12)] for _ in range(2)]
    Spvr = [AR.alloc([4, 132], BF16) for _ in range(2)]
    Spvi = [AR.alloc([4, 132], BF16) for _ in range(2)]
    r_Spv = [R("Spv0"), R("Spv1")]
    r_Ysb = R("Ysb")
    print("arena top after S5 allocs", AR.top, "of", ARENA_W)
    Ysb = balloc([8, YW])
    tmp_cc = balloc([NX])
    vext = balloc([10, 128])
    vs = balloc([16, 6])
    vh = balloc([2])
    acc = balloc([1088])
    tmp_s = balloc([1088])
    bnr = balloc([32, 16])
    bni = balloc([32, 16])
    r_tmpcc = R("tmpcc")
    r_vext = R("vext")
    r_acc = R("acc")
    r_acc2 = R("acc2")
    r_tmps = R("tmps")
    r_bn = R("bn")
    print("BIG top", BT[0], "of", BIGW)

    ld_state["stream"] = "par1"
    ld(bnr, DAP(b_re, 0, [[16, 128], [2048, 32], [1, 16]]), r_bn)
    ld(bni, DAP(b_im, 0, [[16, 128], [2048, 32], [1, 16]]), r_bn)
    ld_group_done()
    for q4 in range(4):
        B.dma(acc[0:16, 0:1024], ssr[:, q4 * 1024:(q4 + 1) * 1024], writes=[r_acc], stream="sin0")
        B.dma(tmp_s[0:16, 0:1024], ssi[:, q4 * 1024:(q4 + 1) * 1024], writes=[r_tmps], stream="sin1")
        for gpl in range(8):
            B.pe(lambda e, gpl=gpl: e.transpose(out=pall[:, 6, gpl * 16:(gpl + 1) * 16], in_=acc[0:16, gpl * 128:(gpl + 1) * 128],
                                                identity=identf[0:16, 0:16]), reads=[r_acc, r_ident], writes=[r_pb[6]])
            B.pe(lambda e, gpl=gpl: e.transpose(out=pall[:, 7, gpl * 16:(gpl + 1) * 16], in_=tmp_s[0:16, gpl * 128:(gpl + 1) * 128],
                                                identity=identf[0:16, 0:16]), reads=[r_tmps, r_ident], writes=[r_pb[7]])
        B.act(lambda e, q4=q4: cp(e, SinTr[:, q4 * 8:(q4 + 1) * 8, :].rearrange("p a b -> p (a b)"), pall[:, 6, 0:128]),
              reads=[r_pb[6]], writes=[r_SinT])
        B.dve(lambda e, q4=q4: cp(e, SinTi[:, q4 * 8:(q4 + 1) * 8, :].rearrange("p a b -> p (a b)"), pall[:, 7, 0:128]),
              reads=[r_pb[7]], writes=[r_SinT])
    B.pool(lambda e: e.iota(jtab, [[1, 256]], base=0, channel_multiplier=0, allow_small_or_imprecise_dtypes=True),
           writes=[r_const])
    B.pool(lambda e: e.memset(hm[0:64, 0:1], 1.0), writes=[r_const])
    B.pool(lambda e: e.memset(hm[64:128, 0:1], 0.0), writes=[r_const])
    B.pool(lambda e: e.memset(hm[0:64, 1:2], 0.0), writes=[r_const])
    B.pool(lambda e: e.memset(hm[64:128, 1:2], 1.0), writes=[r_const])
    for sl_ in range(2):
        B.pool(lambda e, sl_=sl_: e.memset(Ub[sl_][64:128, :, 256:272], 0.0), writes=[r_Ub[sl_][19]])
    B.pool(lambda e: e.memset(mask1, 1.0), writes=[r_const])
    B.pool(lambda e: e.affine_select(out=mask1, in_=mask1, pattern=[[16, 8], [0, 16]],
                                     compare_op=ALU.is_ge, fill=0.0, base=15, channel_multiplier=-1),
           reads=[r_const], writes=[r_const])

    def tt(out, a, b, op, reads, writes):
        B.dve(lambda e: e.tensor_tensor(out=out, in0=a, in1=b, op=op), reads=reads, writes=writes)

    def ts(out, a, s1, s2, op0, op1, reads, writes):
        if s2 is None:
            B.dve(lambda e: e.tensor_scalar(out=out, in0=a, scalar1=s1, scalar2=None, op0=op0), reads=reads, writes=writes)
        else:
            B.dve(lambda e: e.tensor_scalar(out=out, in0=a, scalar1=s1, scalar2=s2, op0=op0, op1=op1), reads=reads,
                  writes=writes)

    def stt(out, a, s, b, op0, op1, reads, writes):
        B.dve(lambda e: e.scalar_tensor_tensor(out=out, in0=a, scalar=s, in1=b, op0=op0, op1=op1), reads=reads,
              writes=writes)

    def actf(out, a, func, reads, writes, scale=1.0, bias=0.0):
        B.act(lambda e: e.activation(out=out, in_=a, func=func, scale=scale, bias=bias), reads=reads, writes=writes)

    def pp(n):
        return P[n], rP[n]

    def ptt(o, a, b, op):
        tt(P[o], P[a], P[b], op, [rP[a], rP[b]], [rP[o]])

    def wrap(o, a):
        ts(P["tC"], P[a], 1.0 / TWO_PI, MAGIC, ALU.mult, ALU.add, [rP[a]], [rP["tC"]])
        ts(P["tC"], P["tC"], MAGIC, None, ALU.subtract, None, [rP["tC"]], [rP["tC"]])
        stt(P[o], P["tC"], -TWO_PI, P[a], ALU.mult, ALU.add, [rP["tC"], rP[a]], [rP[o]])
        ts(P[o], P[o], -PI_S, PI_S, ALU.max, ALU.min, [rP[o]], [rP[o]])

    actf(P["dt"], P["dtl"], AF.Exp, [rP["dtl"]], [rP["dt"]])
    ptt("lrdt", "lamr", "dt", ALU.mult)
    actf(P["mag"], P["lrdt"], AF.Exp, [rP["lrdt"]], [rP["mag"]])
    actf(P["imag"], P["lrdt"], AF.Exp, [rP["lrdt"]], [rP["imag"]], scale=-1.0)
    actf(P["r8"], P["lrdt"], AF.Exp, [rP["lrdt"]], [rP["r8"]], scale=8.0)
    ptt("ang", "lami", "dt", ALU.mult)
    wrap("angw", "ang")
    actf(P["sn"], P["angw"], AF.Sin, [rP["angw"]], [rP["sn"]])
    actf(P["tA"], P["angw"], AF.Abs, [rP["angw"]], [rP["tA"]])
    actf(P["cs"], P["tA"], AF.Sin, [rP["tA"]], [rP["cs"]], scale=-1.0, bias=math.pi / 2)
    ptt("ar", "mag", "cs", ALU.mult)
    ptt("ai", "mag", "sn", ALU.mult)
    ptt("air", "imag", "cs", ALU.mult)
    stt(P["aii"], P["imag"], -1.0, P["sn"], ALU.mult, ALU.mult, [rP["imag"], rP["sn"]], [rP["aii"]])
    ts(P["tB"], P["angw"], 8.0, None, ALU.mult, None, [rP["angw"]], [rP["tB"]])
    wrap("phi8", "tB")
    ts(P["p8s"], P["phi8"], 1.0 / TWO_PI, None, ALU.mult, None, [rP["phi8"]], [rP["p8s"]])
    ptt("den", "lamr", "lamr", ALU.mult)
    ptt("tA", "lami", "lami", ALU.mult)
    ptt("den", "den", "tA", ALU.add)
    B.dve(lambda e: e.reciprocal(out=P["rden"], in_=P["den"]), reads=[rP["den"]], writes=[rP["rden"]])
    ts(P["am1"], P["ar"], -1.0, None, ALU.add, None, [rP["ar"]], [rP["am1"]])
    ptt("tA", "am1", "lamr", ALU.mult)
    ptt("tB", "ai", "lami", ALU.mult)
    ptt("tA", "tA", "tB", ALU.add)
    ptt("fr", "tA", "rden", ALU.mult)
    ptt("tA", "ai", "lamr", ALU.mult)
    ptt("tB", "am1", "lami", ALU.mult)
    ptt("tA", "tA", "tB", ALU.subtract)
    ptt("fi", "tA", "rden", ALU.mult)
    frb = P["fr"].unsqueeze(2).to_broadcast([128, 32, 16])
    fib = P["fi"].unsqueeze(2).to_broadcast([128, 32, 16])
    t16a = balloc([32, 16])
    t16b = balloc([32, 16])
    r_t16 = R("t16")
    tt(t16a, bnr, frb, ALU.mult, [r_bn, rP["fr"]], [r_t16])
    tt(t16b, bni, fib, ALU.mult, [r_bn, rP["fi"]], [r_t16])
    tt(Bbr, t16a, t16b, ALU.subtract, [r_t16], [r_Bb])
    tt(t16a, bni, frb, ALU.mult, [r_bn, rP["fr"]], [r_t16])
    tt(t16b, bnr, fib, ALU.mult, [r_bn, rP["fi"]], [r_t16])
    tt(Bbi, t16a, t16b, ALU.add, [r_t16], [r_Bb])

    def powers(Pr, Pi, r_P, nk, br, bi):
        B.dve(lambda e: e.memset(Pr[:, 0, :], 1.0), writes=[r_P])
        B.dve(lambda e: e.memset(Pi[:, 0, :], 0.0), writes=[r_P])
        for k in range(1, nk):
            tt(P["tA"], Pr[:, k - 1, :], P[br], ALU.mult, [r_P, rP[br]], [rP["tA"]])
            tt(P["tB"], Pi[:, k - 1, :], P[bi], ALU.mult, [r_P, rP[bi]], [rP["tB"]])
            tt(Pr[:, k, :], P["tA"], P["tB"], ALU.subtract, [rP["tA"], rP["tB"]], [r_P])
            tt(P["tA"], Pr[:, k - 1, :], P[bi], ALU.mult, [r_P, rP[bi]], [rP["tA"]])
            tt(P["tB"], Pi[:, k - 1, :], P[br], ALU.mult, [r_P, rP[br]], [rP["tB"]])
            tt(Pi[:, k, :], P["tA"], P["tB"], ALU.add, [rP["tA"], rP["tB"]], [r_P])

    powers(Ppr, Ppi, r_Pp, 17, "ar", "ai")
    powers(Pmr, Pmi, r_Pm, 8, "air", "aii")

    def cmul_b(outr, outi, ar_, ai_, br_, bi_, t1, t2, reads, r_t, w_r, w_i, neg_i=False):
        tt(t1, ar_, br_, ALU.mult, reads, [r_t])
        tt(t2, ai_, bi_, ALU.mult, reads, [r_t])
        tt(outr, t1, t2, ALU.subtract, [r_t], w_r)
        tt(t1, ar_, bi_, ALU.mult, reads, [r_t])
        tt(t2, ai_, br_, ALU.mult, reads, [r_t])
        if neg_i:
            stt(outi, t1, -1.0, t2, ALU.mult, ALU.subtract, [r_t], w_i)
        else:
            tt(outi, t1, t2, ALU.add, [r_t], w_i)

    t16c, t16d, r_t16b = t16a, t16b, r_t16
    a7r = Pmr[:, 7, :].unsqueeze(2).to_broadcast([128, 32, 16])
    a7i = Pmi[:, 7, :].unsqueeze(2).to_broadcast([128, 32, 16])
    cmul_b(Spr, Spi, SinTr, SinTi, a7r, a7i, t16c, t16d, [r_SinT, r_Pm], r_t16b, [r_Sp], [r_Sp])

    def stage_G(t):
        sl = t % 2
        for gl in range(4):
            gp = 4 * t + gl
            pmr = Pmr[:, :, gp:gp + 1].to_broadcast([128, 8, 16])
            pmi = Pmi[:, :, gp:gp + 1].to_broadcast([128, 8, 16])
            bbr = Bbr[:, gp:gp + 1, :].to_broadcast([128, 8, 16])
            bbi = Bbi[:, gp:gp + 1, :].to_broadcast([128, 8, 16])
            v3 = lambda x: x.rearrange("p (k c) -> p k c", k=8)
            cmul_b(v3(Wnr), v3(Wni), pmr, pmi, bbr, bbi, v3(gt1), v3(gt2), [r_Pm, r_Bb], r_gt, [r_Wn], [r_Wn])
            ppr = Ppr[:, 0:8, gp:gp + 1].to_broadcast([128, 8, 16])
            ppi = Ppi[:, 0:8, gp:gp + 1].to_broadcast([128, 8, 16])
            cr = cnr[:, gp:gp + 1, :].to_broadcast([128, 8, 16])
            ci = cni[:, gp:gp + 1, :].to_broadcast([128, 8, 16])
            cmul_b(v3(CSnr), v3(CSni), cr, ci, ppr, ppi, v3(gt1), v3(gt2), [r_cn, r_Pp], r_gt, [r_CSn], [r_CSn],
                   neg_i=True)
            ppr2 = Ppr[:, 8:16, gp:gp + 1].to_broadcast([128, 8, 16])
            ppi2 = Ppi[:, 8:16, gp:gp + 1].to_broadcast([128, 8, 16])
            cmul_b(v3(T2r), v3(T2i), cr, ci, ppr2, ppi2, v3(gt1), v3(gt2),
                   [r_cn, r_Pp], r_gt, [r_T2], [r_T2], neg_i=True)
            for g2 in range(2):
                actf(CS2r[sl][:, 2 * gl + g2, :], T2r, AF.Copy, [r_T2, r_const], [r_CS2[sl]], scale=hm[:, g2:g2 + 1])
                actf(CS2i[sl][:, 2 * gl + g2, :], T2i, AF.Copy, [r_T2, r_const], [r_CS2[sl]], scale=hm[:, g2:g2 + 1])
            if stop == "G1":
                return
            bk = 6 + gl % 2
            B.pe(lambda e, bk=bk: e.transpose(out=pall[:, bk, 0:128], in_=Wnr, identity=identf[:]),
                 reads=[r_Wn, r_ident], writes=[r_pb[bk]])
            B.pe(lambda e, bk=bk: e.transpose(out=pall[:, bk, 128:256], in_=Wni, identity=identf[:]),
                 reads=[r_Wn, r_ident], writes=[r_pb[bk]])
            if stop == "G2":
                return
            for g2 in range(2):
                oc = 256 + g2 * 128
                actf(CSzr, CSnr, AF.Copy, [r_CSn, r_const], [r_CSz], scale=hm[:, g2:g2 + 1])
                actf(CSzi, CSni, AF.Copy, [r_CSn, r_const], [r_CSz], scale=hm[:, g2:g2 + 1])
                B.pe(mm(pall[:, bk, oc:oc + 128], Wnr, CSzr, True, False),
                     reads=[r_Wn, r_CSz], writes=[r_pb[bk]])
                B.pe(mm(pall[:, bk, oc:oc + 128], Wni, CSzi, False, True),
                     reads=[r_Wn, r_CSz], writes=[r_pb[bk]])
            if stop == "G3":
                return
            B.alt(lambda e, bk=bk, sl=sl, gl=gl: cp(e, WTr[sl][:, 2 * gl:2 * gl + 2, :],
                                                   pall[:, bk, 0:128].rearrange("p (a b) -> p a b", a=2)),
                  reads=[r_pb[bk]], writes=[r_WT[sl]])
            B.alt(lambda e, bk=bk, sl=sl, gl=gl: cp(e, WTi[sl][:, 2 * gl:2 * gl + 2, :],
                                                   pall[:, bk, 128:256].rearrange("p (a b) -> p a b", a=2)),
                  reads=[r_pb[bk]], writes=[r_WT[sl]])
            if stop == "G4":
                return
            for g2 in range(2):
                g = 2 * gp + g2
                tt(Mg[sl][:, 2 * gl + g2, :], pall[:, bk, 256 + g2 * 128:384 + g2 * 128], mask1, ALU.mult,
                   [r_pb[bk], r_const], [r_Mg[sl]])
                stt(Mg[sl][:, 2 * gl + g2, :], identf[:], Dcol[:, g:g + 1], Mg[sl][:, 2 * gl + g2, :], ALU.mult, ALU.add,
                    [r_ident, r_const, r_Mg[sl]], [r_Mg[sl]])

    def stage_A(t):
        sl = t % 2
        wv, r_w = load_w([w_in[:, C_SU + t * 128:C_SU + (t + 1) * 128]], 16)
        s = project(wv, r_w, 0, 16, hT, r_hT, SEG_MS)
        B.act(lambda e, s=s: cp(e, suTb[:, 0:1024], pset(s)[:, 0:1024]), reads=rset(s), writes=[r_suTb])
        B.act(lambda e, s=s: cp(e, suTb[:, 1024:1088].rearrange("p (k s) -> p s k", k=4),
                                pset(s)[:, 1024:1088].rearrange("p (s k) -> p s k", k=4)), reads=rset(s), writes=[r_suTb2])
        wv, r_w = load_w([w_in[:, C_SZ + t * 128:C_SZ + (t + 1) * 128]], 16)
        s = project(wv, r_w, 0, 16, hT, r_hT, SEG_MS)
        actf(ysT[:, t, :], pset(s)[:, 0:1088], AF.Silu, rset(s), [r_ys[t]])
        sc = scr1[sl]
        r_sc = r_scr1[sl]
        r_f = fence("act", [r_suTb, r_suTb2])
        B.dma(sc[:, 0:1024], suTp[:, t, :], reads=[r_suTp[t]], writes=[r_sc[0]], stream=f"shA{sl}")
        B.dma(sc[:, 1024:2112], suTb, reads=[r_suTb, r_suTb2, r_f], writes=[r_sc[1]], stream=f"shA{sl}")
        di = 0
        for slab in range(2):
            for k in range(8):
                B.dma(Ub[sl][k * 16:(k + 1) * 16, :, slab * 128:(slab + 1) * 128],
                      DAP(sc, slab * 1024 + k * 128, [[SCR1_W, 16], [16 * SCR1_W, 8], [1, 128]]),
                      reads=r_sc, writes=[r_Ub[sl][di]], stream=f"shB{sl}")
                di += 1
        for k in range(4):
            B.dma(Ub[sl][k * 16:(k + 1) * 16, :, 256:272],
                  DAP(sc, 2048 + k * 16, [[SCR1_W, 16], [16 * SCR1_W, 8], [1, 16]]),
                  reads=r_sc, writes=[r_Ub[sl][di]], stream=f"shB{sl}")
            di += 1

    def stage_B(t):
        sl = t % 2
        for gl in range(4):
            gp = 4 * t + gl
            for (WT_, bk) in ((WTr[sl], 6), (WTi[sl], 7)):
                for g2 in range(2):
                    g8 = 2 * gl + g2
                    rows = slice(g2 * 64, (g2 + 1) * 64)
                    B.pe(mm(pall[rows, bk, 0:256], WT_[:, g8, :], Ub[sl][:, g8, 0:256], True, True),
                         reads=[r_WT[sl]] + r_Ub[sl], writes=[r_pb[bk]])
                    B.pe(mm(pall[rows, bk, 256:272], WT_[:, g8, :], Ub[sl][:, g8, 256:272], True, True),
                         reads=[r_WT[sl]] + r_Ub[sl], writes=[r_pb[bk]])
            zr_ps = pall[:, 6, 0:256]
            zi_ps = pall[:, 7, 0:256]
            p8s = P["p8s"][:, gp:gp + 1]
            ph8 = P["phi8"][:, gp:gp + 1]
            actf(rt1, jtab, AF.Identity, [r_const, rP["p8s"]], [r_rot], scale=p8s, bias=MAGIC)
            actf(rt1, rt1, AF.Identity, [r_rot], [r_rot], bias=-MAGIC)
            actf(rt2, jtab, AF.Copy, [r_const, rP["phi8"]], [r_rot], scale=ph8)
            stt(rt2, rt1, -TWO_PI, rt2, ALU.mult, ALU.add, [r_rot], [r_rot])
            ts(rt2, rt2, -PI_S, PI_S, ALU.max, ALU.min, [r_rot], [r_rot])
            actf(rsin, rt2, AF.Sin, [r_rot], [r_rot])
            actf(rt1, rt2, AF.Abs, [r_rot], [r_rot])
            actf(rcos, rt1, AF.Sin, [r_rot], [r_rot], scale=-1.0, bias=math.pi / 2)
            tt(rt1, zr_ps, rcos, ALU.mult, [r_pb[6], r_rot], [r_rot])
            tt(rt2, zi_ps, rsin, ALU.mult, [r_pb[7], r_rot], [r_rot])
            tt(rzr, rt1, rt2, ALU.add, [r_rot], [r_rot])
            tt(rt1, zi_ps, rcos, ALU.mult, [r_pb[7], r_rot], [r_rot])
            tt(rt2, zr_ps, rsin, ALU.mult, [r_pb[6], r_rot], [r_rot])
            tt(rzi, rt1, rt2, ALU.subtract, [r_rot], [r_rot])
            B.dve(lambda e, gp=gp: e.tensor_copy(out=ZSr[:, gp, :], in_=pall[:, 6, 256:272]), reads=[r_pb[6]],
                  writes=[r_ZS[t]])
            B.dve(lambda e, gp=gp: e.tensor_copy(out=ZSi[:, gp, :], in_=pall[:, 7, 256:272]), reads=[r_pb[7]],
                  writes=[r_ZS[t]])
            r8b = P["r8"][:, gp:gp + 1].to_broadcast([128, 256])
            B.dve(lambda e, r8b=r8b: e.tensor_tensor_scan(out=rt1, data0=r8b, data1=rzr, initial=0.0, op0=ALU.mult,
                                                         op1=ALU.add), reads=[r_rot, rP["r8"]], writes=[r_rot])
            B.dve(lambda e, r8b=r8b: e.tensor_tensor_scan(out=rt2, data0=r8b, data1=rzi, initial=0.0, op0=ALU.mult,
                                                         op1=ALU.add), reads=[r_rot, rP["r8"]], writes=[r_rot])
            c_ = rcos[:, 127:256]
            s_ = rsin[:, 127:256]
            tt(rzr[:, 0:129], rt1[:, 127:256], c_, ALU.mult, [r_rot], [r_rot])
            tt(rzi[:, 0:129], rt2[:, 127:256], s_, ALU.mult, [r_rot], [r_rot])
            tt(Spvr[sl][:, gl, 0:129], rzr[:, 0:129], rzi[:, 0:129], ALU.subtract, [r_rot], [r_Spv[sl]])
            B.dve(lambda e, gp=gp: e.tensor_tensor(out=SFr[:, gp:gp + 1], in0=rzr[:, 128:129], in1=rzi[:, 128:129],
                                                   op=ALU.subtract), reads=[r_rot], writes=[r_SF[t]])
            tt(rzr[:, 0:129], rt2[:, 127:256], c_, ALU.mult, [r_rot], [r_rot])
            tt(rzi[:, 0:129], rt1[:, 127:256], s_, ALU.mult, [r_rot], [r_rot])
            tt(Spvi[sl][:, gl, 0:129], rzr[:, 0:129], rzi[:, 0:129], ALU.add, [r_rot], [r_Spv[sl]])
            B.dve(lambda e, gp=gp: e.tensor_tensor(out=SFi[:, gp:gp + 1], in0=rzr[:, 128:129], in1=rzi[:, 128:129],
                                                   op=ALU.add), reads=[r_rot], writes=[r_SF[t]])

    def stage_C(t):
        sl = t % 2
        for g8 in range(8):
            gl = g8 // 2
            bk = 6 + g8 % 2
            c0 = (g8 // 2) * 128
            B.pe(mm(pall[:, bk, c0:c0 + 128], Mg[sl][:, g8, :], Ub[sl][:, g8, 128:256], True, False),
                 reads=[r_Mg[sl]] + r_Ub[sl], writes=[r_pb[bk]], drain=True)
            B.pe(mm(pall[:, bk, c0:c0 + 128], CS2r[sl][:, g8, :], Spvr[sl][:, gl, 0:128], False, False),
                 reads=[r_CS2[sl], r_Spv[sl]], writes=[r_pb[bk]])
            B.pe(mm(pall[:, bk, c0:c0 + 128], CS2i[sl][:, g8, :], Spvi[sl][:, gl, 0:128], False, True),
                 reads=[r_CS2[sl], r_Spv[sl]], writes=[r_pb[bk]])
        for b2 in range(2):
            B.act(lambda e, b2=b2: cp(e, Ysb[:, b2:8:2, 0:128], pall[:, 6 + b2, :].rearrange("p (a b) -> p a b", a=4)),
                  reads=[r_pb[6 + b2]], writes=[r_Ysb])
        for g8 in range(8):
            gl = g8 // 2
            gp = 4 * t + gl
            bk = 6 + g8 % 2
            c0 = (g8 // 2) * 16
            B.pe(mm(pall[:, bk, c0:c0 + 16], Mg[sl][:, g8, :], Ub[sl][:, g8, 256:272], True, False),
                 reads=[r_Mg[sl]] + r_Ub[sl], writes=[r_pb[bk]], drain=True)
            B.pe(mm(pall[:, bk, c0:c0 + 16], CS2r[sl][:, g8, :], Spr[:, gp, :], False, False),
                 reads=[r_CS2[sl], r_Sp], writes=[r_pb[bk]])
            B.pe(mm(pall[:, bk, c0:c0 + 16], CS2i[sl][:, g8, :], Spi[:, gp, :], False, True),
                 reads=[r_CS2[sl], r_Sp], writes=[r_pb[bk]])
        for b2 in range(2):
            B.act(lambda e, b2=b2: cp(e, Ysb[0:64, b2:8:2, 128:144],
                                      pall[0:64, 6 + b2, 0:64].rearrange("p (a b) -> p a b", a=4)),
                  reads=[r_pb[6 + b2]], writes=[r_Ysb])
        sc = scr2[sl]
        r_sc = r_scr2[sl]
        r_f = fence("act", [r_Ysb])
        for k in range(8):
            B.dma(DAP(sc, k * YW, [[8 * YW, 16], [16 * 8 * YW, 8], [1, 128]]), Ysb[k * 16:(k + 1) * 16, :, 0:128],
                  reads=[r_Ysb, r_f], writes=[r_sc[k]], stream=f"shC{sl}")
        for k in range(4):
            B.dma(DAP(sc, k * YW + 128, [[8 * YW, 16], [16 * 8 * YW, 8], [1, 16]]), Ysb[k * 16:(k + 1) * 16, :, 128:144],
                  reads=[r_Ysb, r_f], writes=[r_sc[8 + k]], stream=f"shC{sl}")
        B.dma(acc[:, 0:1024].rearrange("p (k j) -> p k j", k=8), DAP(sc, 0, [[8 * YW, 128], [YW, 8], [1, 128]]),
              reads=r_sc, writes=[r_acc], stream="shD0")
        B.dma(acc[:, 1024:1088].rearrange("p (k s) -> p k s", k=4), DAP(sc, 128, [[8 * YW, 128], [YW, 4], [1, 16]]),
              reads=r_sc, writes=[r_acc2], stream="shD1")
        acs = acc[:, 1024:1088].rearrange("p (k s) -> p s k", k=4)
        yss = ysT[:, t, 1024:1088].rearrange("p (s k) -> p s k", k=4)
        tt(acc[:, 0:1024], acc[:, 0:1024], ysT[:, t, 0:1024], ALU.mult, [r_acc, r_ys[t]], [r_acc])
        tt(acs, acs, yss, ALU.mult, [r_acc2, r_ys[t]], [r_acc2])
        actf(ysT[:, t, 0:1024], acc[:, 0:1024], AF.Gelu_apprx_tanh, [r_acc], [r_ys[t]])
        actf(yss, acs, AF.Gelu_apprx_tanh, [r_acc2, r_ys[t]], [r_ys[t]])

    def stage_conv(t):
        wv, r_w = load_w([w_in[:, C_CC + t * 128:C_CC + (t + 1) * 128]], 16)
        s = project(wv, r_w, 0, 16, hT, r_hT, SEG_MSH)
        actf(tmp_cc, pset(s)[:, 0:NX], AF.Copy, rset(s), [r_tmpcc])
        wv, r_w = load_w([w_in[:, C_CH + t * 128:C_CH + (t + 1) * 128]], 16)
        s = project(wv, r_w, 0, 16, hT, r_hT, SEG_MSH)
        ps = pset(s)
        tt(vext[:, 2:10, :], ps[:, 0:1024].rearrange("p (k j) -> p k j", k=8),
           tmp_cc[:, 0:1024].rearrange("p (k j) -> p k j", k=8), ALU.mult, rset(s) + [r_tmpcc], [r_vext])
        tt(vs[:, :, 2:6], ps[:, 1024:1088].rearrange("p (s k) -> p s k", k=4),
           tmp_cc[:, 1024:1088].rearrange("p (s k) -> p s k", k=4), ALU.mult, rset(s) + [r_tmpcc], [r_vext])
        tt(vh, ps[:, 1088:1090], tmp_cc[:, 1088:1090], ALU.mult, rset(s) + [r_tmpcc], [r_vext])
        cpv = lambda o, i, rd=[r_vext], wr=[r_vext]: B.dve(lambda e: e.tensor_copy(out=o, in_=i), reads=rd, writes=wr)
        cpv(vext[:, 1, 1:128], vext[:, 9, 0:127])
        cpv(vext[:, 0, 1:128], vext[:, 8, 0:127])
        cpv(vext[:, 1, 0:1], vh[:, 1:2])
        cpv(vext[:, 0, 0:1], vh[:, 0:1])
        cpv(vs[:, :, 0:2], sconvT[:, t, :, :], [r_sconvT, r_vext], [r_vext])
        cpv(ncp[:, t, 0:1], vext[:, 8, 127:128], [r_vext], [r_ncp])
        cpv(ncp[:, t, 1:2], vext[:, 9, 127:128], [r_vext], [r_ncp])
        cpv(ncs[:, t, :, :], vs[:, :, 4:6], [r_vext], [r_ncs])
        accm = acc[:, 0:1024].rearrange("p (k j) -> p k j", k=8)
        accs = acc[:, 1024:1088].rearrange("p (s k) -> p s k", k=4)
        ts(accm, vext[:, 2:10, :], cw[:, t, 2:3], None, ALU.mult, None, [r_vext, r_const], [r_acc])
        stt(accm, vext[:, 1:9, :], cw[:, t, 1:2], accm, ALU.mult, ALU.add, [r_vext, r_const, r_acc], [r_acc])
        stt(accm, vext[:, 0:8, :], cw[:, t, 0:1], accm, ALU.mult, ALU.add, [r_vext, r_const, r_acc], [r_acc])
        ts(accs, vs[:, :, 2:6], cw[:, t, 2:3], None, ALU.mult, None, [r_vext, r_const], [r_acc2])
        stt(accs, vs[:, :, 1:5], cw[:, t, 1:2], accs, ALU.mult, ALU.add, [r_vext, r_const, r_acc2], [r_acc2])
        stt(accs, vs[:, :, 0:4], cw[:, t, 0:1], accs, ALU.mult, ALU.add, [r_vext, r_const, r_acc2], [r_acc2])
        wv, r_w = load_w([w_in[:, C_CZ + t * 128:C_CZ + (t + 1) * 128]], 16)
        s = project(wv, r_w, 0, 16, hT, r_hT, SEG_MS)
        actf(tmp_s, pset(s)[:, 0:1088], AF.Silu, rset(s), [r_tmps])
        wv, r_w = load_w([w_in[:, C_CB + t * 128:C_CB + (t + 1) * 128]], 16)
        s = project(wv, r_w, 0, 16, hT, r_hT, SEG_MS)
        tt(tmp_s, pset(s)[:, 0:1088], tmp_s, ALU.mult, rset(s) + [r_tmps], [r_tmps])
        tt(cact[:, t, :], acc, tmp_s, ALU.mult, [r_acc, r_acc2, r_tmps], [r_cact[t]])

    if stop == "prep":
        return finish()
    for i in range(nloop + 1):
        if i < nloop:
            stage_A(i)
            stage_conv(i)
            stage_G(i)
        if i >= 1:
            stage_C(i - 1)
            if stop == "C":
                return finish()
        if i < nloop:
            stage_B(i)
            if stop == "B" and i == 1:
                return finish()
    if stop == "loop":
        return finish()

    fr_ = P["tA"]
    fi_ = P["tB"]
    r_fin = R("fin")
    t32a = P["am1"]
    t32b = P["den"]
    r_t32 = R("t32")
    cmul_b(fr_, fi_, SFr, SFi, Ppr[:, 7, :], Ppi[:, 7, :], t32a, t32b, r_SF + [r_Pp], r_t32, [r_fin], [r_fin])
    r_f = fence("dve", [r_fin])
    B.dma(DAP(spr_o, 0, [[1, 128], [128, 32]]), fr_, reads=[r_fin, r_f], stream="out", allow_slow_non_contiguous=True)
    B.dma(DAP(spi_o, 0, [[1, 128], [128, 32]]), fi_, reads=[r_fin, r_f], stream="out", allow_slow_non_contiguous=True)
    a4r = Ppr[:, 4, :].unsqueeze(2).to_broadcast([128, 32, 16])
    a4i = Ppi[:, 4, :].unsqueeze(2).to_broadcast([128, 32, 16])
    a3r = Ppr[:, 3, :].unsqueeze(2).to_broadcast([128, 32, 16])
    a3i = Ppi[:, 3, :].unsqueeze(2).to_broadcast([128, 32, 16])
    n1r, n1i = bnr, bni
    n2r = tmp_s[:, 0:512].rearrange("p (a b) -> p a b", a=32)
    n2i = tmp_s[:, 512:1024].rearrange("p (a b) -> p a b", a=32)
    r_n1 = R("n1")
    r_n2 = R("n2")
    cmul_b(n1r, n1i, SinTr, SinTi, a4r, a4i, t16c, t16d, [r_SinT, r_Pp, r_bn], r_t16b, [r_n1, r_bn], [r_n1, r_bn])
    cmul_b(n2r, n2i, ZSr, ZSi, a3r, a3i, t16c, t16d, r_ZS + [r_Pp, r_tmps], r_t16b, [r_n2, r_tmps], [r_n2, r_tmps])
    tt(n1r, n1r, n2r, ALU.add, [r_n1, r_n2], [r_n1])
    tt(n1i, n1i, n2i, ALU.add, [r_n1, r_n2], [r_n1])
    r_f = fence("dve", [r_n1, r_ncp, r_ncs])
    ost = [tmp_cc[0:16, 0:512], tmp_cc[0:16, 512:1024], vext[0:16, :, :].rearrange("p a b -> p (a b)")[:, 0:512],
           vext[0:16, :, :].rearrange("p a b -> p (a b)")[:, 512:1024]]
    r_ost = [R(f"ost{i}") for i in range(4)]
    oi = 0
    for q8 in range(8):
        for (src3, dsto, bk) in ((n1r, snr_o, 6), (n1i, sni_o, 7)):
            for gpl in range(4):
                B.pe(lambda e, src3=src3, bk=bk, gpl=gpl, q8=q8: e.transpose(
                    out=pall[0:16, bk, gpl * 128:(gpl + 1) * 128], in_=src3[:, 4 * q8 + gpl, :], identity=identf[:, :]),
                    reads=[r_n1, r_ident], writes=[r_pb[bk]])
            o_ = ost[oi % 4]
            r_o = r_ost[oi % 4]
            ostream = f"ost{oi % 4}"
            oi += 1
            B.act(lambda e, o_=o_, bk=bk: cp(e, o_, pall[0:16, bk, :]), reads=[r_pb[bk]], writes=[r_o, r_tmpcc, r_vext])
            r_f2 = fence("act", [r_o])
            B.dma(dsto[:, q8 * 512:(q8 + 1) * 512], o_, reads=[r_o, r_f2], stream=ostream)
    nst = acc[0:32, 0:1024]
    for t in range(8):
        bk = 6 + t // 4
        B.pe(lambda e, t=t, bk=bk: e.transpose(out=pall[0:32, bk, (t % 4) * 128:(t % 4 + 1) * 128],
                                               in_=ncs[:, t, :, :].rearrange("p s r -> p (s r)"), identity=identf[:, :]),
             reads=[r_ncs, r_ident], writes=[r_pb[bk]])
    for b2 in range(2):
        B.act(lambda e, b2=b2: cp(e, nst[:, b2 * 512:(b2 + 1) * 512], pall[0:32, 6 + b2, :]), reads=[r_pb[6 + b2]],
              writes=[r_acc])
    r_f3 = fence("act", [r_acc])
    B.dma(ncs_o.rearrange("s r c -> (s r) c"), nst, reads=[r_acc, r_f3], stream="out")
    for r in range(2):
        B.dma(DAP(ncp_o, r * 1024, [[1, 128], [128, 8]]), ncp[:, :, r], reads=[r_ncp, r_f], stream="out",
              allow_slow_non_contiguous=True)
    if "cact" in B.dbg:
        o = B.dout("dbg_cact", [128, 8, 1088], BF16)
        B.dma(o, cact[:, :, :], reads=r_cact, stream="dbg")
        o = B.dout("dbg_ys", [128, 8, 1088], BF16)
        B.dma(o, ysT[:, :, :], reads=r_ys, stream="dbg")
    B.S.barrier()
    if stop == "loopout":
        return finish()

    AR.top = 0
    tg = AR.alloc([1088])
    tb = AR.alloc([1088])
    r_tg = R("tg")
    r_tb = R("tb")
    r_mg = [R(f"mg{i}") for i in range(16)]

    def gate_sig(col, dst, r_dst):
        wv, r_w = load_w([w_in[:, col:col + 128]], 16)
        s = project(wv, r_w, 0, 16, hT, r_hT, SEG_MS)
        actf(dst, pset(s)[:, 0:1088], AF.Sigmoid, rset(s), [r_dst])

    def branch_out(wsrc, j, actbuf, r_actbuf):
        wv, r_w = load_w([wsrc[:, j * 128:(j + 1) * 128]], 8)
        return project(wv, r_w, 0, 8, actbuf, r_actbuf, SEG_MS)

    for j in range(16):
        gate_sig(C_G + 0 * D + j * 128, tg, r_tg)
        s = branch_out(w_conv_out, j, cact, r_cact)
        tt(merged[:, j, :], pset(s)[:, 0:1088], tg, ALU.mult, rset(s) + [r_tg], [r_mg[j]])
    for j in range(16):
        gate_sig(C_G + 1 * D + j * 128, tg, r_tg)
        s = branch_out(w_glu_b, j, ysT, r_ys)
        actf(tb, pset(s)[:, 0:1088], AF.Sigmoid, rset(s), [r_tb])
        tt(tg, tg, tb, ALU.mult, [r_tg, r_tb], [r_tg])
        s = branch_out(w_glu_a, j, ysT, r_ys)
        tt(tb, pset(s)[:, 0:1088], tg, ALU.mult, rset(s) + [r_tg], [r_tb])
        tt(merged[:, j, :], merged[:, j, :], tb, ALU.add, [r_mg[j], r_tb], [r_mg[j]])
    if "m2" in B.dbg:
        o = B.dout("dbg_m2", [128, 16, 1088], BF16)
        B.dma(o, merged, reads=r_mg, stream="dbg")
    B.S.barrier()
    if stop == "post":
        return finish()

    AR.top = 0
    KT = AR.alloc([8, 256], BF16)
    Vb = AR.alloc([2, 1024], BF16)
    a_top = AR.top
    memT = AR.alloc([16, 256], BF16)
    kst = AR.alloc([1024])
    vst = AR.alloc([1024])
    r_memT = [R(f"memT{i}") for i in range(16)]
    r_KT = R("KT")
    r_Vb = R("Vb")
    r_kst = R("kst")
    r_vst = R("vst")
    norm_transpose([(mem[0:128, :], 128, memT, r_memT, 0), (mem[128:256, :], 128, memT, r_memT, 128)], mem_norm_g, AR)
    if stop == "at0":
        return finish()
    for c4 in range(8):
        wv, r_w = load_w([w_mem_k[:, c4 * 128:(c4 + 1) * 128]], 16)
        for kt in range(16):
            B.pe(mm(pall[:, 0, 0:256], wv[:, kt, 0:128], memT[:, kt, :], kt == 0, kt == 15),
                 reads=[r_w, r_memT[kt]], writes=[r_pb[0]])
        B.alt(lambda e, c4=c4: cp(e, KT[:, c4, :], pall[:, 0, 0:256]), reads=[r_pb[0]], writes=[r_KT])
        for kt in range(16):
            B.pe(mm(pall[:, 1, 0:128], memT[:, kt, 0:128], wv[:, kt, 0:128], kt == 0, kt == 15),
                 reads=[r_w, r_memT[kt]], writes=[r_pb[1]])
        B.alt(lambda e, c4=c4: cp(e, kst[:, c4 * 128:(c4 + 1) * 128], pall[:, 1, 0:128]), reads=[r_pb[1]], writes=[r_kst])
    r_fa = fence("act", [r_kst])
    r_fd = fence("dve", [r_kst])
    B.dma(mk_o, kst, reads=[r_kst, r_fa, r_fd], stream="out")
    if stop == "at0k":
        return finish()
    for c4 in range(8):
        wv, r_w = load_w([w_mem_v[:, c4 * 128:(c4 + 1) * 128]], 16)
        for mt in range(1 if stop == "at0v1" else 2):
            bk = 2 + mt
            for kt in range(16):
                B.pe(mm(pall[:, bk, 0:128], memT[:, kt, mt * 128:(mt + 1) * 128], wv[:, kt, 0:128], kt == 0, kt == 15),
                     reads=[r_w, r_memT[kt]], writes=[r_pb[bk]])
            if mt == 0:
                B.alt(lambda e, c4=c4, bk=bk: cp(e, vst[:, c4 * 128:(c4 + 1) * 128], pall[:, bk, 0:128]),
                      reads=[r_pb[bk]], writes=[r_vst])
                B.alt(lambda e, c4=c4: cp(e, Vb[:, 0, c4 * 128:(c4 + 1) * 128], vst[:, c4 * 128:(c4 + 1) * 128]),
                      reads=[r_vst], writes=[r_Vb])
            else:
                B.alt(lambda e, c4=c4, mt=mt, bk=bk: cp(e, Vb[:, mt, c4 * 128:(c4 + 1) * 128], pall[:, bk, 0:128]),
                      reads=[r_pb[bk]], writes=[r_Vb])
    if stop == "at0v":
        return finish()
    r_fa = fence("act", [r_vst])
    r_fd = fence("dve", [r_vst])
    B.dma(mv_o, vst, reads=[r_vst, r_fa, r_fd], stream="out")
    if stop == "at0w":
        return finish()
    B.S.barrier()
    if stop == "at1":
        return finish()
    AR.top = a_top
    qT = AR.alloc([8, 1088], BF16)
    azs = AR.alloc([8, 1088], BF16)
    r_qT = [R(f"qT{i}") for i in range(8)]
    r_azs = [R(f"azs{i}") for i in range(8)]
    for f in range(8):
        wv, r_w = load_w([w_in[:, C_AQ + f * 128:C_AQ + (f + 1) * 128]], 16)
        s = project(wv, r_w, 0, 16, hT, r_hT, SEG_MS)
        B.alt(lambda e, s=s, f=f: cp(e, qT[:, f, :], pset(s)[:, 0:1088]), reads=rset(s), writes=[r_qT[f]])
        wv, r_w = load_w([w_in[:, C_AZ + f * 128:C_AZ + (f + 1) * 128]], 16)
        s = project(wv, r_w, 0, 16, hT, r_hT, SEG_MS)
        actf(azs[:, f, :], pset(s)[:, 0:1088], AF.Silu, rset(s), [r_azs[f]])
    if stop == "at2":
        return finish()
    att = cact
    r_att = [R(f"att{i}") for i in range(8)]
    B.S.barrier()
    PT = [AR.alloc([2, 512], BF16) for _ in range(2)]
    r_PT = [R("PT0"), R("PT1")]
    rsum = AR.alloc([512])
    r_rsum = R("rsum")
    otmp = AR.alloc([512])
    r_otmp = R("otmp")
    SC = 1.0 / 16.0
    it = 0
    for half in range(2):
        cs_ = slice(half * 512, (half + 1) * 512)
        for h in range(4):
            pi_ = it % 2
            it += 1
            for mt in range(2):
                for dt_ in range(2):
                    B.pe(mm(pall[:, mt, :], KT[:, 2 * h + dt_, mt * 128:(mt + 1) * 128], qT[:, 2 * h + dt_, cs_],
                            dt_ == 0, dt_ == 1), reads=[r_KT, r_qT[2 * h + dt_]], writes=[r_pb[mt]])
            for mt in range(2):
                actf(PT[pi_][:, mt, :], pall[:, mt, :], AF.Exp, [r_pb[mt]], [r_PT[pi_]], scale=SC)
            for mt in range(2):
                B.pe(mm(pall[:, 2, :], onesb[:], PT[pi_][:, mt, :], mt == 0, mt == 1), reads=[r_ident, r_PT[pi_]],
                     writes=[r_pb[2]])
            B.dve(lambda e: e.reciprocal(out=rsum, in_=pall[:, 2, :]), reads=[r_pb[2]], writes=[r_rsum])
            for dt_ in range(2):
                bk = 3 + dt_
                f = 2 * h + dt_
                for mt in range(2):
                    B.pe(mm(pall[:, bk, :], Vb[:, mt, f * 128:(f + 1) * 128], PT[pi_][:, mt, :], mt == 0, mt == 1),
                         reads=[r_Vb, r_PT[pi_]], writes=[r_pb[bk]])
                tt(otmp, pall[:, bk, :], rsum, ALU.mult, [r_pb[bk], r_rsum], [r_otmp])
                tt(att[:, f, cs_], otmp, azs[:, f, cs_], ALU.mult, [r_otmp, r_azs[f]], [r_att[f]])
    if stop == "at3":
        return finish()
    Kb = [AR.alloc([2, 1024], BF16) for _ in range(2)]
    Vs = [AR.alloc([2, 1024], BF16) for _ in range(2)]
    KTs = [AR.alloc([8, 256], BF16) for _ in range(2)]
    PTs = AR.alloc([2, 4, 4], BF16)
    rs16 = AR.alloc([4, 4])
    o32 = AR.alloc([8, 4])
    r_Kb = [R("Kb0"), R("Kb1")]
    r_Vs = [R("Vs0"), R("Vs1")]
    r_KTs = [R("KTs0"), R("KTs1")]
    r_PTs = R("PTs")
    r_rs16 = R("rs16")
    r_o32 = R("o32")
    for sq in range(16):
        sl = sq % 2
        B.dma(Kb[sl], ck[sq].rearrange("(a p) c -> p a c", p=128), writes=[r_Kb[sl]], stream=f"kvk{sl}", q="pool")
        B.dma(Vs[sl], cv[sq].rearrange("(a p) c -> p a c", p=128), writes=[r_Vs[sl]], stream=f"kvv{sl}", q="pool")
        for mt in range(2):
            bk = mt
            pv = pall[:, bk, :].bitcast(BF16)
            for f in range(8):
                B.pe(lambda e, pv=pv, sl=sl, mt=mt, f=f: e.transpose(out=pv[:, f * 128:(f + 1) * 128],
                                                                     in_=Kb[sl][:, mt, f * 128:(f + 1) * 128],
                                                                     identity=identb[:]),
                     reads=[r_Kb[sl], r_ident], writes=[r_pb[bk]])
            B.alt(lambda e, pv=pv, sl=sl, mt=mt: cp(e, KTs[sl][:, :, mt * 128:(mt + 1) * 128],
                                                   pv.rearrange("p (a b) -> p a b", a=8)),
                  reads=[r_pb[bk]], writes=[r_KTs[sl]])
        qc = slice(S0 + 4 * sq, S0 + 4 * sq + 4)
        for mt in range(2):
            for h in range(4):
                c0 = (mt * 4 + h) * 4
                for dt_ in range(2):
                    B.pe(mm(pall[:, 7, c0:c0 + 4], KTs[sl][:, 2 * h + dt_, mt * 128:(mt + 1) * 128], qT[:, 2 * h + dt_, qc],
                            dt_ == 0, dt_ == 1), reads=[r_KTs[sl], r_qT[2 * h + dt_]], writes=[r_pb[7]])
        actf(PTs, pall[:, 7, 0:32].rearrange("p (a b c) -> p a b c", a=2, b=4), AF.Exp, [r_pb[7]], [r_PTs], scale=SC)
        for h in range(4):
            for mt in range(2):
                B.pe(mm(pall[:, 7, 64 + h * 4:64 + h * 4 + 4], onesb[:], PTs[:, mt, h, :], mt == 0, mt == 1),
                     reads=[r_ident, r_PTs], writes=[r_pb[7]])
        for h in range(4):
            for dt_ in range(2):
                f = 2 * h + dt_
                for mt in range(2):
                    B.pe(mm(pall[:, 7, 128 + f * 4:128 + f * 4 + 4], Vs[sl][:, mt, f * 128:(f + 1) * 128], PTs[:, mt, h, :],
                            mt == 0, mt == 1), reads=[r_Vs[sl], r_PTs], writes=[r_pb[7]])
        B.dve(lambda e: e.reciprocal(out=rs16, in_=pall[:, 7, 64:80].rearrange("p (a b) -> p a b", a=4)),
              reads=[r_pb[7]], writes=[r_rs16])
        tt(o32.rearrange("p (h d) c -> p h d c", d=2), pall[:, 7, 128:160].rearrange("p (h d c) -> p h d c", h=4, d=2),
           rs16.unsqueeze(2).to_broadcast([128, 4, 2, 4]), ALU.mult, [r_pb[7], r_rs16], [r_o32])
        tt(att[:, :, qc], o32, azs[:, :, qc], ALU.mult, [r_o32] + r_azs, r_att)
    if stop == "at4":
        return finish()
    AR2 = AR.top
    tg2 = AR.alloc([1088])
    tb2 = AR.alloc([1088])
    for j in range(16):
        gate_sig(C_G + 2 * D + j * 128, tg2, r_tg)
        s = branch_out(w_attn_out, j, att, r_att)
        tt(tb2, pset(s)[:, 0:1088], tg2, ALU.mult, rset(s) + [r_tg], [r_tb])
        tt(merged[:, j, :], merged[:, j, :], tb2, ALU.add, [r_mg[j], r_tb], [r_mg[j]])
    if "m3" in B.dbg:
        o = B.dout("dbg_m3", [128, 16, 1088], BF16)
        B.dma(o, merged, reads=r_mg, stream="dbg")
        o = B.dout("dbg_att", [128, 8, 1088], BF16)
        B.dma(o, att[:, :, :], reads=r_att, stream="dbg")
    B.S.barrier()
    if stop == "attn":
        return finish()

    AR.top = 0
    wo = AR.alloc([16, 2048], BF16)
    r_wo = [R(f"wo{i}") for i in range(16)]
    hTf = hT[:, :, :].rearrange("p a b -> p (a b)").bitcast(F32)
    xr = [hTf[:, 0:2048], hTf[:, 2048:4096]]
    fg = hTf[:, 4096:6144]
    sqj = hTf[:, 6144:8192]
    r_xr = [R("xr0"), R("xr1")]
    r_fg = R("fg")
    r_sqj = R("sqj")
    B.dma(fg, final_g.partition_broadcast(128), writes=[r_fg], stream="fg")
    for c in range(16):
        B.dma(wo[:, :, c * 128:(c + 1) * 128], w_out[:, c * 128:(c + 1) * 128].rearrange("(a p) c -> p a c", p=128),
              writes=[r_wo[c]], stream=f"wo{c}", q="pool")
    ym_v = ym_o.rearrange("(j k) d -> k j d", k=8)
    ttiles = [(xm_v[k], ym_v[k], 128, k * 128) for k in range(8)] + [(xs, ys_o, NS, S0)]
    for ti, (xsrc, ydst, rows, col0) in enumerate(ttiles):
        sl = ti % 2
        B.dma(xr[sl][:rows, :], xsrc, writes=[r_xr[sl]], stream=f"xld{sl}")
        for cb_ in range(4):
            bk = (ti % 2) * 4 + cb_
            for kt in range(16):
                B.pe(mm(pall[:rows, bk, :], merged[:, kt, col0:col0 + rows], wo[:, kt, cb_ * 512:(cb_ + 1) * 512],
                        kt == 0, kt == 15), reads=[r_mg[kt]] + r_wo[cb_ * 4:(cb_ + 1) * 4], writes=[r_pb[bk]])
            tt(xr[sl][:rows, cb_ * 512:(cb_ + 1) * 512], xr[sl][:rows, cb_ * 512:(cb_ + 1) * 512], pall[:rows, bk, :], ALU.add,
               [r_xr[sl], r_pb[bk]], [r_xr[sl]])
        si = ti % 32
        sq_ = stat[:rows, 2 * si:2 * si + 1]
        rs_ = stat[:rows, 2 * si + 1:2 * si + 2]
        r_s = r_stat[si]
        B.act(lambda e, sl=sl, rows=rows, sq_=sq_: e.activation(out=sqj[:rows, :], in_=xr[sl][:rows, :], func=AF.Square,
                                                               accum_out=sq_), reads=[r_xr[sl]], writes=[r_sqj, r_s])
        B.act(lambda e, sq_=sq_, rs_=rs_: e.activation(out=rs_, in_=sq_, func=AF.Sqrt, scale=1.0 / D, bias=EPS),
              reads=[r_s], writes=[r_s])
        B.dve(lambda e, rs_=rs_: e.reciprocal(out=rs_, in_=rs_), reads=[r_s], writes=[r_s])
        stt(xr[sl][:rows, :], xr[sl][:rows, :], rs_, fg[:rows, :], ALU.mult, ALU.mult, [r_xr[sl], r_s, r_fg], [r_xr[sl]])
        r_f = fence("dve", [r_xr[sl]])
        B.dma(ydst, xr[sl][:rows, :], reads=[r_xr[sl], r_f], stream=f"yout{sl}")

    return finish()


IN_NAMES = ["xm", "xp", "xs", "mem", "ck", "cv", "sconv", "ssr", "ssi"]


def core_inputs(inp, c):
    b, hb = c // 2, c % 2
    f = np.float32
    A = np.ascontiguousarray
    memr = inp["mem_prompt"][b]
    return {
        "xm": A(inp["x_prompt"][b, hb * NM:(hb + 1) * NM]),
        "xp": A(inp["x_prompt"][b, 0:NM]) if hb == 1 else np.zeros((NM, D), f),
        "xs": A(inp["x_sample"][c * 16:(c + 1) * 16].reshape(NS, D)),
        "mem": A(np.concatenate([memr[hb * 128:(hb + 1) * 128], memr[(1 - hb) * 128:(2 - hb) * 128]], 0)),
        "ck": A(inp["cache_mem_k"][0, c * 16:(c + 1) * 16].reshape(16, 256, 1024)),
        "cv": A(inp["cache_mem_v"][0, c * 16:(c + 1) * 16].reshape(16, 256, 1024)),
        "sconv": A(inp["state_conv"][0, c * 16:(c + 1) * 16]),
        "ssr": A(inp["state_ssm_re"][0, c * 16:(c + 1) * 16].reshape(16, 4096)),
        "ssi": A(inp["state_ssm_im"][0, c * 16:(c + 1) * 16].reshape(16, 4096)),
    }


def shared_inputs(inp):
    A = np.ascontiguousarray
    return {
        "norm_g": A(inp["norm_g"][0]), "mem_norm_g": A(inp["mem_norm_g"][0]), "final_g": A(inp["final_norm_g"]),
        "w_in": A(inp["w_in"][0]), "conv_w": A(inp["conv_w"][0]), "w_conv_out": A(inp["w_conv_out"][0]),
        "lam_re": A(inp["ssm_lambda_re"][0]), "lam_im": A(inp["ssm_lambda_im"][0]), "log_dt": A(inp["ssm_log_dt"][0]),
        "b_re": A(inp["ssm_b_re"][0]), "b_im": A(inp["ssm_b_im"][0]), "c_re": A(inp["ssm_c_re"][0]),
        "c_im": A(inp["ssm_c_im"][0]), "ssm_d": A(inp["ssm_d"][0]), "w_glu_a": A(inp["w_glu_a"][0]),
        "w_glu_b": A(inp["w_glu_b"][0]), "w_mem_k": A(inp["w_mem_k"][0]), "w_mem_v": A(inp["w_mem_v"][0]),
        "w_attn_out": A(inp["w_attn_out"][0]), "w_out": A(inp["w_out"][0]),
    }


_CACHE = {}


def run(inp, dbg=(), ncores=NCORE, stop=None, nloop=8):
    key = (tuple(sorted(dbg)), stop, nloop)
    if key not in _CACHE:
        _CACHE[key] = build(dbg, stop, nloop)
    B = _CACHE[key]
    sh = shared_inputs(inp)
    in_maps = []
    for c in range(ncores):
        m = dict(sh)
        m.update(core_inputs(inp, c))
        in_maps.append({k: v for k, v in m.items() if k in B.ins})
    res = run_bass_kernel_spmd(B.nc, in_maps, core_ids=list(range(ncores)))
    return res.results


def assemble(res):
    f = np.float32
    y_prompt = np.zeros((4, 2048, D), f)
    y_sample = np.zeros((128, 4, D), f)
    mk = np.zeros((1, 4, 256, 4, 256), f)
    mv = np.zeros((1, 4, 256, 4, 256), f)
    ncp = np.zeros((1, 4, 2, 1024), f)
    spr = np.zeros((1, 4, 64, 64), f)
    spi = np.zeros((1, 4, 64, 64), f)
    ncs = np.zeros((1, 128, 2, 1024), f)
    snr = np.zeros((1, 128, 64, 64), f)
    sni = np.zeros((1, 128, 64, 64), f)
    for c in range(NCORE):
        b, hb = c // 2, c % 2
        r = res[c]
        y_prompt[b, hb * NM:(hb + 1) * NM] = r["ym"]
        y_sample[c * 16:(c + 1) * 16] = r["ysm"].reshape(16, 4, D)
        mk[0, b, hb * 128:(hb + 1) * 128] = r["mk"].reshape(128, 4, 256)
        mv[0, b, hb * 128:(hb + 1) * 128] = r["mv"].reshape(128, 4, 256)
        if hb == 1:
            ncp[0, b] = r["ncp"]
            spr[0, b] = r["spr"].reshape(64, 64)
            spi[0, b] = r["spi"].reshape(64, 64)
        ncs[0, c * 16:(c + 1) * 16] = r["ncs"]
        snr[0, c * 16:(c + 1) * 16] = r["snr"].reshape(16, 64, 64)
        sni[0, c * 16:(c + 1) * 16] = r["sni"].reshape(16, 64, 64)
    return (y_prompt, y_sample, mk, mv, ncp, spr, spi, ncs, snr, sni)


def kernel(**inputs):
    inp = {k: np.asarray(v) for k, v in inputs.items()}
    res = run(inp)
    return assemble(res)
```

```python
import math
import numpy as np
import concourse.bass as bass
import concourse.mybir as mybir
from concourse.bass_utils import run_bass_kernel_spmd
from contextlib import ExitStack

F32 = mybir.dt.float32
BF16 = mybir.dt.bfloat16
I32 = mybir.dt.int32
AF = mybir.ActivationFunctionType
ALU = mybir.AluOpType

D = 2048
NIN = 14336
NCORE = 8
EPS = 1e-6
NM = 1024
NS = 64
NX = NM + NS + 2
S0 = NM
H0 = NM + NS
C_CB, C_CC, C_CH, C_CZ, C_SU, C_SZ, C_AQ, C_AZ, C_G = 0, 1024, 2048, 3072, 4096, 5120, 6144, 7168, 8192


class Res:
    __slots__ = ("w", "r", "name")

    def __init__(self, name=""):
        self.w = None
        self.r = []
        self.name = name


class Op:
    __slots__ = ("q", "fn", "deps", "flag", "val", "dma", "idx", "alldeps", "cost", "lat", "seq", "seg", "waits")


DEF_COST = {"pe": 0.15, "act": 0.8, "dve": 0.8, "pool": 1.0, "sp": 0.1}
RESCHED = True
KEEP_ORDER = ("pe",)
PE_FREE_SEGS = ()


class Sched:
    QS = ["pe", "act", "dve", "pool", "sp"]

    def __init__(self):
        self.all_ops = []
        self.streams = {}

    def op(self, q, fn, reads=(), writes=(), dma=None, after=(), cost=None, lat=0.0):
        o = Op()
        o.q = q
        o.fn = fn
        o.flag = False
        o.dma = dma
        o.val = 0
        o.idx = 0
        o.cost = cost if cost is not None else getattr(fn, "cost", None)
        if o.cost is None:
            o.cost = DEF_COST[q]
        o.lat = lat
        deps = []
        seen = set()

        def add(d):
            if d is None or id(d) in seen:
                return
            seen.add(id(d))
            deps.append(d)

        for r in reads:
            add(r.w)
        for w in writes:
            add(w.w)
            for x in w.r:
                add(x)
        for d in after:
            add(d)
        o.alldeps = deps
        for r in reads:
            r.r.append(o)
        for w in writes:
            w.w = o
            w.r = []
        if dma is not None:
            self.streams[dma] = True
        o.seq = len(self.all_ops)
        self.all_ops.append(o)
        return o

    def barrier(self):
        self.all_ops.append(None)

    def _schedule_segment(self, seg):
        import heapq
        inseg = {id(o) for o in seg}
        preds = {id(o): [d for d in o.alldeps if id(d) in inseg] for o in seg}
        last_stream = {}
        for o in seg:
            if o.dma is not None:
                p = last_stream.get(o.dma)
                if p is not None and all(id(p) != id(x) for x in preds[id(o)]):
                    preds[id(o)].append(p)
                last_stream[o.dma] = o
        last_q = {}
        for o in seg:
            if o.q in KEEP_ORDER and not (o.q == "pe" and self.cur_seg in PE_FREE_SEGS):
                p = last_q.get(o.q)
                if p is not None and all(id(p) != id(x) for x in preds[id(o)]):
                    preds[id(o)].append(p)
                last_q[o.q] = o
        if not RESCHED:
            out = {q: [] for q in self.QS}
            for o in seg:
                out[o.q].append(o)
            return out
        succ = {id(o): [] for o in seg}
        indeg = {}
        for o in seg:
            indeg[id(o)] = len(preds[id(o)])
            for d in preds[id(o)]:
                succ[id(d)].append(o)
        done = {}
        ready = {q: [] for q in self.QS}
        for o in seg:
            if indeg[id(o)] == 0:
                heapq.heappush(ready[o.q], (0.0, o.seq, o))
        qfree = {q: 0.0 for q in self.QS}
        out = {q: [] for q in self.QS}
        n = 0
        while n < len(seg):
            best = None
            for q in self.QS:
                hp = ready[q]
                if not hp:
                    continue
                cands = []
                while hp and hp[0][0] <= qfree[q]:
                    cands.append(heapq.heappop(hp))
                if cands:
                    c = min(cands, key=lambda x: x[1])
                    for x in cands:
                        if x is not c:
                            heapq.heappush(hp, x)
                    heapq.heappush(hp, c)
                    st = qfree[q]
                    key = (st, c[1])
                    pick = c
                else:
                    pick = hp[0]
                    st = pick[0]
                    key = (st, pick[1])
                if best is None or key < best[0]:
                    best = (key, q, pick, st)
            _, q, pick, st = best
            hp = ready[q]
            hp.remove(pick)
            heapq.heapify(hp)
            o = pick[2]
            out[q].append(o)
            qfree[q] = st + o.cost
            done[id(o)] = st + o.cost + o.lat
            n += 1
            for s_ in succ[id(o)]:
                indeg[id(s_)] -= 1
                if indeg[id(s_)] == 0:
                    rt = max(done[id(d)] for d in preds[id(s_)])
                    heapq.heappush(ready[s_.q], (rt, s_.seq, s_))
        self.est_time = getattr(self, "est_time", 0.0) + max(qfree.values())
        return out

    def finalize(self):
        self.queues = {q: [] for q in self.QS}
        segs = [[]]
        for o in self.all_ops:
            if o is None:
                segs.append([])
            else:
                segs[-1].append(o)
        for si, seg in enumerate(segs):
            for o in seg:
                o.seg = si
            self.cur_seg = si
            out = self._schedule_segment(seg)
            for q in self.QS:
                self.queues[q].extend(out[q])
            if si < len(segs) - 1:
                lasts = []
                for q in self.QS:
                    for o in reversed(self.queues[q]):
                        if o.dma is None and o.fn is not None:
                            lasts.append(o)
                            break
                lastdma = {}
                for q in self.QS:
                    for o in self.queues[q]:
                        if o.dma is not None:
                            lastdma[o.dma] = o
                lasts += list(lastdma.values())
                for q in self.QS:
                    b = Op()
                    b.q = q
                    b.fn = None
                    b.flag = False
                    b.dma = None
                    b.val = 0
                    b.idx = 0
                    b.seg = -1
                    b.alldeps = lasts
                    self.queues[q].append(b)
        cnt = {}
        for q in self.QS:
            for o in self.queues[q]:
                if o.dma is not None:
                    cnt[o.dma] = cnt.get(o.dma, 0) + 1
                    o.idx = cnt[o.dma]
        pos = {}
        for q in self.QS:
            for i_, o in enumerate(self.queues[q]):
                pos[id(o)] = i_
        for q in self.QS:
            for o in self.queues[q]:
                best = {}
                for d in o.alldeps:
                    if o.seg != -1 and d.seg != o.seg:
                        continue
                    if d.dma is None and d.q == "pe" and q == "pe":
                        continue
                    key = d.q if d.dma is None else ("dma", d.dma)
                    if key not in best or pos[id(d)] > pos[id(best[key])]:
                        best[key] = d
                o.waits = list(best.values())
                for d in o.waits:
                    if d.dma is None:
                        d.flag = True
        for q in self.QS:
            c = 0
            for o in self.queues[q]:
                if o.dma is None and o.flag:
                    c += 1
                    o.val = c

    def emit(self, nc, stack):
        self.finalize()
        qsem = {q: stack.enter_context(nc.semaphore("q_" + q)) for q in self.QS}
        ssem = {s: stack.enter_context(nc.semaphore("s_" + s)) for s in self.streams}
        block = stack.enter_context(nc.Block())

        def tok(d):
            if d.dma is not None:
                return ssem[d.dma], 16 * d.idx
            return qsem[d.q], d.val

        def run(q, eng):
            waited = {}
            for o in self.queues[q]:
                for d in o.waits:
                    sem, val = tok(d)
                    if waited.get(id(sem), 0) >= val:
                        continue
                    eng.wait_ge(sem, val)
                    waited[id(sem)] = val
                if o.fn is None:
                    continue
                ins = o.fn(eng)
                if o.dma is not None:
                    ins.then_inc(ssem[o.dma], 16)
                elif o.flag:
                    ins.then_inc(qsem[q], 1)
            fin = {}
            for o in self.queues[q]:
                if o.dma is not None:
                    fin[o.dma] = max(fin.get(o.dma, 0), o.idx)
            for s, n in fin.items():
                eng.wait_ge(ssem[s], 16 * n)

        @block.tensor
        def _(e):
            run("pe", e)

        @block.scalar
        def _(e):
            run("act", e)

        @block.vector
        def _(e):
            run("dve", e)

        @block.gpsimd
        def _(e):
            run("pool", e)

        @block.sync
        def _(e):
            run("sp", e)


class Builder:
    def __init__(self, dbg=()):
        self.dbg = set(dbg)
        self.nc = bass.Bass("TRN2", target_bir_lowering=False)
        self.S = Sched()
        self.stack = ExitStack()
        self.ins = {}
        self.outs = {}
        self.rr = 0

    def din(self, name, shape, dt=F32):
        t = self.nc.dram_tensor(name, list(shape), dt, kind="ExternalInput").ap()
        self.ins[name] = t
        return t

    def dout(self, name, shape, dt=F32):
        t = self.nc.dram_tensor(name, list(shape), dt, kind="ExternalOutput").ap()
        self.outs[name] = t
        return t

    def sb(self, name, shape, dt=F32):
        return self.stack.enter_context(self.nc.sbuf_tensor(name, list(shape), dt))

    def ps(self, name, shape, dt=F32):
        return self.stack.enter_context(self.nc.psum_tensor(name, list(shape), dt))

    def pe(self, fn, reads=(), writes=(), drain=False):
        after = [self.last_pe] if (drain and getattr(self, "last_pe", None) is not None) else []
        o = self.S.op("pe", fn, reads, writes, after=after)
        self.last_pe = o
        return o

    def act(self, fn, reads=(), writes=()):
        return self.S.op("act", fn, reads, writes)

    def dve(self, fn, reads=(), writes=()):
        return self.S.op("dve", fn, reads, writes)

    def pool(self, fn, reads=(), writes=()):
        return self.S.op("pool", fn, reads, writes)

    def alt(self, fn, reads=(), writes=()):
        self.rr ^= 1
        return self.S.op("act" if self.rr else "dve", fn, reads, writes)

    def dma(self, out, in_, reads=(), writes=(), stream="ld", q="sp", **kw):
        n = 1
        for x in out.shape:
            n *= x
        lat = 2.5 + n * 4 / 200e3
        cost = 1.0 if q == "pool" else 0.15
        if kw.get("allow_slow_non_contiguous"):
            lat += n * 0.004
        return self.S.op(q, lambda e: e.dma_start(out=out, in_=in_, **kw), reads, writes, dma=stream, cost=cost, lat=lat)

    def dump(self, name, ap_sb, res, shape, dt=F32):
        if name not in self.dbg:
            return
        o = self.dout("dbg_" + name, shape, dt)
        self.dma(o, ap_sb, reads=[res], stream="dbg")


class Arena:
    def __init__(self, B, words):
        self.t = B.sb("arena", [128, words], F32)
        self.words = words
        self.top = 0

    def alloc(self, shape, dt=F32):
        n = 1
        for x in shape:
            n *= x
        w = n if dt == F32 else (n + 1) // 2
        w = (w + 7) // 8 * 8
        off = self.top
        self.top += w
        assert self.top <= self.words, f"arena overflow {self.top} > {self.words}"
        v = self.t[:, off:off + w]
        if dt != F32:
            v = v.bitcast(dt)
        v = v[:, 0:n]
        if len(shape) == 2:
            v = v.rearrange("p (a b) -> p a b", a=shape[0])
        elif len(shape) == 3:
            v = v.rearrange("p (a b c) -> p a b c", a=shape[0], b=shape[1])
        elif len(shape) == 4:
            v = v.rearrange("p (a b c d) -> p a b c d", a=shape[0], b=shape[1], c=shape[2])
        return v


def cp(e, out, in_):
    if hasattr(e, "tensor_copy"):
        return e.tensor_copy(out=out, in_=in_)
    return e.activation(out=out, in_=in_, func=AF.Copy)


def mm(out, lhsT, rhs, start, stop):
    f = lambda e: e.matmul(out, lhsT, rhs, start=start, stop=stop)
    n = rhs.shape[-1]
    f.cost = 0.03 + max(n, 64) * (4 if rhs.dtype == F32 else 1) / 2000.0
    return f


MAGIC = 12582912.0
TWO_PI = 2.0 * math.pi
PI_S = 3.1415925
BIGW = 8704
ARENA_W = 23300
NJ = 272
SCR1_W = 2112
YW = 144


def build(dbg=(), stop=None, nloop=8):
    B = Builder(dbg)

    def finish():
        B.S.emit(B.nc, B.stack)
        B.stack.close()
        return B

    nc = B.nc
    R = Res
    xm = B.din("xm", [NM, D])
    xp = B.din("xp", [NM, D])
    xs = B.din("xs", [NS, D])
    mem = B.din("mem", [256, D])
    ck = B.din("ck", [16, 256, 1024])
    cv = B.din("cv", [16, 256, 1024])
    sconv = B.din("sconv", [16, 2, 1024])
    ssr = B.din("ssr", [16, 4096])
    ssi = B.din("ssi", [16, 4096])
    norm_g = B.din("norm_g", [D])
    mem_norm_g = B.din("mem_norm_g", [D])
    final_g = B.din("final_g", [D])
    w_in = B.din("w_in", [D, NIN])
    conv_w = B.din("conv_w", [3, 1024])
    w_conv_out = B.din("w_conv_out", [1024, D])
    lam_re = B.din("lam_re", [64, 64])
    lam_im = B.din("lam_im", [64, 64])
    log_dt = B.din("log_dt", [64])
    b_re = B.din("b_re", [64, 64, 16])
    b_im = B.din("b_im", [64, 64, 16])
    c_re = B.din("c_re", [64, 16, 64])
    c_im = B.din("c_im", [64, 16, 64])
    ssm_d = B.din("ssm_d", [1024])
    w_glu_a = B.din("w_glu_a", [1024, D])
    w_glu_b = B.din("w_glu_b", [1024, D])
    w_mem_k = B.din("w_mem_k", [D, 1024])
    w_mem_v = B.din("w_mem_v", [D, 1024])
    w_attn_out = B.din("w_attn_out", [1024, D])
    w_out = B.din("w_out", [D, D])

    ym_o = B.dout("ym", [NM, D])
    ys_o = B.dout("ysm", [NS, D])
    mk_o = B.dout("mk", [128, 1024])
    mv_o = B.dout("mv", [128, 1024])
    ncp_o = B.dout("ncp", [2, 1024])
    spr_o = B.dout("spr", [4096])
    spi_o = B.dout("spi", [4096])
    ncs_o = B.dout("ncs", [16, 2, 1024])
    snr_o = B.dout("snr", [16, 4096])
    sni_o = B.dout("sni", [16, 4096])

    scr1 = [nc.dram_tensor(f"scr1_{i}", [128, SCR1_W], BF16).ap() for i in range(2)]
    scr2 = [nc.dram_tensor(f"scr2_{i}", [128, 8 * YW], F32).ap() for i in range(2)]

    def DAP(t, off, pat):
        return bass.AP(t.tensor, off, pat)

    hT = B.sb("hT", [128, 16, NX], BF16)
    BIG = B.sb("BIG", [128, BIGW], F32)
    wbs = [B.sb(f"wb{i}", [128, 2048], BF16) for i in range(3)]
    NWB = 3
    identb = B.sb("identb", [128, 128], BF16)
    identf = B.sb("identf", [128, 128], F32)
    onesb = B.sb("onesb", [128, 128], BF16)
    cact = B.sb("cact", [128, 8, 1088], BF16)
    ysT = B.sb("ysT", [128, 8, 1088], BF16)
    stat = B.sb("stat", [128, 64], F32)
    pall = B.ps("pall", [128, 8, 512], F32)
    AR = Arena(B, ARENA_W)

    hTp = BIG[:, 0:8192].bitcast(BF16).rearrange("p (a b) -> p a b", a=16)
    merged = BIG[:, 0:BIGW].bitcast(BF16).rearrange("p (a b) -> p a b", a=16)

    r_pb = [R(f"pb{i}") for i in range(8)]
    r_hT = [R(f"hT{i}") for i in range(16)]
    r_ident = R("ident")
    r_wb = [R(f"wb{i}") for i in range(3)]
    r_stat = [R(f"st{i}") for i in range(32)]
    r_cact = [R(f"cact{i}") for i in range(8)]
    r_ys = [R(f"ys{i}") for i in range(8)]

    fctr = [0]

    def fence(q, reads):
        r = R("fence")
        i = fctr[0] % 2
        fctr[0] += 1
        if q == "act":
            B.act(lambda e: e.activation(out=stat[0:1, 56 + i:57 + i], in_=stat[0:1, 58:59], func=AF.Copy), reads=reads,
                  writes=[r])
        else:
            B.dve(lambda e: e.tensor_copy(out=stat[0:1, 60 + i:61 + i], in_=stat[0:1, 59:60]), reads=reads, writes=[r])
        return r

    def bank(b, n=512):
        return pall[:, b, 0:n]

    def pset(s):
        return pall[:, 3 * s:3 * s + 3, :].rearrange("p b c -> p (b c)")

    B.pool(lambda e: e.memset(identf[:], 0.0), writes=[r_ident])
    B.pool(lambda e: e.affine_select(out=identf[:], in_=identf[:], pattern=[[-1, 128]],
                                     compare_op=ALU.not_equal, fill=1.0, base=0,
                                     channel_multiplier=1), reads=[r_ident], writes=[r_ident])
    B.dve(lambda e: e.tensor_copy(out=identb[:], in_=identf[:]), reads=[r_ident], writes=[r_ident])
    B.dve(lambda e: e.memset(onesb[:], 1.0), writes=[r_ident])
    B.dve(lambda e: e.memset(stat[:], 0.0), writes=r_stat)

    wcount = [0]

    def load_w(parts, kt):
        assert len(parts) == 1
        p = parts[0]
        sl = wcount[0] % NWB
        wcount[0] += 1
        assert p.shape[1] == 128 and kt * 128 <= 2048
        v = wbs[sl][:, 0:kt * 128].rearrange("p (a c) -> p a c", a=kt)
        B.dma(v, p.rearrange("(a p) c -> p a c", p=128), writes=[r_wb[sl]], stream=f"w{sl}", q="pool")
        return v, r_wb[sl]

    setctr = [0]

    def project(wv, r_w, col, nkt, act, r_act, segs, s=None):
        if s is None:
            s = setctr[0] % 2
            setctr[0] += 1
        for kt in range(nkt):
            for (c0, n, bi) in segs:
                bk = 3 * s + bi
                B.pe(mm(pall[:, bk, 0:n], wv[:, kt, col:col + 128], act[:, kt, c0:c0 + n], kt == 0, kt == nkt - 1),
                     reads=[r_w, r_act[kt]], writes=[r_pb[bk]])
        return s

    SEG_MS = [(0, 512, 0), (512, 512, 1), (1024, NS, 2)]
    SEG_MSH = [(0, 512, 0), (512, 512, 1), (1024, NS + 2, 2)]

    def rset(s):
        return [r_pb[3 * s], r_pb[3 * s + 1], r_pb[3 * s + 2]]

    def norm_transpose(tiles, gsrc, AR_):
        xst = [AR_.alloc([D]) for _ in range(2)]
        xbf = [AR_.alloc([D], BF16) for _ in range(2)]
        junk = AR_.alloc([D], BF16)
        gbc = AR_.alloc([D])
        r_xst = [R(), R()]
        r_xbf = [R(), R()]
        r_junk = R()
        r_gbc = R()
        B.dma(gbc, gsrc.partition_broadcast(128), writes=[r_gbc], stream="gbc")
        for ti, (src, rows, dstT, r_dst, col0) in enumerate(tiles):
            sl = ti % 2
            xt, xb = xst[sl], xbf[sl]
            B.dma(xt[:rows, :], src, writes=[r_xst[sl]], stream=f"xld{sl}")
            si = ti % 32
            sq = stat[:rows, 2 * si:2 * si + 1]
            rs = stat[:rows, 2 * si + 1:2 * si + 2]
            r_s = r_stat[si]
            B.act(lambda e, xt=xt, rows=rows, sq=sq: e.activation(out=junk[:rows, :], in_=xt[:rows, :], func=AF.Square,
                                                                 accum_out=sq),
                  reads=[r_xst[sl]], writes=[r_junk, r_s])
            B.act(lambda e, sq=sq, rs=rs: e.activation(out=rs, in_=sq, func=AF.Sqrt, scale=1.0 / D, bias=EPS),
                  reads=[r_s], writes=[r_s])
            B.dve(lambda e, rs=rs: e.reciprocal(out=rs, in_=rs), reads=[r_s], writes=[r_s])
            B.dve(lambda e, xt=xt, xb=xb, rows=rows, rs=rs: e.scalar_tensor_tensor(
                out=xb[:rows, :], in0=xt[:rows, :], scalar=rs, in1=gbc[:rows, :], op0=ALU.mult, op1=ALU.mult),
                reads=[r_xst[sl], r_s, r_gbc], writes=[r_xbf[sl]])
            for g4 in range(4):
                pbk = 6 + (ti * 4 + g4) % 2
                pv = pall[:, pbk, :].bitcast(BF16)
                for i in range(4):
                    dt_ = g4 * 4 + i
                    B.pe(lambda e, pv=pv, xb=xb, rows=rows, dt_=dt_, i=i: e.transpose(
                        out=pv[:, i * 128:i * 128 + rows], in_=xb[:rows, dt_ * 128:(dt_ + 1) * 128],
                        identity=identb[:rows, :rows]),
                        reads=[r_xbf[sl], r_ident], writes=[r_pb[pbk]])
                B.alt(lambda e, pv=pv, dstT=dstT, g4=g4, rows=rows, col0=col0: cp(
                    e, dstT[:, g4 * 4:(g4 + 1) * 4, col0:col0 + rows],
                    pv[:, 0:512].rearrange("p (a b) -> p a b", a=4)[:, :, 0:rows]),
                    reads=[r_pb[pbk]], writes=r_dst[g4 * 4:(g4 + 1) * 4])

    suTp = AR.alloc([8, 1024], BF16)
    r_suTp = [R(f"suTp{i}") for i in range(8)]
    names = ["lamr", "lami", "dtl", "dt", "lrdt", "mag", "imag", "ang", "angw", "sn", "cs", "ar", "ai", "air", "aii",
             "den", "rden", "am1", "fr", "fi", "tA", "tB", "phi8", "p8s", "r8", "tC"]
    P = {n: AR.alloc([32]) for n in names}
    rP = {n: R(n) for n in names}
    Ppr = AR.alloc([17, 32])
    Ppi = AR.alloc([17, 32])
    Pmr = AR.alloc([8, 32])
    Pmi = AR.alloc([8, 32])
    r_Pp = R("Pp")
    r_Pm = R("Pm")
    Bbr = AR.alloc([32, 16])
    Bbi = AR.alloc([32, 16])
    cnr = AR.alloc([32, 16])
    cni = AR.alloc([32, 16])
    r_Bb = R("Bb")
    r_cn = R("cn")
    Dcol = AR.alloc([64])
    cw = AR.alloc([8, 3])
    jtab = AR.alloc([256])
    mask1 = AR.alloc([128])
    r_const = R("const")
    sconvT = AR.alloc([8, 16, 2])
    ncp = AR.alloc([8, 2])
    ncs = AR.alloc([8, 16, 2])
    r_sconvT = R("sconvT")
    r_ncp = R("ncp")
    r_ncs = R("ncs")
    SinTr = AR.alloc([32, 16])
    SinTi = AR.alloc([32, 16])
    Spr = AR.alloc([32, 16], BF16)
    Spi = AR.alloc([32, 16], BF16)
    ZSr = AR.alloc([32, 16])
    ZSi = AR.alloc([32, 16])
    SFr = AR.alloc([32])
    SFi = AR.alloc([32])
    r_SinT = R("SinT")
    r_Sp = R("Sp")
    r_ZS = [R(f"ZS{i}") for i in range(8)]
    r_SF = [R(f"SF{i}") for i in range(8)]
    ld_state = {"stream": "par0", "ops": [], "res": []}

    def ld(dst, src, res, q="act", **kw):
        o = B.dma(dst, src, writes=[res], stream=ld_state["stream"], q=q, **kw)
        ld_state["ops"].append(o)
        ld_state["res"].append(res)

    def ld_group_done():
        last = ld_state["ops"][-1]
        for r_ in ld_state["res"]:
            r_.w = last
        ld_state["ops"], ld_state["res"] = [], []

    nce = dict(allow_slow_non_contiguous=True)
    stf = ysT[:, :, :].rearrange("p a b -> p (a b)").bitcast(F32)
    stc = cact[:, :, :].rearrange("p a b -> p (a b)").bitcast(F32)
    r_stf = R("stf")
    r_stc = R("stc")
    ld_state["stream"] = "par0"
    q0 = "sp"
    ld(stf[0:32, 0:128], lam_re.rearrange("(a b) p -> a (b p)", b=2), r_stf, q=q0)
    ld(stf[0:32, 128:256], lam_im.rearrange("(a b) p -> a (b p)", b=2), r_stf, q=q0)
    ld(stf[0:32, 384:386], log_dt.rearrange("(a b) -> a b", b=2), r_stf, q=q0)
    ld(stf[0:64, 512:528], ssm_d.rearrange("(g c) -> g c", c=16), r_stf, q=q0)
    for i in range(3):
        ld(cw[:, :, i], DAP(conv_w, i * 1024, [[1, 128], [128, 8]]), r_const, q=q0, **nce)
    ld(stf[0:32, 2048:3072], sconv.rearrange("s r c -> (s r) c"), r_stf, q=q0)
    ld_group_done()
    B.dve(lambda e: e.tensor_copy(out=stf[0:32, 256:384].rearrange("p (a b) -> p a b", a=2),
                                  in_=stf[0:32, 384:386].unsqueeze(2).to_broadcast([32, 2, 64])), reads=[r_stf], writes=[r_stf])
    B.dve(lambda e: e.tensor_copy(out=stf[0:64, 640:768].rearrange("p (a b) -> p a b", a=8),
                                  in_=stf[0:64, 512:528].unsqueeze(1).to_broadcast([64, 8, 16])), reads=[r_stf], writes=[r_stf])

    def ptr(out_ps, in_sb, rows, res_in, bk):
        B.pe(lambda e: e.transpose(out=out_ps, in_=in_sb, identity=identf[0:rows, 0:rows]), reads=[res_in, r_ident],
             writes=[r_pb[bk]])

    for i, nm in enumerate(("lamr", "lami", "dtl")):
        ptr(pall[:, 6, i * 32:(i + 1) * 32], stf[0:32, i * 128:(i + 1) * 128], 32, r_stf, 6)
    ptr(pall[:, 6, 128:192], stf[0:64, 640:768], 64, r_stf, 6)
    for i, nm in enumerate(("lamr", "lami", "dtl")):
        B.alt(lambda e, i=i, nm=nm: cp(e, P[nm], pall[:, 6, i * 32:(i + 1) * 32]), reads=[r_pb[6]], writes=[rP[nm]])
    B.alt(lambda e: cp(e, Dcol, pall[:, 6, 128:192]), reads=[r_pb[6]], writes=[r_const])
    if stop == "e1":
        return finish()
    for t in range(8):
        ptr(pall[:, 7, 64 + t * 32:64 + (t + 1) * 32], stf[0:32, 2048 + t * 128:2048 + (t + 1) * 128], 32, r_stf, 7)
    B.alt(lambda e: cp(e, sconvT.rearrange("p t s r -> p t (s r)"), pall[:, 7, 64:320].rearrange("p (a b) -> p a b", a=8)),
          reads=[r_pb[7]], writes=[r_sconvT])
    if stop == "e2":
        return finish()
    if stop == "e3":
        return finish()
    p1_top = AR.top
    r_hTp = [R(f"hTp{i}") for i in range(16)]
    xm_v = xm.rearrange("(j k) d -> k j d", k=8)
    xp_v = xp.rearrange("(j k) d -> k j d", k=8)
    tiles = []
    for k in range(8):
        tiles.append((xp_v[k], 128, hTp, r_hTp, k * 128))
    for k in range(8):
        tiles.append((xm_v[k], 128, hT, r_hT, k * 128))
    tiles.append((xs, NS, hT, r_hT, S0))
    norm_transpose(tiles, norm_g, AR)
    B.dve(lambda e: e.tensor_copy(out=hT[:, :, H0:H0 + 2], in_=hTp[:, :, 6 * 128 + 127:8 * 128:128]),
          reads=r_hTp, writes=r_hT)
    for t in range(8):
        wv, r_w = load_w([w_in[:, C_SU + t * 128:C_SU + (t + 1) * 128]], 16)
        s = project(wv, r_w, 0, 16, hTp, r_hTp, [(0, 512, 0), (512, 512, 1)])
        B.alt(lambda e, s=s, t=t: cp(e, suTp[:, t, :], pset(s)[:, 0:1024]), reads=rset(s)[0:2], writes=[r_suTp[t]])
    ld_state["stream"] = "par2"
    for (src_, base) in ((c_re, 0), (c_im, 1024)):
        st3 = stc[0:64, base:base + 1024].rearrange("p (t x) -> p t x", t=8)
        for t in range(8):
            for gpl in range(4):
                ld(st3[gpl * 16:(gpl + 1) * 16, t, :].rearrange("p (a b) -> p a b", a=2),
                   DAP(src_, (8 * t + 2 * gpl) * 1024, [[64, 16], [1024, 2], [1, 64]]), r_stc, q=q0)
    ld_group_done()
    for (dstC, base, bk) in ((cnr, 0, 6), (cni, 1024, 7)):
        for t in range(8):
            ptr(pall[:, bk, t * 64:(t + 1) * 64], stc[0:64, base + t * 128:base + (t + 1) * 128], 64, r_stc, bk)
        B.alt(lambda e, dstC=dstC, bk=bk: cp(e, dstC.rearrange("p a b -> p (a b)"), pall[:, bk, :]), reads=[r_pb[bk]],
              writes=[r_cn])
    B.S.barrier()
    if stop == "p1":
        return finish()

    AR.top = p1_top
    BT = [0]

    def balloc(shape, dt=F32):
        n = 1
        for x in shape:
            n *= x
        w = n if dt == F32 else (n + 1) // 2
        w = (w + 7) // 8 * 8
        off = BT[0]
        BT[0] += w
        assert BT[0] <= BIGW, "BIG overflow"
        v = BIG[:, off:off + w]
        if dt != F32:
            v = v.bitcast(dt)
        v = v[:, 0:n]
        if len(shape) == 2:
            v = v.rearrange("p (a b) -> p a b", a=shape[0])
        elif len(shape) == 3:
            v = v.rearrange("p (a b c) -> p a b c", a=shape[0], b=shape[1])
        return v

    WTr = [AR.alloc([8, 64], BF16) for _ in range(2)]
    WTi = [AR.alloc([8, 64], BF16) for _ in range(2)]
    CS2r = [AR.alloc([8, 128], BF16) for _ in range(2)]
    CS2i = [AR.alloc([8, 128], BF16) for _ in range(2)]
    Mg = [AR.alloc([8, 128], BF16) for _ in range(2)]
    r_WT = [R("WT0"), R("WT1")]
    r_CS2 = [R("CS20"), R("CS21")]
    r_Mg = [R("Mg0"), R("Mg1")]
    Wnr = AR.alloc([128])
    Wni = AR.alloc([128])
    CSnr = AR.alloc([128])
    CSni = AR.alloc([128])
    gt1 = AR.alloc([128])
    gt2 = AR.alloc([128])
    T2r = AR.alloc([128])
    T2i = AR.alloc([128])
    CSzr = AR.alloc([128])
    CSzi = AR.alloc([128])
    hm = AR.alloc([2])
    r_T2 = R("T2")
    r_CSz = R("CSz")
    r_Wn = R("Wn")
    r_CSn = R("CSn")
    r_gt = R("gt")
    rcos = AR.alloc([256])
    rsin = AR.alloc([256])
    rt1 = AR.alloc([256])
    rt2 = AR.alloc([256])
    rzr = AR.alloc([256])
    rzi = AR.alloc([256])
    r_rot = R("rot")
    suTb = AR.alloc([1088], BF16)
    r_suTb = R("suTb")
    r_suTb2 = R("suTb2")
    Ub = [AR.alloc([8, NJ], BF16) for _ in range(2)]
    r_Ub = [[R(f"Ub{a}_{i}") for i in range(20)] for a in range(2)]
    r_scr1 = [[R(), R()] for _ in range(2)]
    r_scr2 = [[R() for _ in range(12)] for _ in range(2)]
    Spvr = [AR.alloc([4, 132], BF16) for _ in range(2)]
    Spvi = [AR.alloc([4, 132], BF16) for _ in range(2)]
    r_Spv = [R("Spv0"), R("Spv1")]
    r_Ysb = R("Ysb")
    print("arena top after S5 allocs", AR.top, "of", ARENA_W)
    Ysb = balloc([8, YW])
    tmp_cc = balloc([NX])
    vext = balloc([10, 128])
    vs = balloc([16, 6])
    vh = balloc([2])
    acc = balloc([1088])
    tmp_s = balloc([1088])
    bnr = balloc([32, 16])
    bni = balloc([32, 16])
    r_tmpcc = R("tmpcc")
    r_vext = R("vext")
    r_acc = R("acc")
    r_acc2 = R("acc2")
    r_tmps = R("tmps")
    r_bn = R("bn")
    print("BIG top", BT[0], "of", BIGW)

    ld_state["stream"] = "par1"
    ld(bnr, DAP(b_re, 0, [[16, 128], [2048, 32], [1, 16]]), r_bn)
    ld(bni, DAP(b_im, 0, [[16, 128], [2048, 32], [1, 16]]), r_bn)
    ld_group_done()
    for q4 in range(4):
        B.dma(acc[0:16, 0:1024], ssr[:, q4 * 1024:(q4 + 1) * 1024], writes=[r_acc], stream="sin0")
        B.dma(tmp_s[0:16, 0:1024], ssi[:, q4 * 1024:(q4 + 1) * 1024], writes=[r_tmps], stream="sin1")
        for gpl in range(8):
            B.pe(lambda e, gpl=gpl: e.transpose(out=pall[:, 6, gpl * 16:(gpl + 1) * 16], in_=acc[0:16, gpl * 128:(gpl + 1) * 128],
                                                identity=identf[0:16, 0:16]), reads=[r_acc, r_ident], writes=[r_pb[6]])
            B.pe(lambda e, gpl=gpl: e.transpose(out=pall[:, 7, gpl * 16:(gpl + 1) * 16], in_=tmp_s[0:16, gpl * 128:(gpl + 1) * 128],
                                                identity=identf[0:16, 0:16]), reads=[r_tmps, r_ident], writes=[r_pb[7]])
        B.act(lambda e, q4=q4: cp(e, SinTr[:, q4 * 8:(q4 + 1) * 8, :].rearrange("p a b -> p (a b)"), pall[:, 6, 0:128]),
              reads=[r_pb[6]], writes=[r_SinT])
        B.dve(lambda e, q4=q4: cp(e, SinTi[:, q4 * 8:(q4 + 1) * 8, :].rearrange("p a b -> p (a b)"), pall[:, 7, 0:128]),
              reads=[r_pb[7]], writes=[r_SinT])
    B.pool(lambda e: e.iota(jtab, [[1, 256]], base=0, channel_multiplier=0, allow_small_or_imprecise_dtypes=True),
           writes=[r_const])
    B.pool(lambda e: e.memset(hm[0:64, 0:1], 1.0), writes=[r_const])
    B.pool(lambda e: e.memset(hm[64:128, 0:1], 0.0), writes=[r_const])
    B.pool(lambda e: e.memset(hm[0:64, 1:2], 0.0), writes=[r_const])
    B.pool(lambda e: e.memset(hm[64:128, 1:2], 1.0), writes=[r_const])
    for sl_ in range(2):
        B.pool(lambda e, sl_=sl_: e.memset(Ub[sl_][64:128, :, 256:272], 0.0), writes=[r_Ub[sl_][19]])
    B.pool(lambda e: e.memset(mask1, 1.0), writes=[r_const])
    B.pool(lambda e: e.affine_select(out=mask1, in_=mask1, pattern=[[16, 8], [0, 16]],
                                     compare_op=ALU.is_ge, fill=0.0, base=15, channel_multiplier=-1),
           reads=[r_const], writes=[r_const])

    def tt(out, a, b, op, reads, writes):
        B.dve(lambda e: e.tensor_tensor(out=out, in0=a, in1=b, op=op), reads=reads, writes=writes)

    def ts(out, a, s1, s2, op0, op1, reads, writes):
        if s2 is None:
            B.dve(lambda e: e.tensor_scalar(out=out, in0=a, scalar1=s1, scalar2=None, op0=op0), reads=reads, writes=writes)
        else:
            B.dve(lambda e: e.tensor_scalar(out=out, in0=a, scalar1=s1, scalar2=s2, op0=op0, op1=op1), reads=reads,
                  writes=writes)

    def stt(out, a, s, b, op0, op1, reads, writes):
        B.dve(lambda e: e.scalar_tensor_tensor(out=out, in0=a, scalar=s, in1=b, op0=op0, op1=op1), reads=reads,
              writes=writes)

    def actf(out, a, func, reads, writes, scale=1.0, bias=0.0):
        B.act(lambda e: e.activation(out=out, in_=a, func=func, scale=scale, bias=bias), reads=reads, writes=writes)

    def pp(n):
        return P[n], rP[n]

    def ptt(o, a, b, op):
        tt(P[o], P[a], P[b], op, [rP[a], rP[b]], [rP[o]])

    def wrap(o, a):
        ts(P["tC"], P[a], 1.0 / TWO_PI, MAGIC, ALU.mult, ALU.add, [rP[a]], [rP["tC"]])
        ts(P["tC"], P["tC"], MAGIC, None, ALU.subtract, None, [rP["tC"]], [rP["tC"]])
        stt(P[o], P["tC"], -TWO_PI, P[a], ALU.mult, ALU.add, [rP["tC"], rP[a]], [rP[o]])
        ts(P[o], P[o], -PI_S, PI_S, ALU.max, ALU.min, [rP[o]], [rP[o]])

    actf(P["dt"], P["dtl"], AF.Exp, [rP["dtl"]], [rP["dt"]])
    ptt("lrdt", "lamr", "dt", ALU.mult)
    actf(P["mag"], P["lrdt"], AF.Exp, [rP["lrdt"]], [rP["mag"]])
    actf(P["imag"], P["lrdt"], AF.Exp, [rP["lrdt"]], [rP["imag"]], scale=-1.0)
    actf(P["r8"], P["lrdt"], AF.Exp, [rP["lrdt"]], [rP["r8"]], scale=8.0)
    ptt("ang", "lami", "dt", ALU.mult)
    wrap("angw", "ang")
    actf(P["sn"], P["angw"], AF.Sin, [rP["angw"]], [rP["sn"]])
    actf(P["tA"], P["angw"], AF.Abs, [rP["angw"]], [rP["tA"]])
    actf(P["cs"], P["tA"], AF.Sin, [rP["tA"]], [rP["cs"]], scale=-1.0, bias=math.pi / 2)
    ptt("ar", "mag", "cs", ALU.mult)
    ptt("ai", "mag", "sn", ALU.mult)
    ptt("air", "imag", "cs", ALU.mult)
    stt(P["aii"], P["imag"], -1.0, P["sn"], ALU.mult, ALU.mult, [rP["imag"], rP["sn"]], [rP["aii"]])
    ts(P["tB"], P["angw"], 8.0, None, ALU.mult, None, [rP["angw"]], [rP["tB"]])
    wrap("phi8", "tB")
    ts(P["p8s"], P["phi8"], 1.0 / TWO_PI, None, ALU.mult, None, [rP["phi8"]], [rP["p8s"]])
    ptt("den", "lamr", "lamr", ALU.mult)
    ptt("tA", "lami", "lami", ALU.mult)
    ptt("den", "den", "tA", ALU.add)
    B.dve(lambda e: e.reciprocal(out=P["rden"], in_=P["den"]), reads=[rP["den"]], writes=[rP["rden"]])
    ts(P["am1"], P["ar"], -1.0, None, ALU.add, None, [rP["ar"]], [rP["am1"]])
    ptt("tA", "am1", "lamr", ALU.mult)
    ptt("tB", "ai", "lami", ALU.mult)
    ptt("tA", "tA", "tB", ALU.add)
    ptt("fr", "tA", "rden", ALU.mult)
    ptt("tA", "ai", "lamr", ALU.mult)
    ptt("tB", "am1", "lami", ALU.mult)
    ptt("tA", "tA", "tB", ALU.subtract)
    ptt("fi", "tA", "rden", ALU.mult)
    frb = P["fr"].unsqueeze(2).to_broadcast([128, 32, 16])
    fib = P["fi"].unsqueeze(2).to_broadcast([128, 32, 16])
    t16a = balloc([32, 16])
    t16b = balloc([32, 16])
    r_t16 = R("t16")
    tt(t16a, bnr, frb, ALU.mult, [r_bn, rP["fr"]], [r_t16])
    tt(t16b, bni, fib, ALU.mult, [r_bn, rP["fi"]], [r_t16])
    tt(Bbr, t16a, t16b, ALU.subtract, [r_t16], [r_Bb])
    tt(t16a, bni, frb, ALU.mult, [r_bn, rP["fr"]], [r_t16])
    tt(t16b, bnr, fib, ALU.mult, [r_bn, rP["fi"]], [r_t16])
    tt(Bbi, t16a, t16b, ALU.add, [r_t16], [r_Bb])

    def powers(Pr, Pi, r_P, nk, br, bi):
        B.dve(lambda e: e.memset(Pr[:, 0, :], 1.0), writes=[r_P])
        B.dve(lambda e: e.memset(Pi[:, 0, :], 0.0), writes=[r_P])
        for k in range(1, nk):
            tt(P["tA"], Pr[:, k - 1, :], P[br], ALU.mult, [r_P, rP[br]], [rP["tA"]])
            tt(P["tB"], Pi[:, k - 1, :], P[bi], ALU.mult, [r_P, rP[bi]], [rP["tB"]])
            tt(Pr[:, k, :], P["tA"], P["tB"], ALU.subtract, [rP["tA"], rP["tB"]], [r_P])
            tt(P["tA"], Pr[:, k - 1, :], P[bi], ALU.mult, [r_P, rP[bi]], [rP["tA"]])
            tt(P["tB"], Pi[:, k - 1, :], P[br], ALU.mult, [r_P, rP[br]], [rP["tB"]])
            tt(Pi[:, k, :], P["tA"], P["tB"], ALU.add, [rP["tA"], rP["tB"]], [r_P])

    powers(Ppr, Ppi, r_Pp, 17, "ar", "ai")
    powers(Pmr, Pmi, r_Pm, 8, "air", "aii")

    def cmul_b(outr, outi, ar_, ai_, br_, bi_, t1, t2, reads, r_t, w_r, w_i, neg_i=False):
        tt(t1, ar_, br_, ALU.mult, reads, [r_t])
        tt(t2, ai_, bi_, ALU.mult, reads, [r_t])
        tt(outr, t1, t2, ALU.subtract, [r_t], w_r)
        tt(t1, ar_, bi_, ALU.mult, reads, [r_t])
        tt(t2, ai_, br_, ALU.mult, reads, [r_t])
        if neg_i:
            stt(outi, t1, -1.0, t2, ALU.mult, ALU.subtract, [r_t], w_i)
        else:
            tt(outi, t1, t2, ALU.add, [r_t], w_i)

    t16c, t16d, r_t16b = t16a, t16b, r_t16
    a7r = Pmr[:, 7, :].unsqueeze(2).to_broadcast([128, 32, 16])
    a7i = Pmi[:, 7, :].unsqueeze(2).to_broadcast([128, 32, 16])
    cmul_b(Spr, Spi, SinTr, SinTi, a7r, a7i, t16c, t16d, [r_SinT, r_Pm], r_t16b, [r_Sp], [r_Sp])

    def stage_G(t):
        sl = t % 2
        for gl in range(4):
            gp = 4 * t + gl
            pmr = Pmr[:, :, gp:gp + 1].to_broadcast([128, 8, 16])
            pmi = Pmi[:, :, gp:gp + 1].to_broadcast([128, 8, 16])
            bbr = Bbr[:, gp:gp + 1, :].to_broadcast([128, 8, 16])
            bbi = Bbi[:, gp:gp + 1, :].to_broadcast([128, 8, 16])
            v3 = lambda x: x.rearrange("p (k c) -> p k c", k=8)
            cmul_b(v3(Wnr), v3(Wni), pmr, pmi, bbr, bbi, v3(gt1), v3(gt2), [r_Pm, r_Bb], r_gt, [r_Wn], [r_Wn])
            ppr = Ppr[:, 0:8, gp:gp + 1].to_broadcast([128, 8, 16])
            ppi = Ppi[:, 0:8, gp:gp + 1].to_broadcast([128, 8, 16])
            cr = cnr[:, gp:gp + 1, :].to_broadcast([128, 8, 16])
            ci = cni[:, gp:gp + 1, :].to_broadcast([128, 8, 16])
            cmul_b(v3(CSnr), v3(CSni), cr, ci, ppr, ppi, v3(gt1), v3(gt2), [r_cn, r_Pp], r_gt, [r_CSn], [r_CSn],
                   neg_i=True)
            ppr2 = Ppr[:, 8:16, gp:gp + 1].to_broadcast([128, 8, 16])
            ppi2 = Ppi[:, 8:16, gp:gp + 1].to_broadcast([128, 8, 16])
            cmul_b(v3(T2r), v3(T2i), cr, ci, ppr2, ppi2, v3(gt1), v3(gt2),
                   [r_cn, r_Pp], r_gt, [r_T2], [r_T2], neg_i=True)
            for g2 in range(2):
                actf(CS2r[sl][:, 2 * gl + g2, :], T2r, AF.Copy, [r_T2, r_const], [r_CS2[sl]], scale=hm[:, g2:g2 + 1])
                actf(CS2i[sl][:, 2 * gl + g2, :], T2i, AF.Copy, [r_T2, r_const], [r_CS2[sl]], scale=hm[:, g2:g2 + 1])
            if stop == "G1":
                return
            bk = 6 + gl % 2
            B.pe(lambda e, bk=bk: e.transpose(out=pall[:, bk, 0:128], in_=Wnr, identity=identf[:]),
                 reads=[r_Wn, r_ident], writes=[r_pb[bk]])
            B.pe(lambda e, bk=bk: e.transpose(out=pall[:, bk, 128:256], in_=Wni, identity=identf[:]),
                 reads=[r_Wn, r_ident], writes=[r_pb[bk]])
            if stop == "G2":
                return
            for g2 in range(2):
                oc = 256 + g2 * 128
                actf(CSzr, CSnr, AF.Copy, [r_CSn, r_const], [r_CSz], scale=hm[:, g2:g2 + 1])
                actf(CSzi, CSni, AF.Copy, [r_CSn, r_const], [r_CSz], scale=hm[:, g2:g2 + 1])
                B.pe(mm(pall[:, bk, oc:oc + 128], Wnr, CSzr, True, False),
                     reads=[r_Wn, r_CSz], writes=[r_pb[bk]])
                B.pe(mm(pall[:, bk, oc:oc + 128], Wni, CSzi, False, True),
                     reads=[r_Wn, r_CSz], writes=[r_pb[bk]])
            if stop == "G3":
                return
            B.alt(lambda e, bk=bk, sl=sl, gl=gl: cp(e, WTr[sl][:, 2 * gl:2 * gl + 2, :],
                                                   pall[:, bk, 0:128].rearrange("p (a b) -> p a b", a=2)),
                  reads=[r_pb[bk]], writes=[r_WT[sl]])
            B.alt(lambda e, bk=bk, sl=sl, gl=gl: cp(e, WTi[sl][:, 2 * gl:2 * gl + 2, :],
                                                   pall[:, bk, 128:256].rearrange("p (a b) -> p a b", a=2)),
                  reads=[r_pb[bk]], writes=[r_WT[sl]])
            if stop == "G4":
                return
            for g2 in range(2):
                g = 2 * gp + g2
                tt(Mg[sl][:, 2 * gl + g2, :], pall[:, bk, 256 + g2 * 128:384 + g2 * 128], mask1, ALU.mult,
                   [r_pb[bk], r_const], [r_Mg[sl]])
                stt(Mg[sl][:, 2 * gl + g2, :], identf[:], Dcol[:, g:g + 1], Mg[sl][:, 2 * gl + g2, :], ALU.mult, ALU.add,
                    [r_ident, r_const, r_Mg[sl]], [r_Mg[sl]])

    def stage_A(t):
        sl = t % 2
        wv, r_w = load_w([w_in[:, C_SU + t * 128:C_SU + (t + 1) * 128]], 16)
        s = project(wv, r_w, 0, 16, hT, r_hT, SEG_MS)
        B.act(lambda e, s=s: cp(e, suTb[:, 0:1024], pset(s)[:, 0:1024]), reads=rset(s), writes=[r_suTb])
        B.act(lambda e, s=s: cp(e, suTb[:, 1024:1088].rearrange("p (k s) -> p s k", k=4),
                                pset(s)[:, 1024:1088].rearrange("p (s k) -> p s k", k=4)), reads=rset(s), writes=[r_suTb2])
        wv, r_w = load_w([w_in[:, C_SZ + t * 128:C_SZ + (t + 1) * 128]], 16)
        s = project(wv, r_w, 0, 16, hT, r_hT, SEG_MS)
        actf(ysT[:, t, :], pset(s)[:, 0:1088], AF.Silu, rset(s), [r_ys[t]])
        sc = scr1[sl]
        r_sc = r_scr1[sl]
        r_f = fence("act", [r_suTb, r_suTb2])
        B.dma(sc[:, 0:1024], suTp[:, t, :], reads=[r_suTp[t]], writes=[r_sc[0]], stream=f"shA{sl}")
        B.dma(sc[:, 1024:2112], suTb, reads=[r_suTb, r_suTb2, r_f], writes=[r_sc[1]], stream=f"shA{sl}")
        di = 0
        for slab in range(2):
            for k in range(8):
                B.dma(Ub[sl][k * 16:(k + 1) * 16, :, slab * 128:(slab + 1) * 128],
                      DAP(sc, slab * 1024 + k * 128, [[SCR1_W, 16], [16 * SCR1_W, 8], [1, 128]]),
                      reads=r_sc, writes=[r_Ub[sl][di]], stream=f"shB{sl}")
                di += 1
        for k in range(4):
            B.dma(Ub[sl][k * 16:(k + 1) * 16, :, 256:272],
                  DAP(sc, 2048 + k * 16, [[SCR1_W, 16], [16 * SCR1_W, 8], [1, 16]]),
                  reads=r_sc, writes=[r_Ub[sl][di]], stream=f"shB{sl}")
            di += 1

    def stage_B(t):
        sl = t % 2
        for gl in range(4):
            gp = 4 * t + gl
            for (WT_, bk) in ((WTr[sl], 6), (WTi[sl], 7)):
                for g2 in range(2):
                    g8 = 2 * gl + g2
                    rows = slice(g2 * 64, (g2 + 1) * 64)
                    B.pe(mm(pall[rows, bk, 0:256], WT_[:, g8, :], Ub[sl][:, g8, 0:256], True, True),
                         reads=[r_WT[sl]] + r_Ub[sl], writes=[r_pb[bk]])
                    B.pe(mm(pall[rows, bk, 256:272], WT_[:, g8, :], Ub[sl][:, g8, 256:272], True, True),
                         reads=[r_WT[sl]] + r_Ub[sl], writes=[r_pb[bk]])
            zr_ps = pall[:, 6, 0:256]
            zi_ps = pall[:, 7, 0:256]
            p8s = P["p8s"][:, gp:gp + 1]
            ph8 = P["phi8"][:, gp:gp + 1]
            actf(rt1, jtab, AF.Identity, [r_const, rP["p8s"]], [r_rot], scale=p8s, bias=MAGIC)
            actf(rt1, rt1, AF.Identity, [r_rot], [r_rot], bias=-MAGIC)
            actf(rt2, jtab, AF.Copy, [r_const, rP["phi8"]], [r_rot], scale=ph8)
            stt(rt2, rt1, -TWO_PI, rt2, ALU.mult, ALU.add, [r_rot], [r_rot])
            ts(rt2, rt2, -PI_S, PI_S, ALU.max, ALU.min, [r_rot], [r_rot])
            actf(rsin, rt2, AF.Sin, [r_rot], [r_rot])
            actf(rt1, rt2, AF.Abs, [r_rot], [r_rot])
            actf(rcos, rt1, AF.Sin, [r_rot], [r_rot], scale=-1.0, bias=math.pi / 2)
            tt(rt1, zr_ps, rcos, ALU.mult, [r_pb[6], r_rot], [r_rot])
            tt(rt2, zi_ps, rsin, ALU.mult, [r_pb[7], r_rot], [r_rot])
            tt(rzr, rt1, rt2, ALU.add, [r_rot], [r_rot])
            tt(rt1, zi_ps, rcos, ALU.mult, [r_pb[7], r_rot], [r_rot])
            tt(rt2, zr_ps, rsin, ALU.mult, [r_pb[6], r_rot], [r_rot])
            tt(rzi, rt1, rt2, ALU.subtract, [r_rot], [r_rot])
            B.dve(lambda e, gp=gp: e.tensor_copy(out=ZSr[:, gp, :], in_=pall[:, 6, 256:272]), reads=[r_pb[6]],
                  writes=[r_ZS[t]])
            B.dve(lambda e, gp=gp: e.tensor_copy(out=ZSi[:, gp, :], in_=pall[:, 7, 256:272]), reads=[r_pb[7]],
                  writes=[r_ZS[t]])
            r8b = P["r8"][:, gp:gp + 1].to_broadcast([128, 256])
            B.dve(lambda e, r8b=r8b: e.tensor_tensor_scan(out=rt1, data0=r8b, data1=rzr, initial=0.0, op0=ALU.mult,
                                                         op1=ALU.add), reads=[r_rot, rP["r8"]], writes=[r_rot])
            B.dve(lambda e, r8b=r8b: e.tensor_tensor_scan(out=rt2, data0=r8b, data1=rzi, initial=0.0, op0=ALU.mult,
                                                         op1=ALU.add), reads=[r_rot, rP["r8"]], writes=[r_rot])
            c_ = rcos[:, 127:256]
            s_ = rsin[:, 127:256]
            tt(rzr[:, 0:129], rt1[:, 127:256], c_, ALU.mult, [r_rot], [r_rot])
            tt(rzi[:, 0:129], rt2[:, 127:256], s_, ALU.mult, [r_rot], [r_rot])
            tt(Spvr[sl][:, gl, 0:129], rzr[:, 0:129], rzi[:, 0:129], ALU.subtract, [r_rot], [r_Spv[sl]])
            B.dve(lambda e, gp=gp: e.tensor_tensor(out=SFr[:, gp:gp + 1], in0=rzr[:, 128:129], in1=rzi[:, 128:129],
                                                   op=ALU.subtract), reads=[r_rot], writes=[r_SF[t]])
            tt(rzr[:, 0:129], rt2[:, 127:256], c_, ALU.mult, [r_rot], [r_rot])
            tt(rzi[:, 0:129], rt1[:, 127:256], s_, ALU.mult, [r_rot], [r_rot])
            tt(Spvi[sl][:, gl, 0:129], rzr[:, 0:129], rzi[:, 0:129], ALU.add, [r_rot], [r_Spv[sl]])
            B.dve(lambda e, gp=gp: e.tensor_tensor(out=SFi[:, gp:gp + 1], in0=rzr[:, 128:129], in1=rzi[:, 128:129],
                                                   op=ALU.add), reads=[r_rot], writes=[r_SF[t]])

    def stage_C(t):
        sl = t % 2
        for g8 in range(8):
            gl = g8 // 2
            bk = 6 + g8 % 2
            c0 = (g8 // 2) * 128
            B.pe(mm(pall[:, bk, c0:c0 + 128], Mg[sl][:, g8, :], Ub[sl][:, g8, 128:256], True, False),
                 reads=[r_Mg[sl]] + r_Ub[sl], writes=[r_pb[bk]], drain=True)
            B.pe(mm(pall[:, bk, c0:c0 + 128], CS2r[sl][:, g8, :], Spvr[sl][:, gl, 0:128], False, False),
                 reads=[r_CS2[sl], r_Spv[sl]], writes=[r_pb[bk]])
            B.pe(mm(pall[:, bk, c0:c0 + 128], CS2i[sl][:, g8, :], Spvi[sl][:, gl, 0:128], False, True),
                 reads=[r_CS2[sl], r_Spv[sl]], writes=[r_pb[bk]])
        for b2 in range(2):
            B.act(lambda e, b2=b2: cp(e, Ysb[:, b2:8:2, 0:128], pall[:, 6 + b2, :].rearrange("p (a b) -> p a b", a=4)),
                  reads=[r_pb[6 + b2]], writes=[r_Ysb])
        for g8 in range(8):
            gl = g8 // 2
            gp = 4 * t + gl
            bk = 6 + g8 % 2
            c0 = (g8 // 2) * 16
            B.pe(mm(pall[:, bk, c0:c0 + 16], Mg[sl][:, g8, :], Ub[sl][:, g8, 256:272], True, False),
                 reads=[r_Mg[sl]] + r_Ub[sl], writes=[r_pb[bk]], drain=True)
            B.pe(mm(pall[:, bk, c0:c0 + 16], CS2r[sl][:, g8, :], Spr[:, gp, :], False, False),
                 reads=[r_CS2[sl], r_Sp], writes=[r_pb[bk]])
            B.pe(mm(pall[:, bk, c0:c0 + 16], CS2i[sl][:, g8, :], Spi[:, gp, :], False, True),
                 reads=[r_CS2[sl], r_Sp], writes=[r_pb[bk]])
        for b2 in range(2):
            B.act(lambda e, b2=b2: cp(e, Ysb[0:64, b2:8:2, 128:144],
                                      pall[0:64, 6 + b2, 0:64].rearrange("p (a b) -> p a b", a=4)),
                  reads=[r_pb[6 + b2]], writes=[r_Ysb])
        sc = scr2[sl]
        r_sc = r_scr2[sl]
        r_f = fence("act", [r_Ysb])
        for k in range(8):
            B.dma(DAP(sc, k * YW, [[8 * YW, 16], [16 * 8 * YW, 8], [1, 128]]), Ysb[k * 16:(k + 1) * 16, :, 0:128],
                  reads=[r_Ysb, r_f], writes=[r_sc[k]], stream=f"shC{sl}")
        for k in range(4):
            B.dma(DAP(sc, k * YW + 128, [[8 * YW, 16], [16 * 8 * YW, 8], [1, 16]]), Ysb[k * 16:(k + 1) * 16, :, 128:144],
                  reads=[r_Ysb, r_f], writes=[r_sc[8 + k]], stream=f"shC{sl}")
        B.dma(acc[:, 0:1024].rearrange("p (k j) -> p k j", k=8), DAP(sc, 0, [[8 * YW, 128], [YW, 8], [1, 128]]),
              reads=r_sc, writes=[r_acc], stream="shD0")
        B.dma(acc[:, 1024:1088].rearrange("p (k s) -> p k s", k=4), DAP(sc, 128, [[8 * YW, 128], [YW, 4], [1, 16]]),
              reads=r_sc, writes=[r_acc2], stream="shD1")
        acs = acc[:, 1024:1088].rearrange("p (k s) -> p s k", k=4)
        yss = ysT[:, t, 1024:1088].rearrange("p (s k) -> p s k", k=4)
        tt(acc[:, 0:1024], acc[:, 0:1024], ysT[:, t, 0:1024], ALU.mult, [r_acc, r_ys[t]], [r_acc])
        tt(acs, acs, yss, ALU.mult, [r_acc2, r_ys[t]], [r_acc2])
        actf(ysT[:, t, 0:1024], acc[:, 0:1024], AF.Gelu_apprx_tanh, [r_acc], [r_ys[t]])
        actf(yss, acs, AF.Gelu_apprx_tanh, [r_acc2, r_ys[t]], [r_ys[t]])

    def stage_conv(t):
        wv, r_w = load_w([w_in[:, C_CC + t * 128:C_CC + (t + 1) * 128]], 16)
        s = project(wv, r_w, 0, 16, hT, r_hT, SEG_MSH)
        actf(tmp_cc, pset(s)[:, 0:NX], AF.Copy, rset(s), [r_tmpcc])
        wv, r_w = load_w([w_in[:, C_CH + t * 128:C_CH + (t + 1) * 128]], 16)
        s = project(wv, r_w, 0, 16, hT, r_hT, SEG_MSH)
        ps = pset(s)
        tt(vext[:, 2:10, :], ps[:, 0:1024].rearrange("p (k j) -> p k j", k=8),
           tmp_cc[:, 0:1024].rearrange("p (k j) -> p k j", k=8), ALU.mult, rset(s) + [r_tmpcc], [r_vext])
        tt(vs[:, :, 2:6], ps[:, 1024:1088].rearrange("p (s k) -> p s k", k=4),
           tmp_cc[:, 1024:1088].rearrange("p (s k) -> p s k", k=4), ALU.mult, rset(s) + [r_tmpcc], [r_vext])
        tt(vh, ps[:, 1088:1090], tmp_cc[:, 1088:1090], ALU.mult, rset(s) + [r_tmpcc], [r_vext])
        cpv = lambda o, i, rd=[r_vext], wr=[r_vext]: B.dve(lambda e: e.tensor_copy(out=o, in_=i), reads=rd, writes=wr)
        cpv(vext[:, 1, 1:128], vext[:, 9, 0:127])
        cpv(vext[:, 0, 1:128], vext[:, 8, 0:127])
        cpv(vext[:, 1, 0:1], vh[:, 1:2])
        cpv(vext[:, 0, 0:1], vh[:, 0:1])
        cpv(vs[:, :, 0:2], sconvT[:, t, :, :], [r_sconvT, r_vext], [r_vext])
        cpv(ncp[:, t, 0:1], vext[:, 8, 127:128], [r_vext], [r_ncp])
        cpv(ncp[:, t, 1:2], vext[:, 9, 127:128], [r_vext], [r_ncp])
        cpv(ncs[:, t, :, :], vs[:, :, 4:6], [r_vext], [r_ncs])
        accm = acc[:, 0:1024].rearrange("p (k j) -> p k j", k=8)
        accs = acc[:, 1024:1088].rearrange("p (s k) -> p s k", k=4)
        ts(accm, vext[:, 2:10, :], cw[:, t, 2:3], None, ALU.mult, None, [r_vext, r_const], [r_acc])
        stt(accm, vext[:, 1:9, :], cw[:, t, 1:2], accm, ALU.mult, ALU.add, [r_vext, r_const, r_acc], [r_acc])
        stt(accm, vext[:, 0:8, :], cw[:, t, 0:1], accm, ALU.mult, ALU.add, [r_vext, r_const, r_acc], [r_acc])
        ts(accs, vs[:, :, 2:6], cw[:, t, 2:3], None, ALU.mult, None, [r_vext, r_const], [r_acc2])
        stt(accs, vs[:, :, 1:5], cw[:, t, 1:2], accs, ALU.mult, ALU.add, [r_vext, r_const, r_acc2], [r_acc2])
        stt(accs, vs[:, :, 0:4], cw[:, t, 0:1], accs, ALU.mult, ALU.add, [r_vext, r_const, r_acc2], [r_acc2])
        wv, r_w = load_w([w_in[:, C_CZ + t * 128:C_CZ + (t + 1) * 128]], 16)
        s = project(wv, r_w, 0, 16, hT, r_hT, SEG_MS)
        actf(tmp_s, pset(s)[:, 0:1088], AF.Silu, rset(s), [r_tmps])
        wv, r_w = load_w([w_in[:, C_CB + t * 128:C_CB + (t + 1) * 128]], 16)
        s = project(wv, r_w, 0, 16, hT, r_hT, SEG_MS)
        tt(tmp_s, pset(s)[:, 0:1088], tmp_s, ALU.mult, rset(s) + [r_tmps], [r_tmps])
        tt(cact[:, t, :], acc, tmp_s, ALU.mult, [r_acc, r_acc2, r_tmps], [r_cact[t]])

    if stop == "prep":
        return finish()
    for i in range(nloop + 1):
        if i < nloop:
            stage_A(i)
            stage_conv(i)
            stage_G(i)
        if i >= 1:
            stage_C(i - 1)
            if stop == "C":
                return finish()
        if i < nloop:
            stage_B(i)
            if stop == "B" and i == 1:
                return finish()
    if stop == "loop":
        return finish()

    fr_ = P["tA"]
    fi_ = P["tB"]
    r_fin = R("fin")
    t32a = P["am1"]
    t32b = P["den"]
    r_t32 = R("t32")
    cmul_b(fr_, fi_, SFr, SFi, Ppr[:, 7, :], Ppi[:, 7, :], t32a, t32b, r_SF + [r_Pp], r_t32, [r_fin], [r_fin])
    r_f = fence("dve", [r_fin])
    B.dma(DAP(spr_o, 0, [[1, 128], [128, 32]]), fr_, reads=[r_fin, r_f], stream="out", allow_slow_non_contiguous=True)
    B.dma(DAP(spi_o, 0, [[1, 128], [128, 32]]), fi_, reads=[r_fin, r_f], stream="out", allow_slow_non_contiguous=True)
    a4r = Ppr[:, 4, :].unsqueeze(2).to_broadcast([128, 32, 16])
    a4i = Ppi[:, 4, :].unsqueeze(2).to_broadcast([128, 32, 16])
    a3r = Ppr[:, 3, :].unsqueeze(2).to_broadcast([128, 32, 16])
    a3i = Ppi[:, 3, :].unsqueeze(2).to_broadcast([128, 32, 16])
    n1r, n1i = bnr, bni
    n2r = tmp_s[:, 0:512].rearrange("p (a b) -> p a b", a=32)
    n2i = tmp_s[:, 512:1024].rearrange("p (a b) -> p a b", a=32)
    r_n1 = R("n1")
    r_n2 = R("n2")
    cmul_b(n1r, n1i, SinTr, SinTi, a4r, a4i, t16c, t16d, [r_SinT, r_Pp, r_bn], r_t16b, [r_n1, r_bn], [r_n1, r_bn])
    cmul_b(n2r, n2i, ZSr, ZSi, a3r, a3i, t16c, t16d, r_ZS + [r_Pp, r_tmps], r_t16b, [r_n2, r_tmps], [r_n2, r_tmps])
    tt(n1r, n1r, n2r, ALU.add, [r_n1, r_n2], [r_n1])
    tt(n1i, n1i, n2i, ALU.add, [r_n1, r_n2], [r_n1])
    r_f = fence("dve", [r_n1, r_ncp, r_ncs])
    ost = [tmp_cc[0:16, 0:512], tmp_cc[0:16, 512:1024], vext[0:16, :, :].rearrange("p a b -> p (a b)")[:, 0:512],
           vext[0:16, :, :].rearrange("p a b -> p (a b)")[:, 512:1024]]
    r_ost = [R(f"ost{i}") for i in range(4)]
    oi = 0
    for q8 in range(8):
        for (src3, dsto, bk) in ((n1r, snr_o, 6), (n1i, sni_o, 7)):
            for gpl in range(4):
                B.pe(lambda e, src3=src3, bk=bk, gpl=gpl, q8=q8: e.transpose(
                    out=pall[0:16, bk, gpl * 128:(gpl + 1) * 128], in_=src3[:, 4 * q8 + gpl, :], identity=identf[:, :]),
                    reads=[r_n1, r_ident], writes=[r_pb[bk]])
            o_ = ost[oi % 4]
            r_o = r_ost[oi % 4]
            ostream = f"ost{oi % 4}"
            oi += 1
            B.act(lambda e, o_=o_, bk=bk: cp(e, o_, pall[0:16, bk, :]), reads=[r_pb[bk]], writes=[r_o, r_tmpcc, r_vext])
            r_f2 = fence("act", [r_o])
            B.dma(dsto[:, q8 * 512:(q8 + 1) * 512], o_, reads=[r_o, r_f2], stream=ostream)
    nst = acc[0:32, 0:1024]
    for t in range(8):
        bk = 6 + t // 4
        B.pe(lambda e, t=t, bk=bk: e.transpose(out=pall[0:32, bk, (t % 4) * 128:(t % 4 + 1) * 128],
                                               in_=ncs[:, t, :, :].rearrange("p s r -> p (s r)"), identity=identf[:, :]),
             reads=[r_ncs, r_ident], writes=[r_pb[bk]])
    for b2 in range(2):
        B.act(lambda e, b2=b2: cp(e, nst[:, b2 * 512:(b2 + 1) * 512], pall[0:32, 6 + b2, :]), reads=[r_pb[6 + b2]],
              writes=[r_acc])
    r_f3 = fence("act", [r_acc])
    B.dma(ncs_o.rearrange("s r c -> (s r) c"), nst, reads=[r_acc, r_f3], stream="out")
    for r in range(2):
        B.dma(DAP(ncp_o, r * 1024, [[1, 128], [128, 8]]), ncp[:, :, r], reads=[r_ncp, r_f], stream="out",
              allow_slow_non_contiguous=True)
    if "cact" in B.dbg:
        o = B.dout("dbg_cact", [128, 8, 1088], BF16)
        B.dma(o, cact[:, :, :], reads=r_cact, stream="dbg")
        o = B.dout("dbg_ys", [128, 8, 1088], BF16)
        B.dma(o, ysT[:, :, :], reads=r_ys, stream="dbg")
    B.S.barrier()
    if stop == "loopout":
        return finish()

    AR.top = 0
    tg = AR.alloc([1088])
    tb = AR.alloc([1088])
    r_tg = R("tg")
    r_tb = R("tb")
    r_mg = [R(f"mg{i}") for i in range(16)]

    def gate_sig(col, dst, r_dst):
        wv, r_w = load_w([w_in[:, col:col + 128]], 16)
        s = project(wv, r_w, 0, 16, hT, r_hT, SEG_MS)
        actf(dst, pset(s)[:, 0:1088], AF.Sigmoid, rset(s), [r_dst])

    def branch_out(wsrc, j, actbuf, r_actbuf):
        wv, r_w = load_w([wsrc[:, j * 128:(j + 1) * 128]], 8)
        return project(wv, r_w, 0, 8, actbuf, r_actbuf, SEG_MS)

    for j in range(16):
        gate_sig(C_G + 0 * D + j * 128, tg, r_tg)
        s = branch_out(w_conv_out, j, cact, r_cact)
        tt(merged[:, j, :], pset(s)[:, 0:1088], tg, ALU.mult, rset(s) + [r_tg], [r_mg[j]])
    for j in range(16):
        gate_sig(C_G + 1 * D + j * 128, tg, r_tg)
        s = branch_out(w_glu_b, j, ysT, r_ys)
        actf(tb, pset(s)[:, 0:1088], AF.Sigmoid, rset(s), [r_tb])
        tt(tg, tg, tb, ALU.mult, [r_tg, r_tb], [r_tg])
        s = branch_out(w_glu_a, j, ysT, r_ys)
        tt(tb, pset(s)[:, 0:1088], tg, ALU.mult, rset(s) + [r_tg], [r_tb])
        tt(merged[:, j, :], merged[:, j, :], tb, ALU.add, [r_mg[j], r_tb], [r_mg[j]])
    if "m2" in B.dbg:
        o = B.dout("dbg_m2", [128, 16, 1088], BF16)
        B.dma(o, merged, reads=r_mg, stream="dbg")
    B.S.barrier()
    if stop == "post":
        return finish()

    AR.top = 0
    KT = AR.alloc([8, 256], BF16)
    Vb = AR.alloc([2, 1024], BF16)
    a_top = AR.top
    memT = AR.alloc([16, 256], BF16)
    kst = AR.alloc([1024])
    vst = AR.alloc([1024])
    r_memT = [R(f"memT{i}") for i in range(16)]
    r_KT = R("KT")
    r_Vb = R("Vb")
    r_kst = R("kst")
    r_vst = R("vst")
    norm_transpose([(mem[0:128, :], 128, memT, r_memT, 0), (mem[128:256, :], 128, memT, r_memT, 128)], mem_norm_g, AR)
    if stop == "at0":
        return finish()
    for c4 in range(8):
        wv, r_w = load_w([w_mem_k[:, c4 * 128:(c4 + 1) * 128]], 16)
        for kt in range(16):
            B.pe(mm(pall[:, 0, 0:256], wv[:, kt, 0:128], memT[:, kt, :], kt == 0, kt == 15),
                 reads=[r_w, r_memT[kt]], writes=[r_pb[0]])
        B.alt(lambda e, c4=c4: cp(e, KT[:, c4, :], pall[:, 0, 0:256]), reads=[r_pb[0]], writes=[r_KT])
        for kt in range(16):
            B.pe(mm(pall[:, 1, 0:128], memT[:, kt, 0:128], wv[:, kt, 0:128], kt == 0, kt == 15),
                 reads=[r_w, r_memT[kt]], writes=[r_pb[1]])
        B.alt(lambda e, c4=c4: cp(e, kst[:, c4 * 128:(c4 + 1) * 128], pall[:, 1, 0:128]), reads=[r_pb[1]], writes=[r_kst])
    r_fa = fence("act", [r_kst])
    r_fd = fence("dve", [r_kst])
    B.dma(mk_o, kst, reads=[r_kst, r_fa, r_fd], stream="out")
    if stop == "at0k":
        return finish()
    for c4 in range(8):
        wv, r_w = load_w([w_mem_v[:, c4 * 128:(c4 + 1) * 128]], 16)
        for mt in range(1 if stop == "at0v1" else 2):
            bk = 2 + mt
            for kt in range(16):
                B.pe(mm(pall[:, bk, 0:128], memT[:, kt, mt * 128:(mt + 1) * 128], wv[:, kt, 0:128], kt == 0, kt == 15),
                     reads=[r_w, r_memT[kt]], writes=[r_pb[bk]])
            if mt == 0:
                B.alt(lambda e, c4=c4, bk=bk: cp(e, vst[:, c4 * 128:(c4 + 1) * 128], pall[:, bk, 0:128]),
                      reads=[r_pb[bk]], writes=[r_vst])
                B.alt(lambda e, c4=c4: cp(e, Vb[:, 0, c4 * 128:(c4 + 1) * 128], vst[:, c4 * 128:(c4 + 1) * 128]),
                      reads=[r_vst], writes=[r_Vb])
            else:
                B.alt(lambda e, c4=c4, mt=mt, bk=bk: cp(e, Vb[:, mt, c4 * 128:(c4 + 1) * 128], pall[:, bk, 0:128]),
                      reads=[r_pb[bk]], writes=[r_Vb])
    if stop == "at0v":
        return finish()
    r_fa = fence("act", [r_vst])
    r_fd = fence("dve", [r_vst])
    B.dma(mv_o, vst, reads=[r_vst, r_fa, r_fd], stream="out")
    if stop == "at0w":
        return finish()
    B.S.barrier()
    if stop == "at1":
        return finish()
    AR.top = a_top
    qT = AR.alloc([8, 1088], BF16)
    azs = AR.alloc([8, 1088], BF16)
    r_qT = [R(f"qT{i}") for i in range(8)]
    r_azs = [R(f"azs{i}") for i in range(8)]
    for f in range(8):
        wv, r_w = load_w([w_in[:, C_AQ + f * 128:C_AQ + (f + 1) * 128]], 16)
        s = project(wv, r_w, 0, 16, hT, r_hT, SEG_MS)
        B.alt(lambda e, s=s, f=f: cp(e, qT[:, f, :], pset(s)[:, 0:1088]), reads=rset(s), writes=[r_qT[f]])
        wv, r_w = load_w([w_in[:, C_AZ + f * 128:C_AZ + (f + 1) * 128]], 16)
        s = project(wv, r_w, 0, 16, hT, r_hT, SEG_MS)
        actf(azs[:, f, :], pset(s)[:, 0:1088], AF.Silu, rset(s), [r_azs[f]])
    if stop == "at2":
        return finish()
    att = cact
    r_att = [R(f"att{i}") for i in range(8)]
    B.S.barrier()
    PT = [AR.alloc([2, 512], BF16) for _ in range(2)]
    r_PT = [R("PT0"), R("PT1")]
    rsum = AR.alloc([512])
    r_rsum = R("rsum")
    otmp = AR.alloc([512])
    r_otmp = R("otmp")
    SC = 1.0 / 16.0
    it = 0
    for half in range(2):
        cs_ = slice(half * 512, (half + 1) * 512)
        for h in range(4):
            pi_ = it % 2
            it += 1
            for mt in range(2):
                for dt_ in range(2):
                    B.pe(mm(pall[:, mt, :], KT[:, 2 * h + dt_, mt * 128:(mt + 1) * 128], qT[:, 2 * h + dt_, cs_],
                            dt_ == 0, dt_ == 1), reads=[r_KT, r_qT[2 * h + dt_]], writes=[r_pb[mt]])
            for mt in range(2):
                actf(PT[pi_][:, mt, :], pall[:, mt, :], AF.Exp, [r_pb[mt]], [r_PT[pi_]], scale=SC)
            for mt in range(2):
                B.pe(mm(pall[:, 2, :], onesb[:], PT[pi_][:, mt, :], mt == 0, mt == 1), reads=[r_ident, r_PT[pi_]],
                     writes=[r_pb[2]])
            B.dve(lambda e: e.reciprocal(out=rsum, in_=pall[:, 2, :]), reads=[r_pb[2]], writes=[r_rsum])
            for dt_ in range(2):
                bk = 3 + dt_
                f = 2 * h + dt_
                for mt in range(2):
                    B.pe(mm(pall[:, bk, :], Vb[:, mt, f * 128:(f + 1) * 128], PT[pi_][:, mt, :], mt == 0, mt == 1),
                         reads=[r_Vb, r_PT[pi_]], writes=[r_pb[bk]])
                tt(otmp, pall[:, bk, :], rsum, ALU.mult, [r_pb[bk], r_rsum], [r_otmp])
                tt(att[:, f, cs_], otmp, azs[:, f, cs_], ALU.mult, [r_otmp, r_azs[f]], [r_att[f]])
    if stop == "at3":
        return finish()
    Kb = [AR.alloc([2, 1024], BF16) for _ in range(2)]
    Vs = [AR.alloc([2, 1024], BF16) for _ in range(2)]
    KTs = [AR.alloc([8, 256], BF16) for _ in range(2)]
    PTs = AR.alloc([2, 4, 4], BF16)
    rs16 = AR.alloc([4, 4])
    o32 = AR.alloc([8, 4])
    r_Kb = [R("Kb0"), R("Kb1")]
    r_Vs = [R("Vs0"), R("Vs1")]
    r_KTs = [R("KTs0"), R("KTs1")]
    r_PTs = R("PTs")
    r_rs16 = R("rs16")
    r_o32 = R("o32")
    for sq in range(16):
        sl = sq % 2
        B.dma(Kb[sl], ck[sq].rearrange("(a p) c -> p a c", p=128), writes=[r_Kb[sl]], stream=f"kvk{sl}", q="pool")
        B.dma(Vs[sl], cv[sq].rearrange("(a p) c -> p a c", p=128), writes=[r_Vs[sl]], stream=f"kvv{sl}", q="pool")
        for mt in range(2):
            bk = mt
            pv = pall[:, bk, :].bitcast(BF16)
            for f in range(8):
                B.pe(lambda e, pv=pv, sl=sl, mt=mt, f=f: e.transpose(out=pv[:, f * 128:(f + 1) * 128],
                                                                     in_=Kb[sl][:, mt, f * 128:(f + 1) * 128],
                                                                     identity=identb[:]),
                     reads=[r_Kb[sl], r_ident], writes=[r_pb[bk]])
            B.alt(lambda e, pv=pv, sl=sl, mt=mt: cp(e, KTs[sl][:, :, mt * 128:(mt + 1) * 128],
                                                   pv.rearrange("p (a b) -> p a b", a=8)),
                  reads=[r_pb[bk]], writes=[r_KTs[sl]])
        qc = slice(S0 + 4 * sq, S0 + 4 * sq + 4)
        for mt in range(2):
            for h in range(4):
                c0 = (mt * 4 + h) * 4
                for dt_ in range(2):
                    B.pe(mm(pall[:, 7, c0:c0 + 4], KTs[sl][:, 2 * h + dt_, mt * 128:(mt + 1) * 128], qT[:, 2 * h + dt_, qc],
                            dt_ == 0, dt_ == 1), reads=[r_KTs[sl], r_qT[2 * h + dt_]], writes=[r_pb[7]])
        actf(PTs, pall[:, 7, 0:32].rearrange("p (a b c) -> p a b c", a=2, b=4), AF.Exp, [r_pb[7]], [r_PTs], scale=SC)
        for h in range(4):
            for mt in range(2):
                B.pe(mm(pall[:, 7, 64 + h * 4:64 + h * 4 + 4], onesb[:], PTs[:, mt, h, :], mt == 0, mt == 1),
                     reads=[r_ident, r_PTs], writes=[r_pb[7]])
        for h in range(4):
            for dt_ in range(2):
                f = 2 * h + dt_
                for mt in range(2):
                    B.pe(mm(pall[:, 7, 128 + f * 4:128 + f * 4 + 4], Vs[sl][:, mt, f * 128:(f + 1) * 128], PTs[:, mt, h, :],
                            mt == 0, mt == 1), reads=[r_Vs[sl], r_PTs], writes=[r_pb[7]])
        B.dve(lambda e: e.reciprocal(out=rs16, in_=pall[:, 7, 64:80].rearrange("p (a b) -> p a b", a=4)),
              reads=[r_pb[7]], writes=[r_rs16])
        tt(o32.rearrange("p (h d) c -> p h d c", d=2), pall[:, 7, 128:160].rearrange("p (h d c) -> p h d c", h=4, d=2),
           rs16.unsqueeze(2).to_broadcast([128, 4, 2, 4]), ALU.mult, [r_pb[7], r_rs16], [r_o32])
        tt(att[:, :, qc], o32, azs[:, :, qc], ALU.mult, [r_o32] + r_azs, r_att)
    if stop == "at4":
        return finish()
    AR2 = AR.top
    tg2 = AR.alloc([1088])
    tb2 = AR.alloc([1088])
    for j in range(16):
        gate_sig(C_G + 2 * D + j * 128, tg2, r_tg)
        s = branch_out(w_attn_out, j, att, r_att)
        tt(tb2, pset(s)[:, 0:1088], tg2, ALU.mult, rset(s) + [r_tg], [r_tb])
        tt(merged[:, j, :], merged[:, j, :], tb2, ALU.add, [r_mg[j], r_tb], [r_mg[j]])
    if "m3" in B.dbg:
        o = B.dout("dbg_m3", [128, 16, 1088], BF16)
        B.dma(o, merged, reads=r_mg, stream="dbg")
        o = B.dout("dbg_att", [128, 8, 1088], BF16)
        B.dma(o, att[:, :, :], reads=r_att, stream="dbg")
    B.S.barrier()
    if stop == "attn":
        return finish()

    AR.top = 0
    wo = AR.alloc([16, 2048], BF16)
    r_wo = [R(f"wo{i}") for i in range(16)]
    hTf = hT[:, :, :].rearrange("p a b -> p (a b)").bitcast(F32)
    xr = [hTf[:, 0:2048], hTf[:, 2048:4096]]
    fg = hTf[:, 4096:6144]
    sqj = hTf[:, 6144:8192]
    r_xr = [R("xr0"), R("xr1")]
    r_fg = R("fg")
    r_sqj = R("sqj")
    B.dma(fg, final_g.partition_broadcast(128), writes=[r_fg], stream="fg")
    for c in range(16):
        B.dma(wo[:, :, c * 128:(c + 1) * 128], w_out[:, c * 128:(c + 1) * 128].rearrange("(a p) c -> p a c", p=128),
              writes=[r_wo[c]], stream=f"wo{c}", q="pool")
    ym_v = ym_o.rearrange("(j k) d -> k j d", k=8)
    ttiles = [(xm_v[k], ym_v[k], 128, k * 128) for k in range(8)] + [(xs, ys_o, NS, S0)]
    for ti, (xsrc, ydst, rows, col0) in enumerate(ttiles):
        sl = ti % 2
        B.dma(xr[sl][:rows, :], xsrc, writes=[r_xr[sl]], stream=f"xld{sl}")
        for cb_ in range(4):
            bk = (ti % 2) * 4 + cb_
            for kt in range(16):
                B.pe(mm(pall[:rows, bk, :], merged[:, kt, col0:col0 + rows], wo[:, kt, cb_ * 512:(cb_ + 1) * 512],
                        kt == 0, kt == 15), reads=[r_mg[kt]] + r_wo[cb_ * 4:(cb_ + 1) * 4], writes=[r_pb[bk]])
            tt(xr[sl][:rows, cb_ * 512:(cb_ + 1) * 512], xr[sl][:rows, cb_ * 512:(cb_ + 1) * 512], pall[:rows, bk, :], ALU.add,
               [r_xr[sl], r_pb[bk]], [r_xr[sl]])
        si = ti % 32
        sq_ = stat[:rows, 2 * si:2 * si + 1]
        rs_ = stat[:rows, 2 * si + 1:2 * si + 2]
        r_s = r_stat[si]
        B.act(lambda e, sl=sl, rows=rows, sq_=sq_: e.activation(out=sqj[:rows, :], in_=xr[sl][:rows, :], func=AF.Square,
                                                               accum_out=sq_), reads=[r_xr[sl]], writes=[r_sqj, r_s])
        B.act(lambda e, sq_=sq_, rs_=rs_: e.activation(out=rs_, in_=sq_, func=AF.Sqrt, scale=1.0 / D, bias=EPS),
              reads=[r_s], writes=[r_s])
        B.dve(lambda e, rs_=rs_: e.reciprocal(out=rs_, in_=rs_), reads=[r_s], writes=[r_s])
        stt(xr[sl][:rows, :], xr[sl][:rows, :], rs_, fg[:rows, :], ALU.mult, ALU.mult, [r_xr[sl], r_s, r_fg], [r_xr[sl]])
        r_f = fence("dve", [r_xr[sl]])
        B.dma(ydst, xr[sl][:rows, :], reads=[r_xr[sl], r_f], stream=f"yout{sl}")

    return finish()


IN_NAMES = ["xm", "xp", "xs", "mem", "ck", "cv", "sconv", "ssr", "ssi"]


def core_inputs(inp, c):
    b, hb = c // 2, c % 2
    f = np.float32
    A = np.ascontiguousarray
    memr = inp["mem_prompt"][b]
    return {
        "xm": A(inp["x_prompt"][b, hb * NM:(hb + 1) * NM]),
        "xp": A(inp["x_prompt"][b, 0:NM]) if hb == 1 else np.zeros((NM, D), f),
        "xs": A(inp["x_sample"][c * 16:(c + 1) * 16].reshape(NS, D)),
        "mem": A(np.concatenate([memr[hb * 128:(hb + 1) * 128], memr[(1 - hb) * 128:(2 - hb) * 128]], 0)),
        "ck": A(inp["cache_mem_k"][0, c * 16:(c + 1) * 16].reshape(16, 256, 1024)),
        "cv": A(inp["cache_mem_v"][0, c * 16:(c + 1) * 16].reshape(16, 256, 1024)),
        "sconv": A(inp["state_conv"][0, c * 16:(c + 1) * 16]),
        "ssr": A(inp["state_ssm_re"][0, c * 16:(c + 1) * 16].reshape(16, 4096)),
        "ssi": A(inp["state_ssm_im"][0, c * 16:(c + 1) * 16].reshape(16, 4096)),
    }


def shared_inputs(inp):
    A = np.ascontiguousarray
    return {
        "norm_g": A(inp["norm_g"][0]), "mem_norm_g": A(inp["mem_norm_g"][0]), "final_g": A(inp["final_norm_g"]),
        "w_in": A(inp["w_in"][0]), "conv_w": A(inp["conv_w"][0]), "w_conv_out": A(inp["w_conv_out"][0]),
        "lam_re": A(inp["ssm_lambda_re"][0]), "lam_im": A(inp["ssm_lambda_im"][0]), "log_dt": A(inp["ssm_log_dt"][0]),
        "b_re": A(inp["ssm_b_re"][0]), "b_im": A(inp["ssm_b_im"][0]), "c_re": A(inp["ssm_c_re"][0]),
        "c_im": A(inp["ssm_c_im"][0]), "ssm_d": A(inp["ssm_d"][0]), "w_glu_a": A(inp["w_glu_a"][0]),
        "w_glu_b": A(inp["w_glu_b"][0]), "w_mem_k": A(inp["w_mem_k"][0]), "w_mem_v": A(inp["w_mem_v"][0]),
        "w_attn_out": A(inp["w_attn_out"][0]), "w_out": A(inp["w_out"][0]),
    }


_CACHE = {}


def run(inp, dbg=(), ncores=NCORE, stop=None, nloop=8):
    key = (tuple(sorted(dbg)), stop, nloop)
    if key not in _CACHE:
        _CACHE[key] = build(dbg, stop, nloop)
    B = _CACHE[key]
    sh = shared_inputs(inp)
    in_maps = []
    for c in range(ncores):
        m = dict(sh)
        m.update(core_inputs(inp, c))
        in_maps.append({k: v for k, v in m.items() if k in B.ins})
    res = run_bass_kernel_spmd(B.nc, in_maps, core_ids=list(range(ncores)))
    return res.results


def assemble(res):
    f = np.float32
    y_prompt = np.zeros((4, 2048, D), f)
    y_sample = np.zeros((128, 4, D), f)
    mk = np.zeros((1, 4, 256, 4, 256), f)
    mv = np.zeros((1, 4, 256, 4, 256), f)
    ncp = np.zeros((1, 4, 2, 1024), f)
    spr = np.zeros((1, 4, 64, 64), f)
    spi = np.zeros((1, 4, 64, 64), f)
    ncs = np.zeros((1, 128, 2, 1024), f)
    snr = np.zeros((1, 128, 64, 64), f)
    sni = np.zeros((1, 128, 64, 64), f)
    for c in range(NCORE):
        b, hb = c // 2, c % 2
        r = res[c]
        y_prompt[b, hb * NM:(hb + 1) * NM] = r["ym"]
        y_sample[c * 16:(c + 1) * 16] = r["ysm"].reshape(16, 4, D)
        mk[0, b, hb * 128:(hb + 1) * 128] = r["mk"].reshape(128, 4, 256)
        mv[0, b, hb * 128:(hb + 1) * 128] = r["mv"].reshape(128, 4, 256)
        if hb == 1:
            ncp[0, b] = r["ncp"]
            spr[0, b] = r["spr"].reshape(64, 64)
            spi[0, b] = r["spi"].reshape(64, 64)
        ncs[0, c * 16:(c + 1) * 16] = r["ncs"]
        snr[0, c * 16:(c + 1) * 16] = r["snr"].reshape(16, 64, 64)
        sni[0, c * 16:(c + 1) * 16] = r["sni"].reshape(16, 64, 64)
    return (y_prompt, y_sample, mk, mv, ncp, spr, spi, ncs, snr, sni)


def kernel(**inputs):
    inp = {k: np.asarray(v) for k, v in inputs.items()}
    res = run(inp)
    return assemble(res)
```
